# Optimizing a Trainium2 kernel written in Bass

```python
import math
import jax, jax.numpy as jnp
from jax import lax
import numpy as np

D_MODEL = 1024
BATCH = 2
SEQ = 8192
DEPTH = 2
DEC_BATCH = 32
DEC_SEQ = 8
PAST_LEN = 16384
PAGE_SIZE = 128

N_A = DEPTH // 2
N_B = DEPTH - N_A
CHUNK = 128
D_SG = 3 * D_MODEL
N_SG_GROUPS = 8
D_FF = ((8 * D_MODEL // 3 + 127) // 128) * 128
CONV_W = 3
HEAD_DIM = 64
N_HEADS = D_MODEL // HEAD_DIM
N_KV = 4
GROUP = N_HEADS // N_KV
BLK = 64
N_SEL = 16
WINDOW = 512
CMP_HID = 2 * HEAD_DIM
NUM_BUCKETS = 32
REL_MAX_DIST = 1024
Q_BLOCK = 128
ALPHA = (2 * DEPTH) ** 0.25
BETA = (8 * DEPTH) ** -0.25
LN_EPS = 1e-5
NEG_INF = -1e30
FORCED_SCORE = 1e9

kernel_name = 'yoco_sgu_nsa_convffn_decoder_step'


def layer_norm(x, g, b):
    xf = x.astype(jnp.float32)
    mu = xf.mean(-1, keepdims=True)
    var = jnp.square(xf - mu).mean(-1, keepdims=True)
    return ((xf - mu) * lax.rsqrt(var + LN_EPS) * g + b).astype(x.dtype)


def gelu(x):
    return jax.nn.gelu(x, approximate=False)


def spatial_gate(v, w_s, b_s):
    bsz, L, _ = v.shape
    cl = min(CHUNK, L)
    n_ch = -(-L // cl)
    vp = jnp.pad(v, ((0, 0), (0, n_ch * cl - L), (0, 0)))
    vc = vp.reshape(bsz, n_ch, cl, N_SG_GROUPS, D_SG // N_SG_GROUPS)
    w = jnp.tril(w_s[:, :cl, :cl])
    out = jnp.einsum('gts,bnsgc->bntgc', w, vc) + b_s[:, :cl].T[None, None, :, :, None]
    return out.reshape(bsz, n_ch * cl, D_SG)[:, :L]


def sgu_mixer(x, w_in, b_in, g_v, b_v, w_s, b_s, w_out, b_out):
    u, v = jnp.split(gelu(x @ w_in + b_in), 2, axis=-1)
    v = layer_norm(v, g_v, b_v)
    return (u * spatial_gate(v, w_s, b_s)) @ w_out + b_out, v


def conv_ffn(x, conv_state, w_up, b_up, w_dw, b_dw, w_down, b_down):
    a, g = jnp.split(x @ w_up + b_up, 2, axis=-1)
    L = a.shape[1]
    ap = jnp.concatenate([conv_state.astype(a.dtype), a], axis=1)
    c = b_dw
    for k in range(CONV_W):
        c = c + ap[:, k:k + L] * w_dw[k]
    return (gelu(c) * g) @ w_down + b_down, ap[:, L:]


def t5_bucket(dist):
    n = jnp.maximum(dist, 0)
    max_exact = NUM_BUCKETS // 2
    nf = jnp.maximum(n, 1).astype(jnp.float32)
    large = max_exact + (jnp.log(nf / max_exact) / math.log(REL_MAX_DIST / max_exact)
                         * (NUM_BUCKETS - max_exact)).astype(jnp.int32)
    return jnp.where(n < max_exact, n, jnp.minimum(large, NUM_BUCKETS - 1))


def bias_shared(table, dist):
    q, k = dist.shape
    b = table[t5_bucket(dist)].reshape(q, k, N_KV, GROUP)
    return jnp.transpose(b, (0, 2, 3, 1))[None]


def bias_per_group(table, dist):
    tbl = table.reshape(NUM_BUCKETS, N_KV, GROUP)
    b = tbl[t5_bucket(dist), jnp.arange(N_KV)[None, None, :, None]]
    return jnp.swapaxes(b, -1, -2)


def masked_softmax(logits, mask):
    logits = jnp.where(mask, logits, NEG_INF)
    p = jnp.exp(logits - logits.max(-1, keepdims=True)) * mask
    return p / jnp.maximum(p.sum(-1, keepdims=True), 1e-30)


def compress_blocks(rows, pe, w1, b1, w2, b2):
    w1r = w1.reshape(2, BLK, HEAD_DIM, CMP_HID)
    pre = (jnp.einsum('bnlkgd,kldh->bnkgh', rows, w1r)
           + (jnp.einsum('kld,kldh->kh', pe, w1r) + b1)[:, None, :])
    return jnp.einsum('bnkgh,khd->bnkgd', gelu(pre), w2) + b2[:, None, :]


def nsa_core(q, gates, qpos, kvc, cpos, n_blk, select, kvw, kwpos, rel_bias):
    f32 = jnp.float32
    scale = HEAD_DIM ** -0.5
    lc = jnp.einsum('bqgrd,bcgd->bqgrc', q, kvc[:, :, 0], preferred_element_type=f32) * scale
    lc = lc + bias_shared(rel_bias, qpos[:, None] - cpos[None, :])
    pc = masked_softmax(lc, (cpos[None, :] <= qpos[:, None])[None, :, None, None, :])
    o_c = jnp.einsum('bqgrc,bcgd->bqgrd', pc, kvc[:, :, 1])
    imp = jnp.pad(pc.sum(axis=3), ((0, 0), (0, 0), (0, 0), (0, n_blk - pc.shape[-1])))
    blk = jnp.arange(n_blk)
    cur = (qpos // BLK)[:, None]
    forced = ((blk == 0) | (blk == cur) | (blk == cur - 1))[None, :, None, :]
    valid = (blk <= cur)[None, :, None, :]
    score = jnp.where(forced, FORCED_SCORE, jnp.where(valid, imp, -1.0))
    _, idx = lax.top_k(score, min(N_SEL, n_blk))
    ks, vs, kpos, kval = select(idx)
    ls = jnp.einsum('bqgrd,bqgkd->bqgrk', q, ks, preferred_element_type=f32) * scale
    ls = ls + bias_per_group(rel_bias, qpos[None, :, None, None] - kpos)
    ps = masked_softmax(ls, (kval & (kpos <= qpos[None, :, None, None]))[:, :, :, None, :])
    o_s = jnp.einsum('bqgrk,bqgkd->bqgrd', ps, vs)
    dw = qpos[:, None] - kwpos[None, :]
    lw = jnp.einsum('bqgrd,bkgd->bqgrk', q, kvw[:, :, 0], preferred_element_type=f32) * scale
    lw = lw + bias_shared(rel_bias, dw)
    pw = masked_softmax(lw, ((dw >= 0) & (dw < WINDOW) & (kwpos >= 0)[None, :])[None, :, None, None, :])
    o_w = jnp.einsum('bqgrk,bkgd->bqgrd', pw, kvw[:, :, 1])
    return gates[..., 0:1] * o_c + gates[..., 1:2] * o_s + gates[..., 2:3] * o_w


def run_trunk(x, conv_state, prepare, attend, p):
    bsz, L = x.shape[:2]
    new_conv, sg_v = [], []
    ctx, kv_rows = None, None
    for l in range(DEPTH):
        if l < N_A:
            y, v = sgu_mixer(x, p['sg_w_in'][l], p['sg_b_in'][l], p['sg_ln_g'][l], p['sg_ln_b'][l],
                             p['sg_w_s'][l], p['sg_b_s'][l], p['sg_w_out'][l], p['sg_b_out'][l])
            sg_v.append(v)
        else:
            i = l - N_A
            qg = x @ p['nsa_w_qg'][i] + p['nsa_b_qg'][i]
            q = qg[..., :N_HEADS * HEAD_DIM].reshape(bsz, L, N_KV, GROUP, HEAD_DIM)
            gates = jax.nn.sigmoid(qg[..., N_HEADS * HEAD_DIM:].astype(jnp.float32)).reshape(bsz, L, N_KV, GROUP, 3)
            o = attend(q, gates, ctx)
            y = o.reshape(bsz, L, N_HEADS * HEAD_DIM).astype(x.dtype) @ p['nsa_w_o'][i] + p['nsa_b_o'][i]
        x = layer_norm(ALPHA * x + y, p['ln_g'][l, 0], p['ln_b'][l, 0])
        y, cs = conv_ffn(x, conv_state[l], p['ffn_w_up'][l], p['ffn_b_up'][l], p['ffn_w_dw'][l],
                         p['ffn_b_dw'][l], p['ffn_w_down'][l], p['ffn_b_down'][l])
        new_conv.append(cs)
        x = layer_norm(ALPHA * x + y, p['ln_g'][l, 1], p['ln_b'][l, 1])
        if l == N_A - 1:
            kv_rows = (x @ p['kv_w']).reshape(bsz, L, 6, N_KV, HEAD_DIM)
            ctx = prepare(kv_rows)
    return x, jnp.stack(new_conv), jnp.stack(sg_v), kv_rows


def setup_inputs(seed: int = 0) -> dict:
    key = jax.random.key(seed)
    ks = iter(jax.random.split(key, 48))

    def nrm(shape, scale):
        return jax.random.normal(next(ks), shape, jnp.float32) * scale

    n_pages = PAST_LEN // PAGE_SIZE
    n_phys = (DEC_BATCH * n_pages * 5) // 4
    win_buf = min(WINDOW, PAST_LEN)
    kvd = (2, N_KV, HEAD_DIM)
    qg_dim = N_HEADS * HEAD_DIM + 3 * N_HEADS
    page_table = jax.random.permutation(next(ks), n_phys)[:DEC_BATCH * n_pages].reshape(DEC_BATCH, n_pages).astype(jnp.int32)
    return {
        'x_prompt': nrm((BATCH, SEQ, D_MODEL), 1.0),
        'x_sample': nrm((DEC_BATCH, DEC_SEQ, D_MODEL), 1.0),
        'cache_kv_cmp': nrm((n_phys, PAGE_SIZE) + kvd, 1.0),
        'cache_kv_slc': nrm((n_phys, PAGE_SIZE) + kvd, 1.0),
        'cache_win': nrm((DEC_BATCH, win_buf) + kvd, 1.0),
        'state_conv': nrm((DEPTH, DEC_BATCH, CONV_W - 1, D_FF), 1.0),
        'page_table': page_table,
        'ln_g': 1.0 + nrm((DEPTH, 2, D_MODEL), 0.02),
        'ln_b': nrm((DEPTH, 2, D_MODEL), 0.02),
        'sg_w_in': nrm((N_A, D_MODEL, 2 * D_SG), D_MODEL ** -0.5),
        'sg_b_in': nrm((N_A, 2 * D_SG), 0.02),
        'sg_ln_g': 1.0 + nrm((N_A, D_SG), 0.02),
        'sg_ln_b': nrm((N_A, D_SG), 0.02),
        'sg_w_s': nrm((N_A, N_SG_GROUPS, CHUNK, CHUNK), CHUNK ** -0.5),
        'sg_b_s': 1.0 + nrm((N_A, N_SG_GROUPS, CHUNK), 0.02),
        'sg_w_out': nrm((N_A, D_SG, D_MODEL), BETA * D_SG ** -0.5),
        'sg_b_out': nrm((N_A, D_MODEL), 0.02),
        'ffn_w_up': nrm((DEPTH, D_MODEL, 2 * D_FF), D_MODEL ** -0.5),
        'ffn_b_up': nrm((DEPTH, 2 * D_FF), 0.02),
        'ffn_w_dw': nrm((DEPTH, CONV_W, D_FF), CONV_W ** -0.5),
        'ffn_b_dw': nrm((DEPTH, D_FF), 0.02),
        'ffn_w_down': nrm((DEPTH, D_FF, D_MODEL), BETA * D_FF ** -0.5),
        'ffn_b_down': nrm((DEPTH, D_MODEL), 0.02),
        'kv_w': nrm((D_MODEL, 6 * N_KV * HEAD_DIM), D_MODEL ** -0.5),
        'cmp_pe': nrm((2, BLK, HEAD_DIM), 0.1),
        'cmp_w1': nrm((2, BLK * HEAD_DIM, CMP_HID), (BLK * HEAD_DIM) ** -0.5),
        'cmp_b1': nrm((2, CMP_HID), 0.02),
        'cmp_w2': nrm((2, CMP_HID, HEAD_DIM), CMP_HID ** -0.5),
        'cmp_b2': nrm((2, HEAD_DIM), 0.02),
        'nsa_w_qg': nrm((N_B, D_MODEL, qg_dim), D_MODEL ** -0.5),
        'nsa_b_qg': nrm((N_B, qg_dim), 0.02),
        'nsa_w_o': nrm((N_B, N_HEADS * HEAD_DIM, D_MODEL), BETA * (N_HEADS * HEAD_DIM) ** -0.5),
        'nsa_b_o': nrm((N_B, D_MODEL), 0.02),
        'rel_bias': nrm((NUM_BUCKETS, N_HEADS), 0.5),
    }


def reference(x_prompt, x_sample, cache_kv_cmp, cache_kv_slc, cache_win, state_conv, page_table,
              ln_g, ln_b, sg_w_in, sg_b_in, sg_ln_g, sg_ln_b, sg_w_s, sg_b_s, sg_w_out, sg_b_out,
              ffn_w_up, ffn_b_up, ffn_w_dw, ffn_b_dw, ffn_w_down, ffn_b_down,
              kv_w, cmp_pe, cmp_w1, cmp_b1, cmp_w2, cmp_b2,
              nsa_w_qg, nsa_b_qg, nsa_w_o, nsa_b_o, rel_bias):
    p = dict(ln_g=ln_g, ln_b=ln_b, sg_w_in=sg_w_in, sg_b_in=sg_b_in, sg_ln_g=sg_ln_g, sg_ln_b=sg_ln_b,
             sg_w_s=sg_w_s, sg_b_s=sg_b_s, sg_w_out=sg_w_out, sg_b_out=sg_b_out,
             ffn_w_up=ffn_w_up, ffn_b_up=ffn_b_up, ffn_w_dw=ffn_w_dw, ffn_b_dw=ffn_b_dw,
             ffn_w_down=ffn_w_down, ffn_b_down=ffn_b_down, kv_w=kv_w,
             nsa_w_qg=nsa_w_qg, nsa_b_qg=nsa_b_qg, nsa_w_o=nsa_w_o, nsa_b_o=nsa_b_o)
    gi = jnp.arange(N_KV)[None, None, :, None]

    bsz, S, _ = x_prompt.shape
    nb_p = S // BLK

    def prepare_prompt(kv):
        kvc = compress_blocks(kv[:, :, 0:2].reshape(bsz, nb_p, BLK, 2, N_KV, HEAD_DIM),
                              cmp_pe, cmp_w1, cmp_b1, cmp_w2, cmp_b2)
        kvb = kv[:, :, 2:4].reshape(bsz, nb_p, BLK, 2, N_KV, HEAD_DIM)
        kvw_pad = jnp.pad(kv[:, :, 4:6], ((0, 0), (WINDOW, 0), (0, 0), (0, 0), (0, 0)))
        return kvc, kvb, kvw_pad

    def attend_prompt(q, gates, ctx):
        kvc, kvb, kvw_pad = ctx
        nqb = S // Q_BLOCK
        cpos = jnp.arange(nb_p) * BLK + BLK - 1
        bi = jnp.arange(bsz)[:, None, None, None]

        def select(idx):
            lead, n = idx.shape[:3], idx.shape[-1]
            kv = kvb[bi, idx, :, :, gi].reshape(lead + (n * BLK, 2, HEAD_DIM))
            kpos = (idx[..., None] * BLK + jnp.arange(BLK)).reshape(lead + (n * BLK,))
            return kv[..., 0, :], kv[..., 1, :], kpos, jnp.ones(kpos.shape, dtype=bool)

        def block(args):
            qb, gb, i = args
            start = i * Q_BLOCK
            qpos = start + jnp.arange(Q_BLOCK)
            kvw = lax.dynamic_slice_in_dim(kvw_pad, start, Q_BLOCK + WINDOW, axis=1)
            kwpos = start - WINDOW + jnp.arange(Q_BLOCK + WINDOW)
            return nsa_core(qb, gb, qpos, kvc, cpos, nb_p, select, kvw, kwpos, rel_bias)

        qs = jnp.swapaxes(q.reshape(bsz, nqb, Q_BLOCK, N_KV, GROUP, HEAD_DIM), 0, 1)
        gs = jnp.swapaxes(gates.reshape(bsz, nqb, Q_BLOCK, N_KV, GROUP, 3), 0, 1)
        o = lax.map(block, (qs, gs, jnp.arange(nqb)))
        return jnp.swapaxes(o, 0, 1).reshape(bsz, S, N_KV, GROUP, HEAD_DIM)

    y_p, conv_p, _, kv_p = run_trunk(x_prompt, jnp.zeros((DEPTH, bsz, CONV_W - 1, D_FF), x_prompt.dtype),
                                     prepare_prompt, attend_prompt, p)

    dbsz, L, _ = x_sample.shape
    n_pages = page_table.shape[1]
    past_len = n_pages * PAGE_SIZE
    npb = past_len // BLK
    bpp = PAGE_SIZE // BLK
    n_newc = L // BLK
    n_blk_s = npb + -(-L // BLK)
    win_buf = cache_win.shape[1]
    pool_blocks = cache_kv_slc.reshape(-1, BLK, 2, N_KV, HEAD_DIM)

    def prepare_sample(kv):
        past = cache_kv_cmp[page_table].reshape(dbsz, npb, BLK, 2, N_KV, HEAD_DIM)
        new = kv[:, :n_newc * BLK, 0:2].reshape(dbsz, n_newc, BLK, 2, N_KV, HEAD_DIM)
        kvc = jnp.concatenate([compress_blocks(past, cmp_pe, cmp_w1, cmp_b1, cmp_w2, cmp_b2),
                               compress_blocks(new, cmp_pe, cmp_w1, cmp_b1, cmp_w2, cmp_b2).astype(past.dtype)], axis=1)
        kvw = jnp.concatenate([cache_win, kv[:, :, 4:6].astype(cache_win.dtype)], axis=1)
        return kvc, kv[:, :, 2:4], kvw

    def attend_sample(q, gates, ctx):
        kvc, kv_new, kvw = ctx
        qpos = past_len + jnp.arange(L)
        cpos = jnp.arange(kvc.shape[1]) * BLK + BLK - 1
        kwpos = jnp.concatenate([past_len - win_buf + jnp.arange(win_buf), qpos])
        bi = jnp.arange(dbsz)[:, None, None, None]
        new_blk = npb + jnp.arange(L) // BLK

        def select(idx):
            lead, n = idx.shape[:3], idx.shape[-1]
            j = jnp.minimum(idx, npb - 1)
            phys = page_table[bi, j // bpp] * bpp + j % bpp
            kv = pool_blocks[phys, :, :, gi].reshape(lead + (n * BLK, 2, HEAD_DIM))
            kpos = (idx[..., None] * BLK + jnp.arange(BLK)).reshape(lead + (n * BLK,))
            kval = jnp.broadcast_to((idx < npb)[..., None], idx.shape + (BLK,)).reshape(lead + (n * BLK,))
            chosen = jnp.any(idx[..., :, None] == new_blk, axis=-2)
            kv_n = jnp.broadcast_to(jnp.transpose(kv_new, (0, 3, 1, 2, 4))[:, None], lead + (L, 2, HEAD_DIM))
            kv = jnp.concatenate([kv, kv_n.astype(kv.dtype)], axis=3)
            kpos = jnp.concatenate([kpos, jnp.broadcast_to(qpos, lead + (L,))], axis=-1)
            kval = jnp.concatenate([kval, chosen], axis=-1)
            return kv[..., 0, :], kv[..., 1, :], kpos, kval

        return nsa_core(q, gates, qpos, kvc, cpos, n_blk_s, select, kvw, kwpos, rel_bias)

    y_s, conv_s, sgv_s, kv_s = run_trunk(x_sample, state_conv, prepare_sample, attend_sample, p)

    win_p = kv_p[:, S - min(WINDOW, S):, 4:6]
    return (y_p, y_s, kv_p[:, :, 0:2], kv_p[:, :, 2:4], win_p, conv_p,
            kv_s[:, :, 0:2], kv_s[:, :, 2:4], kv_s[:, :, 4:6], conv_s, sgv_s)
```

```python
import contextlib, math
import numpy as np
import concourse.bass as bass
import concourse.mybir as mybir
from concourse.bass_utils import run_bass_kernel_spmd

F32 = mybir.dt.float32; BF16 = mybir.dt.bfloat16; I32 = mybir.dt.int32; U32 = mybir.dt.uint32
AF = mybir.ActivationFunctionType; ALU = mybir.AluOpType; AX = mybir.AxisListType

DEPTH = 2
ALPHA = (2 * DEPTH) ** 0.25
LN_EPS = 1e-5
NEG = -30000.0
BIG = 1e9


class Cfg:
    def __init__(self, S=8192, PAST=16384, NPHYS=5120, stage=9):
        self.S = S; self.PAST = PAST; self.NPHYS = NPHYS; self.stage = stage
        self.D = 1024; self.DSG = 3072; self.DFF = 2816; self.L = 8
        self.TOWN = S // 4; self.NTO = self.TOWN // 128; self.NTP = self.NTO + 1
        self.NKT = S // 128; self.NBLK = S // 64
        self.NPG = PAST // 128; self.NPB = PAST // 64


class Buf:
    __slots__ = ("name", "w", "r")

    def __init__(self, name):
        self.name = name; self.w = None; self.r = {}


class TB:
    __slots__ = ("t", "b")

    def __init__(self, t, b):
        self.t = t; self.b = b


class Eng:
    def __init__(self, name, h, sem):
        self.name = name; self.h = h; self.sem = sem; self.count = 0; self.seen = {}; self.ops = []


class FW:
    NQ = 8

    def __init__(self, nc, stack):
        self.nc = nc; self.stack = stack

        def S(n):
            return stack.enter_context(nc.semaphore(n))
        self.eng = {"pe": Eng("pe", nc.tensor, S("s_pe")), "act": Eng("act", nc.scalar, S("s_act")),
                    "dve": Eng("dve", nc.vector, S("s_dve")), "pool": Eng("pool", nc.gpsimd, S("s_pool")),
                    "sp": Eng("sp", nc.sync, S("s_sp"))}
        self.qsem = {q: [S(f"q_{q}_{i}") for i in range(self.NQ)] for q in ("sp", "pool")}
        self.qcnt = {"sp": 0, "pool": 0}
        self.ccsem = S("s_cc"); self.cccnt = 0
        self.nb = 0

    def buf(self, name=None):
        self.nb += 1
        return Buf(name or f"b{self.nb}")

    def sb(self, name, shape, dt, stack=None):
        name = f"{name}_{self.nb}"
        return TB((stack or self.stack).enter_context(self.nc.sbuf_tensor(name, list(shape), dt)), self.buf(name))

    def ps(self, name, shape, dt):
        return TB(self.stack.enter_context(self.nc.psum_tensor(name, list(shape), dt)), self.buf(name))

    def dram(self, name, shape, dt):
        return TB(self.nc.dram_tensor(name, list(shape), dt), self.buf(name))

    def _deps(self, e, reads, writes):
        deps = {}

        def add(tok):
            if tok is None:
                return
            s, v = tok
            if deps.get(s, 0) < v:
                deps[s] = v
        for b in reads:
            add(b.w)
        for b in writes:
            add(b.w)
            for s, v in b.r.items():
                add((s, v))
        out = []
        for s, v in deps.items():
            if s is e.sem and e.name == "pe":
                continue
            if e.seen.get(s, 0) >= v:
                continue
            e.seen[s] = v; out.append((s, v))
        return out

    def _mark(self, tok, reads, writes):
        s, v = tok
        for b in reads:
            if b.r.get(s, 0) < v:
                b.r[s] = v
        for b in writes:
            b.w = tok; b.r = {}

    def op(self, en, fn, reads=(), writes=()):
        e = self.eng[en]; waits = self._deps(e, reads, writes)
        e.count += 1; tok = (e.sem, e.count)

        def run(h=e.h, waits=waits, fn=fn, sem=e.sem):
            for s, v in waits:
                h.wait_ge(s, v)
            fn(h).then_inc(sem, 1)
        run(); self._mark(tok, reads, writes)
        return tok

    def dma(self, q, fn, reads=(), writes=()):
        e = self.eng[q]; i = self.qcnt[q]; self.qcnt[q] += 1
        sem = self.qsem[q][i % self.NQ]; val = 16 * (i // self.NQ + 1)
        waits = self._deps(e, reads, writes)
        if i >= self.NQ and e.seen.get(sem, 0) < val - 16:
            waits.append((sem, val - 16)); e.seen[sem] = val - 16
        tok = (sem, val)

        def run(h=e.h, waits=waits, fn=fn, sem=sem):
            for s, v in waits:
                h.wait_ge(s, v)
            fn(h).then_inc(sem, 16)
        run(); self._mark(tok, reads, writes)
        return tok

    def allreduce(self, src, dst, n=8):
        e = self.eng["pool"]; waits = self._deps(e, [src.b], [dst.b])
        self.cccnt += 1; tok = (self.ccsem, self.cccnt)

        def run(h=e.h, waits=waits, sem=self.ccsem):
            for s, v in waits:
                h.wait_ge(s, v)
            h.collective_compute("AllReduce", ALU.add, replica_groups=[list(range(n))],
                                 ins=[src.t.ap().opt()], outs=[dst.t.ap().opt()]).then_inc(sem)
        run(); self._mark(tok, [src.b], [dst.b])
        return tok

    def barrier(self):
        toks = [(e.sem, e.count) for e in self.eng.values() if e.count > 0]
        for q, n in self.qcnt.items():
            for k in range(min(n, self.NQ)):
                last = ((n - 1 - k) // self.NQ) * self.NQ + k
                toks.append((self.qsem[q][k], 16 * (last // self.NQ + 1)))
        if self.cccnt:
            toks.append((self.ccsem, self.cccnt))
        for e in self.eng.values():
            for s_, v in toks:
                if s_ is e.sem:
                    continue
                if e.seen.get(s_, 0) >= v:
                    continue
                e.seen[s_] = v
                e.h.wait_ge(s_, v)

    def finish(self, final_bufs):
        e = self.eng["sp"]
        for s_, v in self._deps(e, final_bufs, []):
            e.h.wait_ge(s_, v)


def t5_bucket_np(n):
    n = np.maximum(n, 0)
    nf = np.maximum(n, 1).astype(np.float32)
    large = 16 + (np.log(nf / np.float32(16)) / np.float32(math.log(1024 / 16)) * np.float32(16)).astype(np.int32)
    return np.where(n < 16, n, np.minimum(large, 31))


def host_consts(cfg):
    c = {}
    c["ident"] = np.eye(128, dtype=np.float32)
    s = np.arange(128)
    c["tril_p"] = (s[:, None] <= s[None, :]).astype(np.float32)
    c["tril_s"] = ((s[:, None] // 8 == s[None, :] // 8) & (s[:, None] <= s[None, :])).astype(np.float32)
    m = np.arange(1152)
    bk = t5_bucket_np(m - 127)
    oh = np.zeros((32, 1152), np.float32); oh[bk, m] = 1.0
    c["oh1"] = oh; c["oh1r"] = np.ascontiguousarray(oh[:, ::-1])
    p = np.arange(128)
    kl = 127 - p
    c["causT"] = np.where(kl[:, None] > s[None, :], NEG * 8, 0.0).astype(np.float32)
    c["winfarT"] = np.where(kl[:, None] <= s[None, :], NEG * 8, 0.0).astype(np.float32)
    NBLK, NKT = cfg.NBLK, cfg.NKT
    E = np.zeros((NBLK, NKT, 128), np.float32)
    for kt in range(NKT):
        E[2 * kt + 1, kt, 0:64] = 1.0
        E[2 * kt, kt, 64:128] = 1.0
    c["Emat"] = E.reshape(NBLK, NKT * 128)
    q = np.arange(128)
    cm = np.zeros((128, 16), np.float32)
    cm[:, 15] = np.where(q < 127, NEG, 0.0); cm[:, 14] = np.where(q < 63, NEG, 0.0)
    c["cmaskc"] = cm
    fq = np.zeros((128, 3), np.float32)
    fq[:, 2] = np.where(q >= 64, BIG, -1.0)
    fq[:, 1] = BIG
    fq[:, 0] = np.where(q < 64, BIG, 0.0)
    c["fq"] = fq
    NPB = cfg.NPB
    c["iota_c"] = np.arange(NPB + 1, dtype=np.float32).reshape(1, -1)
    fs = np.zeros((1, NPB + 1), np.float32); fs[0, 0] = BIG; fs[0, NPB - 1] = 2 * BIG; fs[0, NPB] = 3 * BIG
    c["forced_s"] = fs
    c["iota13"] = np.arange(13, dtype=np.float32).reshape(1, -1)
    tt = np.arange(128) % 8
    c["nmask"] = np.where(np.arange(8)[None, :] <= tt[:, None], 0.0, NEG).astype(np.float32)
    c["wmask0"] = np.where(np.arange(64)[None, :] >= tt[:, None] + 1, 0.0, NEG).astype(np.float32)
    c["idxw"] = ((np.arange(128) // 8)[:, None] * 8 + np.arange(8)[None, :]).astype(np.int32)
    return c


def build(cfg):
    nc = bass.Bass("TRN2", target_bir_lowering=False)
    D, DSG, DFF = cfg.D, cfg.DSG, cfg.DFF
    NTP, NTO = cfg.NTP, cfg.NTO
    NTOK = NTP * 128

    def din(name, shape, dt=F32):
        return nc.dram_tensor(name, list(shape), dt, kind="ExternalInput").ap()

    def dout(name, shape, dt=F32):
        return nc.dram_tensor(name, list(shape), dt, kind="ExternalOutput").ap()

    I = {}
    I["xp"] = din("xp", [NTOK, D]); I["xs"] = din("xs", [128, D])
    I["cvec"] = din("cvec", [1, 8])
    I["sconv"] = din("sconv", [2, 32, DFF])
    I["sidx"] = din("sidx", [128, NTO], I32); I["cidx"] = din("cidx", [128, 1], I32)
    I["gidx"] = din("gidx", [128, cfg.NKT], I32); I["gidxb"] = din("gidxb", [128, 1], I32)
    NPG, NPB = cfg.NPG, cfg.NPB
    I["pool_cmp"] = din("pool_cmp", [cfg.NPHYS * 8, 2048]); I["pool_slc"] = din("pool_slc", [cfg.NPHYS * 2, 8192])
    I["cwin"] = din("cwin", [128, 8192]); I["ptab"] = din("ptab", [16, NPG], I32)
    I["wkv_g"] = din("wkv_g", [D, 256]); I["relb_g"] = din("relb_g", [32, 4])
    I["iota_c"] = din("iota_c", [1, NPB + 1]); I["forced_s"] = din("forced_s", [1, NPB + 1]); I["iota13"] = din("iota13", [1, 13])
    I["nmask"] = din("nmask", [128, 8]); I["wmask0"] = din("wmask0", [128, 64])
    I["idxw"] = din("idxw", [128, 8], I32); I["yidx"] = din("yidx", [128, 1], I32)
    for n, shp in [("ln_g", [4, D]), ("ln_b", [4, D]), ("sg_w_in", [D, 2 * DSG]), ("sg_b_in", [1, 2 * DSG]),
                   ("sg_ln_g", [1, DSG]), ("sg_ln_b", [1, DSG]), ("sg_w_s", [8, 128, 128]), ("sg_b_s", [8, 128]),
                   ("sg_w_out", [DSG, D]), ("sg_b_out", [1, D]), ("ffn_w_up", [2, D, 2 * DFF]),
                   ("ffn_b_up", [2, 2 * DFF]), ("ffn_w_dw", [2, 3, DFF]), ("ffn_b_dw", [2, DFF]),
                   ("ffn_w_down", [2, DFF, D]), ("ffn_b_down", [2, D]), ("kv_w", [D, 1536]),
                   ("nsa_w_qg", [D, 1072]), ("nsa_b_qg", [1, 1072]), ("nsa_w_o", [D, D]), ("nsa_b_o", [1, D]),
                   ("wq_g", [D, 268]), ("bq_g", [1, 268]), ("wo_g", [256, D]),
                   ("ident", [128, 128]), ("tril_p", [128, 128]), ("tril_s", [128, 128]),
                   ("cmp_pe", [2, 64, 64]), ("cmp_w1", [2, 4096, 128]), ("cmp_b1", [2, 128]), ("cmp_w2", [2, 128, 64]),
                   ("cmp_b2", [1, 128]), ("rel_bias", [32, 16]), ("oh1", [32, 1152]), ("oh1r", [32, 1152]),
                   ("causT", [128, 128]), ("winfarT", [128, 128]), ("Emat", [cfg.NBLK, cfg.NKT * 128]),
                   ("cmaskc", [128, 16]), ("fq", [128, 3]), ("kvalid", [128, cfg.NKT]),
                   ("cvalid", [1, cfg.NBLK]), ("cval01", [1, cfg.NBLK]), ("firstblk", [1, cfg.NBLK])]:
        I[n] = din(n, shp)
    O = {}
    O["yp"] = dout("o_yp", [NTO * 128, D]); O["kvp"] = dout("o_kvp", [NTO * 128, 1536])
    O["convp"] = dout("o_convp", [2, 2, DFF])
    O["ys"] = dout("o_ys", [128, D]); O["kvs"] = dout("o_kvs", [128, 1536])
    O["convs"] = dout("o_convs", [2, 32, DFF]); O["sgv"] = dout("o_sgv", [128, DSG])

    with contextlib.ExitStack() as st:
        fw = FW(nc, st)
        OB = {k: fw.buf("out_" + k) for k in O}

        def MM(o, oap, l, lap, r, rap, start, stop):
            fw.op("pe", lambda h: h.matmul(oap, lhsT=lap, rhs=rap, start=start, stop=stop), [l.b, r.b], [o.b])

        def TR(o, oap, i, iap, idn):
            n = iap.shape[0]
            fw.op("pe", lambda h: h.transpose(out=oap, in_=iap, identity=idn.t[0:n, 0:n]), [i.b, idn.b], [o.b])

        def ACT(o, oap, i, iap, func, bias=None, scale=1.0, extra=()):
            rd = [i.b] + [x.b for x in extra]
            if bias is None:
                fw.op("act", lambda h: h.activation(out=oap, in_=iap, func=func, scale=scale), rd, [o.b])
            else:
                fw.op("act", lambda h: h.activation(out=oap, in_=iap, func=func, bias=bias, scale=scale), rd, [o.b])

        def TT(en, o, oap, a, aap, b, bap, op):
            fw.op(en, lambda h: h.tensor_tensor(out=oap, in0=aap, in1=bap, op=op), [a.b, b.b], [o.b])

        def TS(en, o, oap, a, aap, s1, s2, op0, op1=None, extra=()):
            rd = [a.b] + [x.b for x in extra]
            if op1 is None:
                fw.op(en, lambda h: h.tensor_scalar(out=oap, in0=aap, scalar1=s1, scalar2=None, op0=op0), rd, [o.b])
            else:
                fw.op(en, lambda h: h.tensor_scalar(out=oap, in0=aap, scalar1=s1, scalar2=s2, op0=op0, op1=op1), rd, [o.b])

        def STT(o, oap, a, aap, sc, b, bap, op0, op1, extra=()):
            rd = [a.b, b.b] + [x.b for x in extra]
            fw.op("dve", lambda h: h.scalar_tensor_tensor(out=oap, in0=aap, scalar=sc, in1=bap, op0=op0, op1=op1), rd, [o.b])

        def CP(en, o, oap, i, iap):
            if en == "act":
                fw.op("act", lambda h: h.copy(out=oap, in_=iap), [i.b], [o.b])
            else:
                fw.op(en, lambda h: h.tensor_copy(out=oap, in_=iap), [i.b], [o.b])

        def LD(o, oap, src_ap, src_b=None, q="sp", **kw):
            fw.dma(q, lambda h: h.dma_start(out=oap, in_=src_ap, **kw), [src_b] if src_b else [], [o.b])

        def STO(dst_ap, dst_b, i, iap, q="sp", **kw):
            fw.dma(q, lambda h: h.dma_start(out=dst_ap, in_=iap, **kw), [i.b], [dst_b])

        def bcast(ap_row, n, parts=128):
            return bass.AP(tensor=ap_row.tensor, offset=ap_row.offset, ap=[[0, parts], [1, n]])

        PM = [fw.ps(f"pm{i}", [128, 512], F32) for i in range(3)]
        PF = [fw.ps(f"pf{i}", [128, 4, 128], F32) for i in range(3)]
        PT = [fw.ps(f"pt{i}", [128, 8, 128], BF16) for i in range(2)]
        rr = {"pm": 0, "pf": 0, "pt": 0}

        def nxt(kind):
            lst = {"pm": PM, "pf": PF, "pt": PT}[kind]
            rr[kind] = (rr[kind] + 1) % len(lst)
            return lst[rr[kind]]

        idf = fw.sb("idf", [128, 128], F32); idb = fw.sb("idb", [128, 128], BF16)
        LD(idf, idf.t[:], I["ident"][:, :])
        CP("dve", idb, idb.t[:], idf, idf.t[:])
        cv = fw.sb("cv", [128, 8], F32)
        LD(cv, cv.t[:], bcast(I["cvec"][0:1, :], 8))

        WS = {}

        def prep_w(name, src, K, cols):
            nks = (K + 1023) // 1024
            scr = fw.dram("ws_" + name, [len(cols), nks, 128, 8, 512], BF16)
            for cb, pieces in enumerate(cols):
                for ks in range(nks):
                    nkc = min(8, K // 128 - ks * 8)
                    off = 0
                    for (c0, w) in pieces:
                        sap = src[ks * 1024: ks * 1024 + nkc * 128, c0:c0 + w].rearrange("(kc p) n -> p kc n", p=128)
                        dap = scr.t.ap()[cb, ks, :, 0:nkc, off:off + w]
                        fw.dma("pool", lambda h, dap=dap, sap=sap: h.dma_start(out=dap, in_=sap), [], [scr.b])
                        off += w
            WS[name] = (scr, nks, K)

        def blocks(n0, n, w=512):
            return [[(c, min(w, n0 + n - c))] for c in range(n0, n0 + n, w)]

        prep_w("w_in_v", I["sg_w_in"], D, blocks(DSG, DSG))
        prep_w("w_in_u", I["sg_w_in"], D, blocks(0, DSG))
        prep_w("w_out", I["sg_w_out"], DSG, blocks(0, D))
        for l in range(2):
            prep_w(f"w_up{l}", I["ffn_w_up"][l], D, [[(256 * j, 256), (DFF + 256 * j, 256)] for j in range(11)])
            prep_w(f"w_dn{l}", I["ffn_w_down"][l], DFF, blocks(0, D))
        prep_w("w_kv", I["kv_w"], D, blocks(0, 1536))
        prep_w("w_q", I["nsa_w_qg"], D, blocks(0, 1024))
        prep_w("w_gt", I["nsa_w_qg"], D, [[(1024, 48)]])
        prep_w("w_o", I["nsa_w_o"], D, blocks(0, D))
        prep_w("w_qs", I["wq_g"], D, [[(0, 268)]])
        prep_w("w_kvs", I["wkv_g"], D, [[(0, 256)]])
        prep_w("w_og", I["wo_g"], 256, blocks(0, D))

        NSLOT = 4
        slots = [fw.sb(f"slab{i}", [128, 8, 512], BF16) for i in range(NSLOT)]

        class SlabStream:
            def __init__(self, specs, look=3):
                self.specs = specs; self.issued = 0; self.used = 0; self.look = look

            def _issue(self):
                name, cb, ks = self.specs[self.issued]
                scr, nks, K = WS[name]
                sl = slots[self.issued % NSLOT]
                LD(sl, sl.t[:], scr.t.ap()[cb, ks], scr.b)
                self.issued += 1

            def get(self, name, cb, ks=0):
                assert self.specs[self.used] == (name, cb, ks), (self.specs[self.used], name, cb, ks)
                while self.issued < min(len(self.specs), self.used + self.look):
                    self._issue()
                sl = slots[self.used % NSLOT]
                self.used += 1
                return sl

        bias_list = [("b_in_v", I["sg_b_in"][0:1, DSG:2 * DSG], DSG), ("b_out", I["sg_b_out"][0:1, :], D),
                     ("b_dn0", I["ffn_b_down"][0:1, :], D), ("b_dn1", I["ffn_b_down"][1:2, :], D),
                     ("b_o", I["nsa_b_o"][0:1, :], D), ("b_gt", I["nsa_b_qg"][0:1, 1024:1072], 48),
                     ("b_qs", I["bq_g"][0:1, :], 268)]
        NBP = 128 * 60
        bcat = fw.dram("bcat", [NBP], F32); bhs = fw.dram("bhs", [NBP], BF16); bls = fw.dram("bls", [NBP], BF16)
        bias_hl = fw.sb("bias_hl", [128, 6 * 512], BF16)
        ones_t = fw.sb("ones_t", [128, 128], BF16)
        fw.op("dve", lambda h: h.memset(ones_t.t[:], 1.0), [], [ones_t.b])
        BIASPOS = {}
        with contextlib.ExitStack() as sb_:
            bf_ = fw.sb("bf_", [128, 60], F32, sb_); bh_ = fw.sb("bh_", [128, 60], BF16, sb_)
            bt_ = fw.sb("bt_", [128, 60], F32, sb_); bl_ = fw.sb("bl_", [128, 60], BF16, sb_)
            fw.op("dve", lambda h: h.memset(bf_.t[:], 0.0), [], [bf_.b])
            STO(bcat.t.ap().rearrange("(p k) -> p k", k=60), bcat.b, bf_, bf_.t[:])
            off = 0; blk = 0
            for (nm, row, n) in bias_list:
                fw.dma("sp", lambda h, row=row, off=off, n=n: h.dma_start(out=bcat.t.ap()[off:off + n].rearrange("(o n) -> o n", o=1), in_=row),
                       [], [bcat.b])
                for c0 in range(0, n, 512):
                    BIASPOS[(nm, c0 // 512)] = (32 * (blk % 3), 512 * (blk // 3), off + c0, min(512, n - c0)); blk += 1
                off += n
            assert off <= NBP and blk <= 18
            LD(bf_, bf_.t[:], bcat.t.ap().rearrange("(p k) -> p k", k=60), bcat.b)
            CP("dve", bh_, bh_.t[:], bf_, bf_.t[:])
            CP("dve", bt_, bt_.t[:], bh_, bh_.t[:])
            TT("dve", bt_, bt_.t[:], bf_, bf_.t[:], bt_, bt_.t[:], ALU.subtract)
            CP("dve", bl_, bl_.t[:], bt_, bt_.t[:])
            STO(bhs.t.ap().rearrange("(p k) -> p k", k=60), bhs.b, bh_, bh_.t[:])
            STO(bls.t.ap().rearrange("(p k) -> p k", k=60), bls.b, bl_, bl_.t[:])
            for key, (pb, col, o, w) in BIASPOS.items():
                LD(bias_hl, bias_hl.t[pb:pb + 1, col:col + w], bhs.t.ap()[o:o + w].rearrange("(o n) -> o n", o=1), bhs.b)
                LD(bias_hl, bias_hl.t[pb + 1:pb + 2, col:col + w], bls.t.ap()[o:o + w].rearrange("(o n) -> o n", o=1), bls.b)
            fw.barrier()

        def add_bias(p, pap, nm, cb, w):
            pb, col, o, wb = BIASPOS[(nm, cb)]
            MM(p, pap, ones_t, ones_t.t[pb:pb + 2, :], bias_hl, bias_hl.t[pb:pb + 2, col:col + w], False, True)

        def bc_tile(name, src_row, n, stack=None):
            t = fw.sb(name, [128, n], F32, stack)
            LD(t, t.t[:], bcast(src_row, n))
            return t

        lng = fw.sb("lng", [128, 2, D], F32); lnb = fw.sb("lnb", [128, 2, D], F32)

        def load_ln(l):
            for j in range(2):
                LD(lng, lng.t[:, j, :], bcast(I["ln_g"][2 * l + j:2 * l + j + 1, :], D))
                LD(lnb, lnb.t[:, j, :], bcast(I["ln_b"][2 * l + j:2 * l + j + 1, :], D))

        def pp_tile(name, src_row, nch):
            t = fw.sb(name, [128, nch], F32)
            fw.dma("sp", lambda h: h.dma_start(out=t.t[:], in_=src_row.rearrange("o (c p) -> p (o c)", p=128),
                                               allow_slow_non_contiguous=True), [], [t.b])
            return t

        b_in_u = pp_tile("b_in_u", I["sg_b_in"][0:1, 0:DSG], 24)
        b_up = [pp_tile(f"b_up{l}", I["ffn_b_up"][l:l + 1, :], 44) for l in range(2)]
        b_dw = [pp_tile(f"b_dw{l}", I["ffn_b_dw"][l:l + 1, :], 22) for l in range(2)]
        w_dw = [[pp_tile(f"w_dw{l}_{k}", I["ffn_w_dw"][l, k:k + 1, :], 22) for k in range(3)] for l in range(2)]
        bq = fw.sb("bq", [64, 16], F32)
        fw.dma("sp", lambda h: h.dma_start(out=bq.t[:], in_=I["nsa_b_qg"][0:1, 0:1024].rearrange("o (c p) -> p (o c)", p=64),
                                           allow_slow_non_contiguous=True), [], [bq.b])

        wts_shared = []

        def make_wsT(name, sample, stack):
            if not wts_shared:
                wts_shared.append(fw.sb("wts_f", [128, 8, 128], F32, stack)); wts_shared.append(fw.sb("wts_m", [128, 128], F32, stack))
            wts, msk = wts_shared
            if not sample:
                LD(wts, wts.t[:], I["sg_w_s"].rearrange("g t s -> t g s"))
            else:
                fw.op("pool", lambda h: h.memset(wts.t[:], 0.0), [], [wts.b])
                for b in range(16):
                    LD(wts, wts.t[8 * b:8 * b + 8, :, 8 * b:8 * b + 8],
                       I["sg_w_s"][:, 0:8, 0:8].rearrange("g t s -> t g s"), allow_slow_non_contiguous=True)
            LD(msk, msk.t[:], I["tril_s" if sample else "tril_p"][:, :])
            wT = fw.sb(name, [128, 8, 128], BF16, stack)
            for g in range(8):
                p = nxt("pf")
                fw.op("pe", lambda h, p=p, g=g: h.transpose(out=p.t[:, 0, :], in_=wts.t[:, g, :], identity=idf.t[:]),
                      [wts.b, idf.b], [p.b])
                TT("dve", wT, wT.t[:, g, :], p, p.t[:, 0, :], msk, msk.t[:], ALU.mult)
            bs = fw.sb(name + "_b", [128, 8, 128], F32, stack)
            if not sample:
                LD(bs, bs.t[:], bass.AP(tensor=I["sg_b_s"].tensor, offset=I["sg_b_s"].offset, ap=[[0, 128], [128, 8], [1, 128]]))
            else:
                for g in range(8):
                    LD(bs, bs.t[:, g, :].rearrange("p (b t) -> p b t", t=8),
                       bass.AP(tensor=I["sg_b_s"].tensor, offset=I["sg_b_s"].offset + 128 * g, ap=[[0, 128], [0, 16], [1, 8]]),
                       allow_slow_non_contiguous=True)
            return wT, bs

        gates = fw.sb("gates", [128, NTP, 48], F32)
        qs = fw.sb("qs", [128, 256], F32); gates_s = fw.sb("gates_s", [128, 12], F32)
        kvn = fw.sb("kvn", [128, 256], F32); kvnscr = fw.dram("kvnscr", [128, 256], F32)
        stats = fw.sb("stats", [128, 6, 6], F32); mv = fw.sb("mv", [128, 2], F32); rstd = fw.sb("rstd", [128, 1], F32)
        xcur = fw.sb("xcur", [128, D], F32); xb = fw.sb("xb", [128, D], BF16); xT = fw.sb("xT", [128, 8, 128], BF16)
        rtmp = fw.sb("rtmp", [128, D], F32)
        hT = fw.sb("hT", [128, 22, 128], BF16)
        a2 = fw.sb("a2", [128, 2, 160], F32)
        cm = fw.sb("cm", [128, 2, 128], F32); hc = fw.sb("hc", [128, 2, 128], F32)
        cstate_p = fw.sb("cstate_p", [128, 22, 2], F32)
        cstate_s = fw.sb("cstate_s", [128, 22, 32], F32)
        sA = contextlib.ExitStack(); sA0 = contextlib.ExitStack()
        cmpT = fw.sb("cmpT", [64, 2, 128, 4 * NTP], BF16, sA0)
        zt = fw.sb("zt", [128, 1024], BF16, sA0)
        sidx = fw.sb("sidx_sb", [128, NTO], I32, sA0); cidx = fw.sb("cidx_sb", [128, 1], I32, sA0)
        vg = fw.sb("vg", [128, DSG], F32, sA); v_bf = fw.sb("v_bf", [128, DSG], BF16, sA)
        u4 = fw.sb("u4", [128, 4, 128], F32, sA); g4 = fw.sb("g4", [128, 4, 128], F32, sA)
        zT = fw.sb("zT", [128, 24, 128], BF16, sA)
        kv = fw.sb("kv", [128, 1536], F32, sA); kvb = fw.sb("kvb", [128, 1024], BF16, sA)
        qTt = fw.sb("qTt", [64, 16, 128], BF16, sA)
        cst = [(vg, vg.t[:, :])]
        gv_bc = bc_tile("gv_bc", I["sg_ln_g"][0:1, :], DSG, sA)
        bv_bc = bc_tile("bv_bc", I["sg_ln_b"][0:1, :], DSG, sA)
        wsT_p, bs_p = make_wsT("wsT_p", False, sA)
        wsT_s, bs_s = make_wsT("wsT_s", True, sA)

        x1scr = fw.dram("x1scr", [NTOK + 128, D], F32)
        qscr = fw.dram("qscr", [NTP, 16, 64, 128], BF16)
        NROWS = 2 * cfg.S + 2 * cfg.NKT
        xsrc = fw.dram("xsrc", [NROWS, 1024], BF16); xdst = fw.dram("xdst", [NROWS, 1024], BF16)
        fw.op("pool", lambda h: h.memset(zt.t[:], 0.0), [], [zt.b])
        for r0 in range(0, NROWS, 128):
            nr = min(128, NROWS - r0)
            fw.dma("pool", lambda h, r0=r0, nr=nr: h.dma_start(out=xsrc.t.ap()[r0:r0 + nr, :], in_=zt.t[0:nr, :]), [zt.b], [xsrc.b])
        LD(sidx, sidx.t[:], I["sidx"][:, :]); LD(cidx, cidx.t[:], I["cidx"][:, :])

        def scatter_rows(src, src_ap, idx, idx_ap):
            fw.dma("pool", lambda h: h.indirect_dma_start(out=xsrc.t.ap()[:, :], out_offset=bass.IndirectOffsetOnAxis(ap=idx_ap, axis=0),
                                                          in_=src_ap, in_offset=None, bounds_check=NROWS - 1, oob_is_err=False),
                   [src.b, idx.b], [xsrc.b])

        def to_T(src, dstT):
            CP("act", xb, xb.t[:], src, src.t[:])
            p = nxt("pt")
            for c in range(8):
                TR(p, p.t[:, c, :], xb, xb.t[:, c * 128:(c + 1) * 128], idb)
            CP("dve", dstT, dstT.t[:], p, p.t[:])

        def layer_norm(x, n, g=None, g_ap=None, b=None, b_ap=None):
            nch = n // 512
            for c in range(nch):
                fw.op("dve", lambda h, c=c: h.bn_stats(out=stats.t[:, c, :], in_=x.t[:, c * 512:(c + 1) * 512]), [x.b], [stats.b])
            fw.op("dve", lambda h: h.bn_aggr(out=mv.t[:], in_=stats.t[:, 0:nch, :]), [stats.b], [mv.b])
            TS("dve", rstd, rstd.t[:], mv, mv.t[:, 1:2], LN_EPS, None, ALU.add)
            fw.op("act", lambda h: h.activation(out=rstd.t[:], in_=rstd.t[:], func=AF.Sqrt), [rstd.b], [rstd.b])
            fw.op("dve", lambda h: h.reciprocal(out=rstd.t[:], in_=rstd.t[:]), [rstd.b], [rstd.b])
            TS("dve", x, x.t[:, 0:n], x, x.t[:, 0:n], mv.t[:, 0:1], rstd.t[:, 0:1], ALU.subtract, ALU.mult, extra=(mv, rstd))
            if g is not None:
                TT("pool", x, x.t[:, 0:n], x, x.t[:, 0:n], g, g_ap, ALU.mult)
                TT("dve", x, x.t[:, 0:n], x, x.t[:, 0:n], b, b_ap, ALU.add)

        def proj_tm(stream, wname, ncb, lhs, nkc_total, bias_hl, consume):
            nks = (nkc_total + 7) // 8
            for cb in range(ncb):
                p = nxt("pm")
                for ks in range(nks):
                    sl = stream.get(wname, cb, ks)
                    for kc in range(min(8, nkc_total - ks * 8)):
                        cc = ks * 8 + kc
                        MM(p, p.t[:], lhs, lhs.t[:, cc, :], sl, sl.t[:, kc, :], cc == 0, (bias_hl is None and cc == nkc_total - 1))
                if bias_hl is not None:
                    add_bias(p, p.t[:], bias_hl, cb, 512)
                consume(cb, p)

        def ffn(stream, l, sample, ti):
            B, T = (16, 8) if sample else (1, 128)
            cstate = cstate_s if sample else cstate_p
            a2v = a2.t[:, :, 0:B * (T + 2)].rearrange("p c (b t) -> p c b t", t=T + 2)
            to_T(xcur, xT)
            for j in range(11):
                sl = stream.get(f"w_up{l}", j, 0)
                p = nxt("pf")
                for q in range(4):
                    for kc in range(8):
                        MM(p, p.t[:, q, :], sl, sl.t[:, kc, q * 128:(q + 1) * 128], xT, xT.t[:, kc, :], kc == 0, kc == 7)
                CP("pool", a2, a2v[:, :, :, 0:2], cstate, cstate.t[:, 2 * j:2 * j + 2, :].rearrange("p c (b k) -> p c b k", k=2))
                for q in range(2):
                    c = 2 * j + q
                    ACT(a2, a2v[:, q, :, 2:T + 2], p, p.t[:, q, :].rearrange("p (b t) -> p b t", t=T), AF.Identity,
                        bias=b_up[l].t[:, c:c + 1], extra=(b_up[l],))
                CP("pool", cstate, cstate.t[:, 2 * j:2 * j + 2, :].rearrange("p c (b k) -> p c b k", k=2), a2, a2v[:, :, :, T:T + 2])
                for q in range(2):
                    c = 2 * j + q
                    cmv = cm.t[:, q, :].rearrange("p (b t) -> p b t", t=T)
                    TS("dve", cm, cmv, a2, a2v[:, q, :, 0:T], w_dw[l][0].t[:, c:c + 1], None, ALU.mult, extra=(w_dw[l][0],))
                    STT(cm, cmv, a2, a2v[:, q, :, 1:T + 1], w_dw[l][1].t[:, c:c + 1], cm, cmv, ALU.mult, ALU.add, extra=(w_dw[l][1],))
                    STT(cm, cmv, a2, a2v[:, q, :, 2:T + 2], w_dw[l][2].t[:, c:c + 1], cm, cmv, ALU.mult, ALU.add, extra=(w_dw[l][2],))
                    ACT(hc, hc.t[:, q, :], cm, cm.t[:, q, :], AF.Gelu, bias=b_dw[l].t[:, c:c + 1], extra=(b_dw[l],))
                    STT(hT, hT.t[:, c, :], p, p.t[:, 2 + q, :], b_up[l].t[:, 22 + c:23 + c], hc, hc.t[:, q, :], ALU.add, ALU.mult,
                        extra=(b_up[l],))

            def cons(cb, p):
                STT(rtmp, rtmp.t[:, cb * 512:(cb + 1) * 512], xcur, xcur.t[:, cb * 512:(cb + 1) * 512], ALPHA, p, p.t[:], ALU.mult, ALU.add)
            proj_tm(stream, f"w_dn{l}", 2, hT, 22, f"b_dn{l}", cons)
            CP("act", xcur, xcur.t[:], rtmp, rtmp.t[:])
            layer_norm(xcur, D, lng, lng.t[:, 1, :], lnb, lnb.t[:, 1, :])

        def conv_state_out(l, sample):
            cstate = cstate_s if sample else cstate_p
            ncol = 32 if sample else 2
            for c in range(22):
                p = nxt("pf")
                fw.op("pe", lambda h, p=p, c=c: h.transpose(out=p.t[0:ncol, 0, :], in_=cstate.t[:, c, 0:ncol], identity=idf.t[:]),
                      [cstate.b, idf.b], [p.b])
                CP("dve", cst[0][0], cst[0][1][0:ncol, c * 128:(c + 1) * 128], p, p.t[0:ncol, 0, :])
            if sample:
                STO(O["convs"][l], OB["convs"], cst[0][0], cst[0][1][0:32, 0:DFF])
            else:
                STO(O["convp"][l], OB["convp"], cst[0][0], cst[0][1][0:2, 0:DFF])

        def tile_specs_A(sample):
            sp = [("w_in_v", cb, 0) for cb in range(6)] + [("w_in_u", cb, 0) for cb in range(6)]
            sp += [("w_out", cb, ks) for cb in range(2) for ks in range(3)]
            sp += [("w_up0", j, 0) for j in range(11)] + [("w_dn0", cb, ks) for cb in range(2) for ks in range(3)]
            sp += [("w_kv", cb, 0) for cb in range(3)]
            sp += [("w_qs", 0, 0), ("w_kvs", 0, 0)] if sample else [("w_q", 0, 0), ("w_q", 1, 0), ("w_gt", 0, 0)]
            return sp

        specsA = []
        for ti in range(NTP):
            specsA += tile_specs_A(False)
        specsA += tile_specs_A(True)
        SA = SlabStream(specsA)
        load_ln(0)

        def phaseA_tile(ti, sample):
            wsT, bs = (wsT_s, bs_s) if sample else (wsT_p, bs_p)
            LD(xcur, xcur.t[:], I["xs"][:, :] if sample else I["xp"][ti * 128:(ti + 1) * 128, :])
            to_T(xcur, xT)

            def cons_v(cb, p):
                ACT(vg, vg.t[:, cb * 512:(cb + 1) * 512], p, p.t[:], AF.Gelu)
            proj_tm(SA, "w_in_v", 6, xT, 8, "b_in_v", cons_v)
            layer_norm(vg, DSG, gv_bc, gv_bc.t[:], bv_bc, bv_bc.t[:])
            if sample:
                STO(O["sgv"][:, :], OB["sgv"], vg, vg.t[:])
            CP("act", v_bf, v_bf.t[:], vg, vg.t[:])
            for cb in range(6):
                sl = SA.get("w_in_u", cb, 0)
                p = nxt("pf")
                for j in range(4):
                    for kc in range(8):
                        MM(p, p.t[:, j, :], sl, sl.t[:, kc, j * 128:(j + 1) * 128], xT, xT.t[:, kc, :], kc == 0, kc == 7)
                for j in range(4):
                    ACT(u4, u4.t[:, j, :], p, p.t[:, j, :], AF.Gelu, bias=b_in_u.t[:, cb * 4 + j:cb * 4 + j + 1], extra=(b_in_u,))
                pg = nxt("pf")
                for j in range(4):
                    cc = cb * 4 + j
                    MM(pg, pg.t[:, j, :], v_bf, v_bf.t[:, cc * 128:(cc + 1) * 128], wsT, wsT.t[:, cc // 3, :], True, True)
                for j in range(4):
                    cc = cb * 4 + j
                    TT("dve", g4, g4.t[:, j, :], pg, pg.t[:, j, :], bs, bs.t[:, cc // 3, :], ALU.add)
                TT("dve", zT, zT.t[:, cb * 4:cb * 4 + 4, :], u4, u4.t[:], g4, g4.t[:], ALU.mult)

            def cons_o(cb, p):
                STT(rtmp, rtmp.t[:, cb * 512:(cb + 1) * 512], xcur, xcur.t[:, cb * 512:(cb + 1) * 512], ALPHA, p, p.t[:], ALU.mult, ALU.add)
            proj_tm(SA, "w_out", 2, zT, 24, "b_out", cons_o)
            CP("act", xcur, xcur.t[:], rtmp, rtmp.t[:])
            layer_norm(xcur, D, lng, lng.t[:, 0, :], lnb, lnb.t[:, 0, :])
            if sample:
                LD(cst[0][0], cst[0][1][0:32, 0:DFF], I["sconv"][0])
                for c in range(22):
                    p = nxt("pf")
                    fw.op("pe", lambda h, p=p, c=c: h.transpose(out=p.t[:, 0, 0:32], in_=cst[0][1][0:32, c * 128:(c + 1) * 128],
                                                                identity=idf.t[0:32, 0:32]), [cst[0][0].b, idf.b], [p.b])
                    CP("dve", cstate_s, cstate_s.t[:, c, :], p, p.t[:, 0, 0:32])
            elif ti == 0:
                fw.op("pool", lambda h: h.memset(cstate_p.t[:], 0.0), [], [cstate_p.b])
            ffn(SA, 0, sample, ti)
            if not sample and ti == 0:
                TS("dve", cstate_p, cstate_p.t[:], cstate_p, cstate_p.t[:], cv.t[:, 0:1], None, ALU.mult, extra=(cv,))
            if sample or ti == NTP - 1:
                conv_state_out(0, sample)
            row0 = NTOK if sample else ti * 128
            STO(x1scr.t.ap()[row0:row0 + 128, :], x1scr.b, xcur, xcur.t[:])
            to_T(xcur, xT)

            def cons_kv(cb, p):
                CP("act", kv, kv.t[:, cb * 512:(cb + 1) * 512], p, p.t[:])
            proj_tm(SA, "w_kv", 3, xT, 8, None, cons_kv)
            if sample:
                STO(O["kvs"][:, :], OB["kvs"], kv, kv.t[:])
            elif ti >= 1:
                STO(O["kvp"][(ti - 1) * 128:ti * 128, :], OB["kvp"], kv, kv.t[:])
            if not sample:
                for k in range(2):
                    p = nxt("pf")
                    for g in range(4):
                        fw.op("pe", lambda h, p=p, g=g, k=k: h.transpose(out=p.t[0:64, g, :], in_=kv.t[:, k * 256 + g * 64:k * 256 + g * 64 + 64],
                                                                         identity=idf.t[:]), [kv.b, idf.b], [p.b])
                    CP("dve", cmpT, cmpT.t[:, k, :, :].rearrange("p t (g i) -> p g i t", g=4)[:, :, ti, :], p, p.t[0:64, :, :])
                CP("pool", kvb, kvb.t[:], kv, kv.t[:, 512:1536])
                if ti >= 1:
                    scatter_rows(kvb, kvb.t[:], sidx, sidx.t[:, ti - 1:ti])
                for cb in range(2):
                    sl = SA.get("w_q", cb, 0)
                    for hh in range(8):
                        p = nxt("pf")
                        for kc in range(8):
                            MM(p, p.t[0:64, 0, :], sl, sl.t[:, kc, hh * 64:(hh + 1) * 64], xT, xT.t[:, kc, :], kc == 0, kc == 7)
                        hd = cb * 8 + hh
                        ACT(qTt, qTt.t[:, hd, :], p, p.t[0:64, 0, :], AF.Identity, bias=bq.t[:, hd:hd + 1], extra=(bq,))
                STO(qscr.t.ap()[ti].rearrange("h d t -> d h t"), qscr.b, qTt, qTt.t[:])
                sl = SA.get("w_gt", 0, 0)
                p = nxt("pm")
                for kc in range(8):
                    MM(p, p.t[:, 0:48], xT, xT.t[:, kc, :], sl, sl.t[:, kc, 0:48], kc == 0, False)
                add_bias(p, p.t[:, 0:48], "b_gt", 0, 48)
                ACT(gates, gates.t[:, ti, :], p, p.t[:, 0:48], AF.Sigmoid)
            else:
                sl = SA.get("w_qs", 0, 0)
                p = nxt("pm")
                for kc in range(8):
                    MM(p, p.t[:, 0:268], xT, xT.t[:, kc, :], sl, sl.t[:, kc, 0:268], kc == 0, False)
                add_bias(p, p.t[:, 0:268], "b_qs", 0, 268)
                CP("dve", qs, qs.t[:], p, p.t[:, 0:256])
                ACT(gates_s, gates_s.t[:], p, p.t[:, 256:268], AF.Sigmoid)
                sl = SA.get("w_kvs", 0, 0)
                p = nxt("pm")
                for kc in range(8):
                    MM(p, p.t[:, 0:256], xT, xT.t[:, kc, :], sl, sl.t[:, kc, 0:256], kc == 0, kc == 7)
                CP("dve", kvn, kvn.t[:], p, p.t[:, 0:256])
                STO(kvnscr.t.ap()[:, :], kvnscr.b, kvn, kvn.t[:])

        def phaseS():
            NPG, NPB = cfg.NPG, cfg.NPB
            SCALE = 0.125
            IOA = bass.IndirectOffsetOnAxis
            sS = contextlib.ExitStack()

            def T(name, shape, dt=F32, stack=None):
                return fw.sb("s_" + name, shape, dt, stack or sS)
            sT = contextlib.ExitStack()
            relg = T("relg", [32, 4], F32, sT); r31g = T("r31g", [32, 4], F32, sT)
            LD(relg, relg.t[:], I["relb_g"][:, :]); LD(r31g, r31g.t[:], bcast(I["relb_g"][31:32, :], 4, 32))
            TT("dve", relg, relg.t[:], relg, relg.t[:], r31g, r31g.t[:], ALU.subtract)
            oh1r = T("oh1r", [32, 1152], F32, sT); LD(oh1r, oh1r.t[:], I["oh1r"][:, :])
            Frs = T("Frs", [4, 1152], F32, sT); frscr = fw.dram("frscr", [4, 1152], F32)
            for c3 in range(3):
                p = nxt("pm")
                MM(p, p.t[0:4, 0:384], relg, relg.t[:], oh1r, oh1r.t[:, c3 * 384:(c3 + 1) * 384], True, True)
                CP("dve", Frs, Frs.t[:, c3 * 384:(c3 + 1) * 384], p, p.t[0:4, 0:384])
            STO(frscr.t.ap()[:, :], frscr.b, Frs, Frs.t[:])
            fw.barrier()
            sT.close()

            def tbl(dst, dst_ap_fn, base, inner):
                for t in range(8):
                    LD(dst, dst_ap_fn(t), bass.AP(tensor=frscr.t, offset=base - t, ap=[[0, 16], [1152, 4]] + inner), frscr.b,
                       allow_slow_non_contiguous=True)
            bias_cs = T("bias_cs", [128, 4, 16])
            for t in range(8):
                for r in range(4):
                    LD(bias_cs, bias_cs.t[t:128:8, r, :], bass.AP(tensor=frscr.t, offset=63 - t + 1152 * r, ap=[[0, 16], [64, 16]]), frscr.b,
                       allow_slow_non_contiguous=True)
            wbias = T("wbias", [128, 4, 512]); tbl(wbias, lambda t: wbias.t[t:128:8, :, :], 512, [[1, 512]])
            nbias = T("nbias", [128, 4, 8]); tbl(nbias, lambda t: nbias.t[t:128:8, :, :], 1024, [[1, 8]])
            cand = T("cand", [128, 13, 4, 64])
            for jr in range(13):
                tbl(cand, lambda t, jr=jr: cand.t[t:128:8, jr, :, :], 960 - 64 * jr, [[1, 64]])
            nmask = T("nmask", [128, 8]); LD(nmask, nmask.t[:], I["nmask"][:, :])
            wmask0 = T("wmask0", [128, 64]); LD(wmask0, wmask0.t[:], I["wmask0"][:, :])
            idxw = T("idxw", [128, 8], I32); LD(idxw, idxw.t[:], I["idxw"][:, :])
            yidx = T("yidx", [128, 1], I32); LD(yidx, yidx.t[:], I["yidx"][:, :])
            iotc = T("iotc", [128, NPB + 1]); LD(iotc, iotc.t[:], bcast(I["iota_c"][0:1, :], NPB + 1))
            forced = T("forced", [128, NPB + 1]); LD(forced, forced.t[:], bcast(I["forced_s"][0:1, :], NPB + 1))
            iot13 = T("iot13", [128, 13]); LD(iot13, iot13.t[:], bcast(I["iota13"][0:1, :], 13))
            Vc_all = T("Vc_all", [128, 16, 2, 65], BF16)
            fw.op("pool", lambda h: h.memset(Vc_all.t[:, :, :, 64:65], 1.0), [], [Vc_all.b])
            lcs = T("lcs", [128, 4, NPB]); ees = T("ees", [128, 4, NPB])
            imps = T("imps", [128, NPB + 1]); sc2s = T("sc2s", [128, NPB + 1])
            qTs = T("qTs", [64, 4, 128], BF16)
            p = nxt("pf")
            for r in range(4):
                fw.op("pe", lambda h, p=p, r=r: h.transpose(out=p.t[0:64, r, :], in_=qs.t[:, r * 64:(r + 1) * 64], identity=idf.t[:]),
                      [qs.b, idf.b], [p.b])
            CP("dve", qTs, qTs.t[:], p, p.t[0:64, :, :])

            sS1 = contextlib.ExitStack()
            Wz, pebias, w2sb, b2bc = load_compress_weights(sS1)
            b2pp = T("b2pp", [64, 2], F32, sS1)
            fw.dma("sp", lambda h: h.dma_start(out=b2pp.t[:], in_=I["cmp_b2"].rearrange("o (k d) -> d (o k)", k=2), allow_slow_non_contiguous=True),
                   [], [b2pp.b])
            ptT = T("ptT", [128, 16], I32, sS1)
            fw.dma("sp", lambda h: h.dma_start(out=ptT.t[0:NPG, :], in_=I["ptab"].rearrange("b p -> p b"), allow_slow_non_contiguous=True), [], [ptT.b])
            ptf8 = T("ptf8", [128, 16], F32, sS1); pidxf = T("pidxf", [128, 16, 8], F32, sS1); pidx = T("pidx", [128, 16, 8], I32, sS1)
            CP("dve", ptf8, ptf8.t[0:NPG, :], ptT, ptT.t[0:NPG, :])
            TS("dve", ptf8, ptf8.t[0:NPG, :], ptf8, ptf8.t[0:NPG, :], 8.0, None, ALU.mult)
            TT("dve", pidxf, pidxf.t[0:NPG], ptf8, ptf8.t[0:NPG, :].unsqueeze(2).to_broadcast([NPG, 16, 8]),
               iotc, iotc.t[0:NPG, 0:8].unsqueeze(1).to_broadcast([NPG, 16, 8]), ALU.add)
            CP("dve", pidx, pidx.t[0:NPG], pidxf, pidxf.t[0:NPG])
            pgs = [T(f"pg{i}", [128, 2048], F32, sS1) for i in range(2)]
            rTs = [T(f"rT{i}", [64, 8, 128], BF16, sS1) for i in range(2)]
            pgbs = [T(f"pgb{i}", [128, 2048], BF16, sS1) for i in range(2)]
            hidTs = T("hidTs", [128, 4, 128], BF16, sS1)
            KcTb = [T(f"KcTb{i}", [64, 256], BF16, sS1) for i in range(2)]
            qbs = [T(f"qb{i}", [64, 4, 128], BF16, sS1) for i in range(2)]
            ACCc = PM[0]; LC = [PM[1], PM[2]]
            accv = ACCc.t[:].rearrange("p (a c) -> p a c", a=4)
            cnt = 0
            for b in range(16):
                for r8 in range(8):
                    pg = pgs[cnt % 2]; cnt += 1
                    fw.dma("pool", lambda h, pg=pg, b=b, r8=r8: h.indirect_dma_start(
                        out=pg.t[0:NPG, :], out_offset=None, in_=I["pool_cmp"][:, :],
                        in_offset=IOA(ap=pidx.t[0:NPG, b, r8:r8 + 1], axis=0)), [pidx.b], [pg.b])
                    pgb = pgbs[cnt % 2]
                    CP("act" if cnt % 2 == 0 else "pool", pgb, pgb.t[0:NPG, :], pg, pg.t[0:NPG, :])
                    for r4 in range(4):
                        p = nxt("pt"); rt = rTs[r4 % 2]
                        for q4 in range(4):
                            rr = r4 * 4 + q4
                            for k in range(2):
                                TR(p, p.t[0:64, q4 * 2 + k, 0:NPG], pgb, pgb.t[0:NPG, rr * 128 + k * 64:rr * 128 + k * 64 + 64], idb)
                        CP("dve", rt, rt.t[:, :, 0:NPG], p, p.t[0:64, :, 0:NPG])
                        for q4 in range(4):
                            lp = r8 * 16 + r4 * 4 + q4; h2 = lp // 64; l = lp % 64
                            for k in range(2):
                                MM(ACCc, accv[:, h2 * 2 + k, 0:NPG], Wz, Wz.t[:, l, k, :], rt, rt.t[:, q4 * 2 + k, 0:NPG], l == 0, l == 63)
                for h2 in range(2):
                    for k in range(2):
                        ACT(hidTs, hidTs.t[:, h2 * 2 + k, 0:NPG], ACCc, accv[:, h2 * 2 + k, 0:NPG], AF.Gelu, bias=pebias.t[:, k:k + 1], extra=(pebias,))
                KcT = KcTb[b % 2]
                pk = nxt("pf")
                for h2 in range(2):
                    MM(pk, pk.t[0:64, h2, 0:NPG], w2sb, w2sb.t[:, 0, :], hidTs, hidTs.t[:, h2 * 2, 0:NPG], True, True)
                for h2 in range(2):
                    ACT(KcT, KcT.t[:, 0:NPB].rearrange("d (p h) -> d h p", h=2)[:, h2, :], pk, pk.t[0:64, h2, 0:NPG], AF.Identity,
                        bias=b2pp.t[:, 0:1], extra=(b2pp,))
                pv = nxt("pf")
                for h2 in range(2):
                    MM(pv, pv.t[0:NPG, h2, 0:64], hidTs, hidTs.t[:, h2 * 2 + 1, 0:NPG], w2sb, w2sb.t[:, 1, :], True, True)
                TT("dve", Vc_all, Vc_all.t[0:NPG, b, :, 0:64], pv, pv.t[0:NPG, 0:2, 0:64], b2bc,
                   b2bc.t[0:NPG, 64:128].unsqueeze(1).to_broadcast([NPG, 2, 64]), ALU.add)
                qbt = qbs[b % 2]
                fw.op("pool", lambda h, qbt=qbt: h.memset(qbt.t[:], 0.0), [], [qbt.b])
                CP("dve", qbt, qbt.t[:, :, 8 * b:8 * b + 8], qTs, qTs.t[:, :, 8 * b:8 * b + 8])
                for r in range(4):
                    MM(LC[r // 2], LC[r // 2].t[:].rearrange("p (a c) -> p a c", a=2)[:, r % 2, 0:NPB], qbt, qbt.t[:, r, :], KcT, KcT.t[:, 0:NPB],
                       b == 0, b == 15)
            for hf in range(2):
                TS("dve", lcs, lcs.t[:, 2 * hf:2 * hf + 2, :], LC[hf], LC[hf].t[:].rearrange("p (a c) -> p a c", a=2)[:, :, 0:NPB], SCALE, None, ALU.mult)
            fw.barrier()
            sS1.close()

            sS2 = contextlib.ExitStack()
            rmx = T("rmx", [128, 4], F32, sS2); sms = T("sms", [128, 4], F32, sS2)
            TT("dve", lcs, lcs.t[:, :, NPB - 16:NPB], lcs, lcs.t[:, :, NPB - 16:NPB], bias_cs, bias_cs.t[:], ALU.add)
            fw.op("dve", lambda h: h.tensor_reduce(out=rmx.t[:], in_=lcs.t[:], axis=AX.X, op=ALU.max), [lcs.b], [rmx.b])
            TS("dve", rmx, rmx.t[:], rmx, rmx.t[:], -100.0, -1.0, ALU.max, ALU.mult)
            for r in range(4):
                ACT(ees, ees.t[:, r, :], lcs, lcs.t[:, r, :], AF.Exp, bias=rmx.t[:, r:r + 1], extra=(rmx,))
            fw.op("dve", lambda h: h.tensor_reduce(out=sms.t[:], in_=ees.t[:], axis=AX.X, op=ALU.add), [ees.b], [sms.b])
            TS("dve", sms, sms.t[:], sms, sms.t[:], 1e-30, None, ALU.max)
            fw.op("dve", lambda h: h.reciprocal(out=sms.t[:], in_=sms.t[:]), [sms.b], [sms.b])
            fw.op("pool", lambda h: h.memset(imps.t[:], 0.0), [], [imps.b])
            TS("dve", imps, imps.t[:, 0:NPB], ees, ees.t[:, 0, :], sms.t[:, 0:1], None, ALU.mult, extra=(sms,))
            for r in range(1, 4):
                STT(imps, imps.t[:, 0:NPB], ees, ees.t[:, r, :], sms.t[:, r:r + 1], imps, imps.t[:, 0:NPB], ALU.mult, ALU.add, extra=(sms,))
            TT("dve", imps, imps.t[:], imps, imps.t[:], forced, forced.t[:], ALU.add)
            m8s = T("m8s", [128, 16], F32, sS2); i8s = T("i8s", [128, 16], U32, sS2)
            fw.op("dve", lambda h: h.max(out=m8s.t[:, 0:8], in_=imps.t[:]), [imps.b], [m8s.b])
            fw.op("dve", lambda h: h.max_index(out=i8s.t[:, 0:8], in_max=m8s.t[:, 0:8], in_values=imps.t[:]), [imps.b, m8s.b], [i8s.b])
            fw.op("dve", lambda h: h.match_replace(out=sc2s.t[:], in_to_replace=m8s.t[:, 0:8], in_values=imps.t[:], imm_value=-1e30),
                  [imps.b, m8s.b], [sc2s.b])
            fw.op("dve", lambda h: h.max(out=m8s.t[:, 8:16], in_=sc2s.t[:]), [sc2s.b], [m8s.b])
            fw.op("dve", lambda h: h.max_index(out=i8s.t[:, 8:16], in_max=m8s.t[:, 8:16], in_values=sc2s.t[:]), [sc2s.b, m8s.b], [i8s.b])
            idxf = T("idxf", [128, 16], F32, sS2)
            CP("dve", idxf, idxf.t[:], i8s, i8s.t[:])
            ecbs = T("ecbs", [128, 4, NPB], BF16, sS2)
            CP("pool", ecbs, ecbs.t[:], ees, ees.t[:])
            ecTs = T("ecTs", [128, 8, 128], BF16, sS2)
            p = nxt("pt")
            for h2 in range(2):
                for r in range(4):
                    TR(p, p.t[0:NPG, h2 * 4 + r, :], ecbs, ecbs.t[:, r, h2:NPB:2], idb)
            CP("dve", ecTs, ecTs.t[0:NPG], p, p.t[0:NPG])
            pmb = [T(f"pmb{i}", [128, 8, 128], BF16, sS2) for i in range(2)]
            ACO = PM[0]
            for b in range(16):
                t_ = pmb[b % 2]
                fw.op("pool", lambda h, t_=t_: h.memset(t_.t[:], 0.0), [], [t_.b])
                CP("dve", t_, t_.t[0:NPG, :, 8 * b:8 * b + 8], ecTs, ecTs.t[0:NPG, :, 8 * b:8 * b + 8])
                for h2 in range(2):
                    for r in range(4):
                        MM(ACO, ACO.t[:, r * 65:(r + 1) * 65], t_, t_.t[0:NPG, h2 * 4 + r, :], Vc_all, Vc_all.t[0:NPG, b, h2, :],
                           b == 0 and h2 == 0, b == 15 and h2 == 1)
            ptb = T("ptb", [128, 128], I32, sS2)
            for t in range(8):
                LD(ptb, ptb.t[t:128:8, 0:NPG], I["ptab"][:, :])
            tblf = T("tblf", [128, 128, 2], F32, sS2)
            CP("dve", tblf, tblf.t[:, 0:NPG, 0], ptb, ptb.t[:, 0:NPG])
            TS("dve", tblf, tblf.t[:, 0:NPG, 0], tblf, tblf.t[:, 0:NPG, 0], 2.0, None, ALU.mult)
            TS("dve", tblf, tblf.t[:, 0:NPG, 1], tblf, tblf.t[:, 0:NPG, 0], 1.0, None, ALU.add)
            tblv = tblf.t[:, 0:NPG, :].rearrange("p a b -> p (a b)")
            eqt = T("eqt", [128, 256], F32, sS2); physf = T("physf", [128, 16], F32, sS2); physi = T("physi", [128, 16], I32, sS2)
            for n in range(16):
                TS("dve", eqt, eqt.t[:, 0:NPB], iotc, iotc.t[:, 0:NPB], idxf.t[:, n:n + 1], None, ALU.is_equal, extra=(idxf,))
                TT("dve", eqt, eqt.t[:, 0:NPB], eqt, eqt.t[:, 0:NPB], tblf, tblv, ALU.mult)
                fw.op("dve", lambda h, n=n: h.tensor_reduce(out=physf.t[:, n:n + 1], in_=eqt.t[:, 0:NPB], axis=AX.X, op=ALU.add), [eqt.b], [physf.b])
            CP("dve", physi, physi.t[:], physf, physf.t[:])
            validn = T("validn", [128, 16], F32, sS2); jrf = T("jrf", [128, 16], F32, sS2)
            TS("dve", validn, validn.t[:], idxf, idxf.t[:], float(NPB), None, ALU.is_lt)
            TS("dve", jrf, jrf.t[:], idxf, idxf.t[:], -1.0, float(NPB - 1), ALU.mult, ALU.add)
            ohjr = T("ohjr", [128, 16, 13], F32, sS2)
            TT("dve", ohjr, ohjr.t[:], jrf, jrf.t[:].unsqueeze(2).to_broadcast([128, 16, 13]), iot13,
               iot13.t[:].unsqueeze(1).to_broadcast([128, 16, 13]), ALU.is_equal)

            blk = T("blk", [128, 8192], F32, sS2); prod = T("prod", [128, 64, 64], F32, sS2)
            newkv = T("newkv", [128, 8, 256], F32, sS2)
            for t in range(8):
                LD(newkv, newkv.t[t:128:8].rearrange("p a b -> p (a b)"), kvnscr.t.ap().rearrange("(b t) c -> b (t c)", t=8), kvnscr.b)
            lg = T("lg", [128, 4, 64], F32, sS2); pr = T("pr", [128, 4, 64], F32, sS2); sbn = T("sbn", [128, 4, 64], F32, sS2)
            otmp = T("otmp", [128, 4, 64], F32, sS2); dtmp = T("dtmp", [128, 4], F32, sS2)
            den2 = T("den2", [128, 2, 4], F32, sS2); acc2 = T("acc2", [128, 2, 4, 64], F32, sS2)
            fw.op("pool", lambda h: h.memset(den2.t[:], 0.0), [], [den2.b])
            fw.op("pool", lambda h: h.memset(acc2.t[:], 0.0), [], [acc2.b])

            def attend(src, K_ap, V_ap, L, bi, bias=None, mask=None, valid=None):
                for r in range(4):
                    TT("dve", prod, prod.t[:, 0:L, :], src, K_ap, qs, qs.t[:, r * 64:(r + 1) * 64].unsqueeze(1).to_broadcast([128, L, 64]), ALU.mult)
                    fw.op("dve", lambda h, r=r: h.tensor_reduce(out=lg.t[:, r, 0:L], in_=prod.t[:, 0:L, :], axis=AX.X, op=ALU.add), [prod.b], [lg.b])
                if bias is not None:
                    STT(lg, lg.t[:, :, 0:L], lg, lg.t[:, :, 0:L], SCALE, bias[0], bias[1], ALU.mult, ALU.add)
                else:
                    TS("dve", lg, lg.t[:, :, 0:L], lg, lg.t[:, :, 0:L], SCALE, None, ALU.mult)
                if mask is not None:
                    TT("dve", lg, lg.t[:, :, 0:L], lg, lg.t[:, :, 0:L], mask[0], mask[1].unsqueeze(1).to_broadcast([128, 4, L]), ALU.add)
                ACT(pr, pr.t[:, :, 0:L], lg, lg.t[:, :, 0:L], AF.Exp)
                if valid is not None:
                    TS("dve", pr, pr.t[:, :, 0:L], pr, pr.t[:, :, 0:L], valid[1], None, ALU.mult, extra=(valid[0],))
                fw.op("dve", lambda h: h.tensor_reduce(out=dtmp.t[:], in_=pr.t[:, :, 0:L], axis=AX.X, op=ALU.add), [pr.b], [dtmp.b])
                TT("dve", den2, den2.t[:, bi, :], den2, den2.t[:, bi, :], dtmp, dtmp.t[:], ALU.add)
                for r in range(4):
                    TT("dve", prod, prod.t[:, :, 0:L], src, V_ap, pr, pr.t[:, r, 0:L].unsqueeze(1).to_broadcast([128, 64, L]), ALU.mult)
                    fw.op("dve", lambda h, r=r: h.tensor_reduce(out=otmp.t[:, r, :], in_=prod.t[:, :, 0:L], axis=AX.X, op=ALU.add), [prod.b], [otmp.b])
                TT("dve", acc2, acc2.t[:, bi], acc2, acc2.t[:, bi], otmp, otmp.t[:], ALU.add)

            blk4 = blk.t[:, :].rearrange("p (l k d) -> p l k d", k=2, d=64)
            Kb = blk4[:, :, 0, :]; Vb = blk4[:, :, 1, :].rearrange("p l d -> p d l")
            for n in range(16):
                fw.dma("pool", lambda h, n=n: h.indirect_dma_start(out=blk.t[:, :], out_offset=None, in_=I["pool_slc"][:, :],
                                                                  in_offset=IOA(ap=physi.t[:, n:n + 1], axis=0)), [physi.b], [blk.b])
                TS("dve", sbn, sbn.t[:], cand, cand.t[:, 0], ohjr.t[:, n, 0:1], None, ALU.mult, extra=(ohjr,))
                for jr in range(1, 13):
                    STT(sbn, sbn.t[:], cand, cand.t[:, jr], ohjr.t[:, n, jr:jr + 1], sbn, sbn.t[:], ALU.mult, ALU.add, extra=(ohjr,))
                attend(blk, Kb, Vb, 64, 0, bias=(sbn, sbn.t[:]), valid=(validn, validn.t[:, n:n + 1]))
            attend(newkv, newkv.t[:, :, 0:64], newkv.t[:, :, 64:128].rearrange("p l d -> p d l"), 8, 0,
                   bias=(nbias, nbias.t[:]), mask=(nmask, nmask.t[:]))
            for wb in range(8):
                fw.dma("pool", lambda h, wb=wb: h.indirect_dma_start(out=blk.t[:, :], out_offset=None, in_=I["cwin"][:, :],
                                                                    in_offset=IOA(ap=idxw.t[:, wb:wb + 1], axis=0)), [idxw.b], [blk.b])
                attend(blk, Kb, Vb, 64, 1, bias=(wbias, wbias.t[:, :, wb * 64:(wb + 1) * 64]), mask=(wmask0, wmask0.t[:]) if wb == 0 else None)
            attend(newkv, newkv.t[:, :, 128:192], newkv.t[:, :, 192:256].rearrange("p l d -> p d l"), 8, 1,
                   bias=(nbias, nbias.t[:]), mask=(nmask, nmask.t[:]))
            dens = T("dens", [128, 3, 4], F32, sS2); wgt = T("wgt", [128, 3, 4], F32, sS2); osum = T("osum", [128, 4, 64], F32, sS2)
            CP("dve", dens, dens.t[:, 0, :], ACO, ACO.t[:, 0:260].rearrange("p (r e) -> p r e", e=65)[:, :, 64])
            CP("dve", dens, dens.t[:, 1:3, :], den2, den2.t[:])
            TS("dve", dens, dens.t[:], dens, dens.t[:], 1e-30, None, ALU.max)
            fw.op("dve", lambda h: h.reciprocal(out=dens.t[:], in_=dens.t[:]), [dens.b], [dens.b])
            TT("dve", wgt, wgt.t[:], dens, dens.t[:], gates_s, gates_s.t[:].rearrange("p (r b) -> p b r", b=3), ALU.mult)
            for r in range(4):
                TS("dve", osum, osum.t[:, r, :], ACO, ACO.t[:, r * 65:r * 65 + 64], wgt.t[:, 0, r:r + 1], None, ALU.mult, extra=(wgt,))
                for bi in (1, 2):
                    STT(osum, osum.t[:, r, :], acc2, acc2.t[:, bi - 1, r, :], wgt.t[:, bi, r:r + 1], osum, osum.t[:, r, :], ALU.mult, ALU.add,
                        extra=(wgt,))
            osb = T("osb", [128, 256], BF16, sS2); oTs = T("oTs", [128, 2, 128], BF16, sS2)
            CP("act", osb, osb.t[:], osum, osum.t[:].rearrange("p r d -> p (r d)"))
            p = nxt("pt")
            for c2 in range(2):
                TR(p, p.t[:, c2, :], osb, osb.t[:, c2 * 128:(c2 + 1) * 128], idb)
            CP("dve", oTs, oTs.t[:], p, p.t[:, 0:2, :])
            SS = SlabStream([("w_og", 0, 0), ("w_og", 1, 0)] + [("w_up1", jj, 0) for jj in range(11)] + [("w_dn1", cb, ks) for cb in range(2) for ks in range(3)])
            ypart = T("ypart", [128, 1024], F32, sS2)
            for cb in range(2):
                sl = SS.get("w_og", cb, 0)
                p = nxt("pm")
                for kc in range(2):
                    MM(p, p.t[:], oTs, oTs.t[:, kc, :], sl, sl.t[:, kc, :], kc == 0, kc == 1)
                CP("act", ypart, ypart.t[:, cb * 512:(cb + 1) * 512], p, p.t[:])
            ysrc = fw.dram("ysrc", [256, 1024], F32); ydst = fw.dram("ydst", [256, 1024], F32)
            fw.op("pool", lambda h: h.memset(blk.t[:, 0:1024], 0.0), [], [blk.b])
            for hh in range(2):
                STO(ysrc.t.ap()[hh * 128:(hh + 1) * 128, :], ysrc.b, blk, blk.t[:, 0:1024])
            fw.dma("pool", lambda h: h.indirect_dma_start(out=ysrc.t.ap()[:, :], out_offset=IOA(ap=yidx.t[:, 0:1], axis=0), in_=ypart.t[:, :],
                                                          in_offset=None), [ypart.b, yidx.b], [ysrc.b])
            fw.allreduce(ysrc, ydst)
            fw.dma("pool", lambda h: h.indirect_dma_start(out=ypart.t[:, :], out_offset=None, in_=ydst.t.ap()[:, :],
                                                          in_offset=IOA(ap=yidx.t[:, 0:1], axis=0)), [ydst.b, yidx.b], [ypart.b])
            load_ln(1)
            LD(xcur, xcur.t[:], x1scr.t.ap()[NTOK:NTOK + 128, :], x1scr.b)
            for cb in range(2):
                pb, col, o_, w_ = BIASPOS[("b_o", cb)]
                p = nxt("pm")
                MM(p, p.t[:], ones_t, ones_t.t[pb:pb + 2, :], bias_hl, bias_hl.t[pb:pb + 2, col:col + 512], True, True)
                TT("dve", rtmp, rtmp.t[:, cb * 512:(cb + 1) * 512], ypart, ypart.t[:, cb * 512:(cb + 1) * 512], p, p.t[:], ALU.add)
            STT(rtmp, rtmp.t[:], xcur, xcur.t[:], ALPHA, rtmp, rtmp.t[:], ALU.mult, ALU.add)
            CP("act", xcur, xcur.t[:], rtmp, rtmp.t[:])
            layer_norm(xcur, D, lng, lng.t[:, 0, :], lnb, lnb.t[:, 0, :])
            cst[0] = (blk, blk.t[:, :])
            LD(blk, blk.t[0:32, 0:DFF], I["sconv"][1])
            for c in range(22):
                p = nxt("pf")
                fw.op("pe", lambda h, p=p, c=c: h.transpose(out=p.t[:, 0, 0:32], in_=blk.t[0:32, c * 128:(c + 1) * 128],
                                                            identity=idf.t[0:32, 0:32]), [blk.b, idf.b], [p.b])
                CP("dve", cstate_s, cstate_s.t[:, c, :], p, p.t[:, 0, 0:32])
            ffn(SS, 1, True, 0)
            STO(O["ys"][:, :], OB["ys"], xcur, xcur.t[:])
            conv_state_out(1, True)
            fw.barrier()
            sS2.close(); sS.close()

        def phaseC():
            S_, NKT, NBLK = cfg.S, cfg.NKT, cfg.NBLK
            SCALE = 0.125
            NW = NTP + 4; KW0 = NKT - NW
            sC = contextlib.ExitStack()
            relb = fw.sb("relb", [32, 16], F32, sC); r31 = fw.sb("r31", [32, 16], F32, sC)
            LD(relb, relb.t[:], I["rel_bias"][:, :]); LD(r31, r31.t[:], bcast(I["rel_bias"][31:32, :], 16, 32))
            TT("dve", relb, relb.t[:], relb, relb.t[:], r31, r31.t[:], ALU.subtract)
            oh1 = fw.sb("oh1_sb", [32, 1152], F32, sC)
            LD(oh1, oh1.t[:], I["oh1"][:, :])
            Fsb = fw.sb("Fsb", [16, 1152], F32, sC)
            fscr = fw.dram("fscr", [16, 1152], F32)
            for c3 in range(3):
                p = nxt("pm")
                MM(p, p.t[0:16, 0:384], relb, relb.t[:], oh1, oh1.t[:, c3 * 384:(c3 + 1) * 384], True, True)
                CP("dve", Fsb, Fsb.t[:, c3 * 384:(c3 + 1) * 384], p, p.t[0:16, 0:384])
            STO(fscr.t.ap()[:, :], fscr.b, Fsb, Fsb.t[:])
            def ldc(name, shape, src, dt=F32, q="sp"):
                t = fw.sb(name + "_sb", shape, dt, sC)
                LD(t, t.t[:], src, q=q)
                return t
            Eb = fw.sb("Eb", [NBLK, NKT, 128], BF16, sC)
            LD(Eb, Eb.t[:].rearrange("c k p -> c (k p)"), I["Emat"][:, :], q="pool")
            causT = fw.sb("causT4", [128, 4, 128], BF16, sC); winfarT = fw.sb("winfarT4", [128, 4, 128], BF16, sC)
            for r in range(4):
                LD(causT, causT.t[:, r, :], I["causT"][:, :], q="pool"); LD(winfarT, winfarT.t[:, r, :], I["winfarT"][:, :], q="pool")
            cmaskc = ldc("cmaskc", [128, 16], I["cmaskc"][:, :]); fq = ldc("fq", [128, 3], I["fq"][:, :])
            kvalid = ldc("kvalid", [128, NKT], I["kvalid"][:, :])
            cvn = ldc("cvn", [128, NBLK], bcast(I["cvalid"][0:1, :], NBLK))
            cv01 = ldc("cv01", [128, NBLK], bcast(I["cval01"][0:1, :], NBLK))
            fbk = ldc("fbk", [128, NBLK], bcast(I["firstblk"][0:1, :], NBLK))
            TT("dve", cv01, cv01.t[:], cv01, cv01.t[:], fbk, fbk.t[:], ALU.add)
            gidx = ldc("gidx", [128, NKT], I["gidx"][:, :], I32); gidxb = ldc("gidxb", [128, 1], I["gidxb"][:, :], I32)
            ktscr = fw.dram("ktscr", [2, 4, 64, S_], BF16); vscr = fw.dram("vscr", [2, 4, NKT, 128, 64], BF16)
            oscr = fw.dram("oscr", [NTOK, 1024], F32)
            kvts = [fw.sb(f"kvt{i}", [128, 1024], BF16, sC) for i in range(2)]
            ktt = [fw.sb(f"ktt{i}", [64, 4, 128], BF16, sC) for i in range(2)]
            for kt in range(NKT):
                kvt = kvts[kt % 2]
                fw.dma("pool", lambda h, kvt=kvt, kt=kt: h.indirect_dma_start(
                    out=kvt.t[:], out_offset=None, in_=xdst.t.ap()[:, :],
                    in_offset=bass.IndirectOffsetOnAxis(ap=gidx.t[:, kt:kt + 1], axis=0)), [xdst.b, gidx.b], [kvt.b])
                for kind in range(2):
                    if kind == 1 and kt < KW0:
                        continue
                    p = nxt("pt"); kk = ktt[kind]
                    for g in range(4):
                        TR(p, p.t[0:64, g, :], kvt, kvt.t[:, kind * 512 + g * 64:kind * 512 + g * 64 + 64], idb)
                    CP("dve", kk, kk.t[:], p, p.t[0:64, 0:4, :])
                    STO(ktscr.t.ap()[kind, :, :, kt * 128:(kt + 1) * 128].rearrange("g d t -> d g t"), ktscr.b, kk, kk.t[:])
                    STO(vscr.t.ap()[kind, :, kt].rearrange("g p d -> p g d"), vscr.b, kvt,
                        kvt.t[:, kind * 512 + 256:kind * 512 + 512].rearrange("p (g d) -> p g d", g=4))
            kcb = fw.sb("kcb", [128, 512], BF16, sC)
            xdst512 = xdst.t.ap().rearrange("r (two c) -> (r two) c", two=2)
            fw.dma("pool", lambda h: h.indirect_dma_start(out=kcb.t[0:NBLK, :], out_offset=None, in_=xdst512,
                                                          in_offset=bass.IndirectOffsetOnAxis(ap=gidxb.t[0:NBLK, 0:1], axis=0)),
                   [xdst.b, gidxb.b], [kcb.b])
            KTs = fw.sb("KTs", [64, S_], BF16, sC); Vs = fw.sb("Vs", [128, NKT, 65], BF16, sC)
            KTw = fw.sb("KTw", [64, NW * 128], BF16, sC); Vw = fw.sb("Vw", [128, NW, 65], BF16, sC)
            KcT = fw.sb("KcT", [64, 128], BF16, sC); Vc = fw.sb("Vc", [128, 65], BF16, sC)
            qTg = fw.sb("qTg", [64, NTP, 4, 128], BF16, sC)
            Tn = fw.sb("Tn", [128, 4, 1024], F32, sC); Bc = fw.sb("Bc", [128, 4, 16], F32, sC); Bcr = fw.sb("Bcr", [128, 4, 16], F32, sC)
            fw.op("pool", lambda h: h.memset(Vs.t[:, :, 64:65], 1.0), [], [Vs.b])
            fw.op("pool", lambda h: h.memset(Vw.t[:, :, 64:65], 1.0), [], [Vw.b])
            fw.op("pool", lambda h: h.memset(Vc.t[:, 64:65], 1.0), [], [Vc.b])
            lc = fw.sb("lc", [128, 4, 128], F32, sC); ee = fw.sb("ee", [128, 4, 128], F32, sC)
            ecb = fw.sb("ecb", [128, 4, 128], BF16, sC); ecT = fw.sb("ecT", [128, 4, 128], BF16, sC)
            rmx = fw.sb("rmx", [128, 4], F32, sC); sms = fw.sb("sms", [128, 4], F32, sC)
            imp = fw.sb("imp", [128, 128], F32, sC); sc2 = fw.sb("sc2", [128, 128], F32, sC)
            m8 = fw.sb("m8", [128, 16], F32, sC)
            selneg = fw.sb("selneg", [128, 128], F32, sC); selT = fw.sb("selT", [128, 4, 128], BF16, sC)
            fw.op("pool", lambda h: h.memset(selneg.t[:], 0.0), [], [selneg.b])
            stmp = [fw.sb(f"stmp{i}", [128, 4, 128], F32, sC) for i in range(2)]
            PTb = [fw.sb(f"PTb{i}", [128, 4, 128], BF16, sC) for i in range(3)]
            dens = fw.sb("dens", [128, 3, 4], F32, sC); wgt = fw.sb("wgt", [128, 3, 4], F32, sC)
            og = fw.sb("og", [128, 4, 64], F32, sC)
            ACC = PM
            cnt = {"pt": 0, "st": 0}

            for g in range(4):
                LD(KTs, KTs.t[:], ktscr.t.ap()[0, g], ktscr.b)
                LD(Vs, Vs.t[:, :, 0:64], vscr.t.ap()[0, g].rearrange("k p d -> p k d"), vscr.b)
                LD(KTw, KTw.t[:], ktscr.t.ap()[1, g, :, KW0 * 128:], ktscr.b)
                LD(Vw, Vw.t[:, :, 0:64], vscr.t.ap()[1, g, KW0:].rearrange("k p d -> p k d"), vscr.b)
                p = nxt("pt")
                TR(p, p.t[0:64, 0, 0:NBLK], kcb, kcb.t[0:NBLK, g * 64:(g + 1) * 64], idb)
                CP("dve", KcT, KcT.t[:, 0:NBLK], p, p.t[0:64, 0, 0:NBLK])
                CP("dve", Vc, Vc.t[0:NBLK, 0:64], kcb, kcb.t[0:NBLK, 256 + g * 64:256 + (g + 1) * 64])
                for jj in range(NTP):
                    LD(qTg, qTg.t[:, jj, :, :], qscr.t.ap()[jj, 4 * g:4 * g + 4].rearrange("h d t -> d h t"), qscr.b)
                for r in range(4):
                    LD(Tn, Tn.t[:, r, :], bass.AP(tensor=fscr.t, offset=(4 * g + r) * 1152, ap=[[1, 128], [1, 1024]]), fscr.b)
                    LD(Bcr, Bcr.t[:, r, :], bass.AP(tensor=fscr.t, offset=(4 * g + r) * 1152, ap=[[1, 128], [64, 16]]), fscr.b,
                       allow_slow_non_contiguous=True)
                for a in range(16):
                    TS("dve", Bc, Bc.t[:, :, a], Bcr, Bcr.t[:, :, 15 - a], cmaskc.t[:, a:a + 1], None, ALU.add, extra=(cmaskc,))

                for j in range(NTP):
                    qt = NKT - NTP + j; ncol = 2 * qt + 2
                    qsl = qTg.t[:, j, :, :].rearrange("p r t -> p (r t)")
                    pl = nxt("pf")
                    for r in range(4):
                        MM(pl, pl.t[:, r, 0:ncol], qTg, qTg.t[:, j, r, :], KcT, KcT.t[:, 0:ncol], True, True)
                    STT(lc, lc.t[:, :, 0:ncol], pl, pl.t[:, :, 0:ncol], SCALE, cvn,
                        cvn.t[:, 0:ncol].unsqueeze(1).to_broadcast([128, 4, ncol]), ALU.mult, ALU.add)
                    TT("dve", lc, lc.t[:, :, ncol - 16:ncol], lc, lc.t[:, :, ncol - 16:ncol], Bc, Bc.t[:], ALU.add)
                    fw.op("dve", lambda h, ncol=ncol: h.tensor_reduce(out=rmx.t[:], in_=lc.t[:, :, 0:ncol], axis=AX.X, op=ALU.max), [lc.b], [rmx.b])
                    TS("dve", rmx, rmx.t[:], rmx, rmx.t[:], -100.0, -1.0, ALU.max, ALU.mult)
                    for r in range(4):
                        ACT(ee, ee.t[:, r, 0:ncol], lc, lc.t[:, r, 0:ncol], AF.Exp, bias=rmx.t[:, r:r + 1], extra=(rmx,))
                    fw.op("dve", lambda h, ncol=ncol: h.tensor_reduce(out=sms.t[:], in_=ee.t[:, :, 0:ncol], axis=AX.X, op=ALU.add), [ee.b], [sms.b])
                    TS("dve", sms, sms.t[:], sms, sms.t[:], 1e-30, None, ALU.max)
                    fw.op("dve", lambda h: h.reciprocal(out=sms.t[:], in_=sms.t[:]), [sms.b], [sms.b])
                    TS("dve", imp, imp.t[:, 0:ncol], ee, ee.t[:, 0, 0:ncol], sms.t[:, 0:1], None, ALU.mult, extra=(sms,))
                    for r in range(1, 4):
                        STT(imp, imp.t[:, 0:ncol], ee, ee.t[:, r, 0:ncol], sms.t[:, r:r + 1], imp, imp.t[:, 0:ncol], ALU.mult, ALU.add, extra=(sms,))
                    CP("pool", ecb, ecb.t[:, :, 0:ncol], ee, ee.t[:, :, 0:ncol])
                    p = nxt("pt")
                    for r in range(4):
                        TR(p, p.t[0:ncol, r, :], ecb, ecb.t[:, r, 0:ncol], idb)
                    CP("dve", ecT, ecT.t[0:ncol, :, :], p, p.t[0:ncol, 0:4, :])
                    for r in range(4):
                        MM(ACC[0], ACC[0].t[:, r * 65:(r + 1) * 65], ecT, ecT.t[0:ncol, r, :], Vc, Vc.t[0:ncol, :], True, True)
                    TT("dve", imp, imp.t[:, 0:ncol], imp, imp.t[:, 0:ncol], cv01, cv01.t[:, 0:ncol], ALU.add)
                    TT("dve", imp, imp.t[:, ncol - 3:ncol], imp, imp.t[:, ncol - 3:ncol], fq, fq.t[:], ALU.add)
                    fw.op("dve", lambda h, ncol=ncol: h.max(out=m8.t[:, 0:8], in_=imp.t[:, 0:ncol]), [imp.b], [m8.b])
                    fw.op("dve", lambda h, ncol=ncol: h.match_replace(out=sc2.t[:, 0:ncol], in_to_replace=m8.t[:, 0:8], in_values=imp.t[:, 0:ncol],
                                                                      imm_value=-1e30), [imp.b, m8.b], [sc2.b])
                    fw.op("dve", lambda h, ncol=ncol: h.max(out=m8.t[:, 8:16], in_=sc2.t[:, 0:ncol]), [sc2.b], [m8.b])
                    TS("dve", selneg, selneg.t[:, 0:ncol], imp, imp.t[:, 0:ncol], m8.t[:, 15:16], 1.0, ALU.is_ge, ALU.subtract, extra=(m8,))
                    TS("dve", selneg, selneg.t[:, 0:ncol], selneg, selneg.t[:, 0:ncol], -NEG * 8.0, None, ALU.mult)
                    pf_ = nxt("pf")
                    fw.op("pe", lambda h, pf_=pf_: h.transpose(out=pf_.t[0:NBLK, 0, :], in_=selneg.t[:, 0:NBLK], identity=idf.t[:]),
                          [selneg.b, idf.b], [pf_.b])
                    CP("dve", selT, selT.t[0:NBLK, :, :], pf_, pf_.t[0:NBLK, 0:1, :].to_broadcast([NBLK, 4, 128]))
                    selb = selT.t[0:NBLK, :, :].rearrange("p r t -> p (r t)")
                    def s1(tk):
                        (acc, KT, KTb, kcol, Vt, Vb, vidx, kt, first, last, masks, near) = tk
                        ps = nxt("pf")
                        MM(ps, ps.t[:].rearrange("p r t -> p (r t)"), KTb, KT[:, kcol * 128:(kcol + 1) * 128], qTg, qsl, True, len(masks) == 0)
                        for mi, (ltb, lap, rtb, rap) in enumerate(masks):
                            MM(ps, ps.t[:].rearrange("p r t -> p (r t)"), ltb, lap, rtb, rap, False, mi == len(masks) - 1)
                        return ps

                    def s2(tk, ps):
                        (acc, KT, KTb, kcol, Vt, Vb, vidx, kt, first, last, masks, near) = tk
                        pt_ = PTb[cnt["pt"] % 3]; cnt["pt"] += 1
                        if near is not None:
                            st_ = stmp[cnt["st"] % 2]; cnt["st"] += 1
                            STT(st_, st_.t[:], ps, ps.t[:], SCALE, Tn, Tn.t[:, :, near * 128:(near + 1) * 128], ALU.mult, ALU.add)
                            ACT(pt_, pt_.t[:], st_, st_.t[:], AF.Exp, bias=kvalid.t[:, kt:kt + 1], extra=(kvalid,))
                        else:
                            ACT(pt_, pt_.t[:], ps, ps.t[:], AF.Exp, bias=kvalid.t[:, kt:kt + 1], scale=SCALE, extra=(kvalid,))
                        return pt_

                    def s3(tk, pt_):
                        (acc, KT, KTb, kcol, Vt, Vb, vidx, kt, first, last, masks, near) = tk
                        for r in range(4):
                            MM(acc, acc.t[:, r * 65:(r + 1) * 65], pt_, pt_.t[:, r, :], Vb, Vt[:, vidx, :], first, last)

                    caus = (idb, idb.t[:], causT, causT.t[:].rearrange("p r t -> p (r t)"))
                    wfar = (idb, idb.t[:], winfarT, winfarT.t[:].rearrange("p r t -> p (r t)"))
                    tasks = []
                    for kt in range(qt + 1):
                        masks = [(Eb, Eb.t[0:NBLK, kt, :], selT, selb)]
                        if kt == qt:
                            masks.append(caus)
                        delta = qt - kt
                        tasks.append((ACC[1], KTs.t, KTs, kt, Vs.t, Vs, kt, kt, kt == 0, kt == qt, masks, delta if delta <= 7 else None))
                    for kt in range(qt - 4, qt + 1):
                        masks = []
                        if kt == qt:
                            masks.append(caus)
                        if kt == qt - 4:
                            masks.append(wfar)
                        tasks.append((ACC[2], KTw.t, KTw, kt - KW0, Vw.t, Vw, kt - KW0, kt, kt == qt - 4, kt == qt, masks, qt - kt))
                    LOOK = 2
                    live = {}
                    for i in range(min(LOOK, len(tasks))):
                        live[i] = s1(tasks[i])
                    for i in range(len(tasks)):
                        pt_ = s2(tasks[i], live.pop(i))
                        if i + LOOK < len(tasks):
                            live[i + LOOK] = s1(tasks[i + LOOK])
                        s3(tasks[i], pt_)
                    for bi in range(3):
                        CP("dve", dens, dens.t[:, bi, :], ACC[bi], ACC[bi].t[:, 0:260].rearrange("p (r e) -> p r e", e=65)[:, :, 64])
                    TS("dve", dens, dens.t[:], dens, dens.t[:], 1e-30, None, ALU.max)
                    fw.op("dve", lambda h: h.reciprocal(out=dens.t[:], in_=dens.t[:]), [dens.b], [dens.b])
                    TT("dve", wgt, wgt.t[:], dens, dens.t[:], gates,
                       gates.t[:, j, 12 * g:12 * g + 12].rearrange("p (r b) -> p b r", b=3), ALU.mult)
                    for r in range(4):
                        TS("dve", og, og.t[:, r, :], ACC[0], ACC[0].t[:, r * 65:r * 65 + 64], wgt.t[:, 0, r:r + 1], None, ALU.mult, extra=(wgt,))
                        for bi in (1, 2):
                            STT(og, og.t[:, r, :], ACC[bi], ACC[bi].t[:, r * 65:r * 65 + 64], wgt.t[:, bi, r:r + 1], og, og.t[:, r, :],
                                ALU.mult, ALU.add, extra=(wgt,))
                    STO(oscr.t.ap()[j * 128:(j + 1) * 128, g * 256:(g + 1) * 256], oscr.b, og, og.t[:].rearrange("p r d -> p (r d)"))

            load_ln(1)
            specsC = []
            for ti in range(NTP):
                specsC += [("w_o", 0, 0), ("w_o", 1, 0)] + [("w_up1", jj, 0) for jj in range(11)] + [("w_dn1", cb, ks) for cb in range(2) for ks in range(3)]
            SC = SlabStream(specsC)
            fw.op("pool", lambda h: h.memset(cstate_p.t[:], 0.0), [], [cstate_p.b])
            for ti in range(NTP):
                LD(rtmp, rtmp.t[:], oscr.t.ap()[ti * 128:(ti + 1) * 128, :], oscr.b)
                to_T(rtmp, xT)
                LD(xcur, xcur.t[:], x1scr.t.ap()[ti * 128:(ti + 1) * 128, :], x1scr.b)

                def cons_o(cb, p):
                    STT(rtmp, rtmp.t[:, cb * 512:(cb + 1) * 512], xcur, xcur.t[:, cb * 512:(cb + 1) * 512], ALPHA, p, p.t[:], ALU.mult, ALU.add)
                proj_tm(SC, "w_o", 2, xT, 8, "b_o", cons_o)
                CP("act", xcur, xcur.t[:], rtmp, rtmp.t[:])
                layer_norm(xcur, D, lng, lng.t[:, 0, :], lnb, lnb.t[:, 0, :])
                ffn(SC, 1, False, ti)
                if ti == 0:
                    TS("dve", cstate_p, cstate_p.t[:], cstate_p, cstate_p.t[:], cv.t[:, 0:1], None, ALU.mult, extra=(cv,))
                else:
                    STO(O["yp"][(ti - 1) * 128:ti * 128, :], OB["yp"], xcur, xcur.t[:])
            cst[0] = (Tn, Tn.t[:].rearrange("p r n -> p (r n)"))
            conv_state_out(1, False)
            fw.barrier()
            sC.close()

        for ti in range(NTP):
            phaseA_tile(ti, False)
        phaseA_tile(0, True)

        def load_compress_weights(stack):
            Wz = fw.sb("Wz", [64, 64, 2, 128], BF16, stack)
            for k in range(2):
                fw.dma("pool", lambda h, k=k: h.dma_start(out=Wz.t[:, :, k, :], in_=I["cmp_w1"][k].rearrange("(l d) h -> d l h", d=64)), [], [Wz.b])
            pef = fw.sb("pef", [64, 128], F32, stack); pez = fw.sb("pez", [64, 2, 64], BF16, stack)
            LD(pef, pef.t[:].rearrange("l (k d) -> l k d", k=2), I["cmp_pe"].rearrange("k l d -> l k d"))
            p = nxt("pf")
            for k in range(2):
                fw.op("pe", lambda h, k=k: h.transpose(out=p.t[0:64, k, 0:64], in_=pef.t[:, k * 64:(k + 1) * 64], identity=idf.t[0:64, 0:64]),
                      [pef.b, idf.b], [p.b])
            CP("dve", pez, pez.t[:], p, p.t[0:64, 0:2, 0:64])
            b1pp = fw.sb("b1pp", [128, 2], F32, stack)
            fw.dma("sp", lambda h: h.dma_start(out=b1pp.t[:], in_=I["cmp_b1"].rearrange("k h -> h k"), allow_slow_non_contiguous=True), [], [b1pp.b])
            pebias = fw.sb("pebias", [128, 2], F32, stack)
            p2 = nxt("pf")
            for k in range(2):
                for l in range(64):
                    MM(p2, p2.t[:, k, 0:1], Wz, Wz.t[:, l, k, :], pez, pez.t[:, k, l:l + 1], l == 0, l == 63)
            TT("dve", pebias, pebias.t[:], p2, p2.t[:, 0:2, 0], b1pp, b1pp.t[:], ALU.add)
            w2sb = fw.sb("w2sb", [128, 2, 64], BF16, stack)
            fw.dma("pool", lambda h: h.dma_start(out=w2sb.t[:], in_=I["cmp_w2"].rearrange("k h d -> h k d")), [], [w2sb.b])
            b2bc = fw.sb("b2bc", [128, 128], F32, stack)
            LD(b2bc, b2bc.t[:], bcast(I["cmp_b2"][0:1, :], 128))
            return Wz, pebias, w2sb, b2bc

        fw.barrier()
        sA.close()
        sA2 = contextlib.ExitStack()
        Wz, pebias, w2sb, b2bc = load_compress_weights(sA2)
        NCC = 4 * NTP
        hidT = fw.sb("hidT", [128, 4, NCC], BF16, sA2)
        for h2 in range(2):
            for k in range(2):
                p = nxt("pm")
                for l in range(64):
                    MM(p, p.t[:, 0:NCC], Wz, Wz.t[:, l, k, :], cmpT, cmpT.t[:, k, h2 * 64 + l, :], l == 0, l == 63)
                ACT(hidT, hidT.t[:, h2 * 2 + k, :], p, p.t[:, 0:NCC], AF.Gelu, bias=pebias.t[:, k:k + 1], extra=(pebias,))
        kvc_own = fw.sb("kvc_own", [128, 1024], BF16, sA2)
        for h2 in range(2):
            p = nxt("pm")
            for k in range(2):
                for g in range(4):
                    MM(p, p.t[0:NTP, (k * 4 + g) * 64:(k * 4 + g + 1) * 64], hidT, hidT.t[:, h2 * 2 + k, g * NTP:(g + 1) * NTP],
                       w2sb, w2sb.t[:, k, :], True, True)
            for k in range(2):
                TT("dve", kvc_own, kvc_own.t[0:NTP, h2 * 512 + k * 256:h2 * 512 + (k + 1) * 256].rearrange("p (g d) -> p g d", g=4),
                   p, p.t[0:NTP, k * 256:(k + 1) * 256].rearrange("p (g d) -> p g d", g=4),
                   b2bc, b2bc.t[0:NTP, k * 64:(k + 1) * 64].unsqueeze(1).to_broadcast([NTP, 4, 64]), ALU.add)
        scatter_rows(kvc_own, kvc_own.t[0:NTP, :], cidx, cidx.t[0:NTP, 0:1])
        fw.barrier()
        sA2.close(); sA0.close()
        fw.allreduce(xsrc, xdst)

        if cfg.stage >= 3:
            phaseS()
        if cfg.stage >= 2:
            phaseC()
            fw.finish(list(OB.values()))
            return nc

        if cfg.stage <= 1:
            for ti in range(1, NTP):
                LD(xcur, xcur.t[:], x1scr.t.ap()[ti * 128:(ti + 1) * 128, :], x1scr.b)
                STO(O["yp"][(ti - 1) * 128:ti * 128, :], OB["yp"], xcur, xcur.t[:])
            LD(xcur, xcur.t[:], x1scr.t.ap()[NTOK:NTOK + 128, :], x1scr.b)
            STO(O["ys"][:, :], OB["ys"], xcur, xcur.t[:])
            fw.finish(list(OB.values()))
            return nc

        fw.finish(list(OB.values()))
    return nc


def make_in_maps(cfg, inp):
    S = cfg.S; TOWN = cfg.TOWN
    consts = host_consts(cfg)
    f32 = np.float32
    shared = {
        "ln_g": np.ascontiguousarray(inp["ln_g"].reshape(4, 1024), f32), "ln_b": np.ascontiguousarray(inp["ln_b"].reshape(4, 1024), f32),
        "sg_w_in": inp["sg_w_in"][0], "sg_b_in": inp["sg_b_in"].reshape(1, -1), "sg_ln_g": inp["sg_ln_g"].reshape(1, -1),
        "sg_ln_b": inp["sg_ln_b"].reshape(1, -1), "sg_w_s": inp["sg_w_s"][0], "sg_b_s": inp["sg_b_s"][0],
        "sg_w_out": inp["sg_w_out"][0], "sg_b_out": inp["sg_b_out"].reshape(1, -1), "ffn_w_up": inp["ffn_w_up"],
        "ffn_b_up": inp["ffn_b_up"], "ffn_w_dw": inp["ffn_w_dw"], "ffn_b_dw": inp["ffn_b_dw"], "ffn_w_down": inp["ffn_w_down"],
        "ffn_b_down": inp["ffn_b_down"], "kv_w": inp["kv_w"], "nsa_w_qg": inp["nsa_w_qg"][0], "nsa_b_qg": inp["nsa_b_qg"].reshape(1, -1),
        "nsa_w_o": inp["nsa_w_o"][0], "nsa_b_o": inp["nsa_b_o"].reshape(1, -1),
        "cmp_pe": inp["cmp_pe"], "cmp_w1": inp["cmp_w1"], "cmp_b1": inp["cmp_b1"], "cmp_w2": inp["cmp_w2"],
        "cmp_b2": inp["cmp_b2"].reshape(1, -1), "rel_bias": inp["rel_bias"],
    }
    shared = {k: np.ascontiguousarray(v, dtype=f32) for k, v in shared.items()}
    shared.update(consts)
    maps = []
    for c in range(8):
        s, i = c // 4, c % 4
        g, bh = c % 4, c // 4
        m = dict(shared)
        xp = np.zeros((cfg.NTP * 128, 1024), f32)
        t0 = i * TOWN
        if i > 0:
            xp[0:128] = inp["x_prompt"][s, t0 - 128:t0]
        xp[128:] = inp["x_prompt"][s, t0:t0 + TOWN]
        m["xp"] = xp
        m["xs"] = np.ascontiguousarray(inp["x_sample"][16 * bh:16 * bh + 16].reshape(128, 1024), f32)
        cvec = np.zeros((1, 8), f32); cvec[0, 0] = 1.0 if i > 0 else 0.0
        m["cvec"] = cvec
        m["sconv"] = np.ascontiguousarray(inp["state_conv"][:, 16 * bh:16 * bh + 16].reshape(2, 32, cfg.DFF), f32)
        wq = inp["nsa_w_qg"][0]; bqv = inp["nsa_b_qg"][0]
        qcols = list(range(256 * g, 256 * g + 256)) + list(range(1024 + 12 * g, 1024 + 12 * g + 12))
        m["wq_g"] = np.ascontiguousarray(wq[:, qcols], f32)
        m["bq_g"] = np.ascontiguousarray(bqv[qcols].reshape(1, -1), f32)
        m["wo_g"] = np.ascontiguousarray(inp["nsa_w_o"][0][256 * g:256 * g + 256], f32)
        NKT, NBLK, NTO, NTP = cfg.NKT, cfg.NBLK, cfg.NTO, cfg.NTP
        NROWS = 2 * S + 2 * NKT
        pp = np.arange(128)
        m["sidx"] = (s * S + i * TOWN + np.arange(NTO)[None, :] * 128 + pp[:, None]).astype(np.int32)
        cidx = np.full((128, 1), NROWS + 7, np.int32)
        cidx[1:NTP, 0] = 2 * S + s * NKT + i * NTO + np.arange(NTO)
        m["cidx"] = cidx
        pad = (3 - i) * TOWN
        pos = np.arange(NKT)[None, :] * 128 + 127 - pp[:, None]
        real = pos - pad
        m["gidx"] = (s * S + np.maximum(real, 0)).astype(np.int32)
        m["kvalid"] = np.where(real >= 0, 0.0, NEG).astype(f32)
        rb = np.arange(NBLK) - pad // 64
        gb = np.zeros((128, 1), np.int32)
        gb[:NBLK, 0] = 4 * S + 2 * s * NKT + np.maximum(rb, 0)
        m["gidxb"] = gb
        m["cvalid"] = np.where(rb >= 0, 0.0, NEG).astype(f32).reshape(1, -1)
        m["cval01"] = np.where(rb >= 0, 0.0, -1.0).astype(f32).reshape(1, -1)
        fb = np.zeros((1, NBLK), f32); fb[0, pad // 64] = BIG
        m["firstblk"] = fb
        NPH = cfg.NPHYS
        m["pool_cmp"] = np.ascontiguousarray(inp["cache_kv_cmp"][:, :, :, g, :]).reshape(NPH * 8, 2048)
        m["pool_slc"] = np.ascontiguousarray(inp["cache_kv_slc"][:, :, :, g, :]).reshape(NPH * 2, 8192)
        m["cwin"] = np.ascontiguousarray(inp["cache_win"][16 * bh:16 * bh + 16, :, :, g, :]).reshape(128, 8192)
        m["ptab"] = np.ascontiguousarray(inp["page_table"][16 * bh:16 * bh + 16], np.int32)
        kvw = inp["kv_w"]
        kcols = [k0 * 256 + g * 64 + d for k0 in (2, 3, 4, 5) for d in range(64)]
        m["wkv_g"] = np.ascontiguousarray(kvw[:, kcols], f32)
        m["relb_g"] = np.ascontiguousarray(inp["rel_bias"][:, 4 * g:4 * g + 4], f32)
        m["yidx"] = (128 * bh + np.arange(128)).astype(np.int32).reshape(128, 1)
        maps.append(m)
    return maps


def assemble(cfg, res, inp):
    S = cfg.S; TOWN = cfg.TOWN; DFF = cfg.DFF
    B = inp["x_prompt"].shape[0]
    yp = np.zeros((B, S, 1024), np.float32); kvp = np.zeros((B, S, 1536), np.float32)
    convp = np.zeros((2, B, 2, DFF), np.float32)
    ys = np.zeros((32, 8, 1024), np.float32); kvs = np.zeros((32, 8, 1536), np.float32)
    convs = np.zeros((2, 32, 2, DFF), np.float32); sgv = np.zeros((1, 32, 8, 3072), np.float32)
    for c in range(8):
        s, i = c // 4, c % 4
        r = res[c]
        yp[s, i * TOWN:(i + 1) * TOWN] = r["o_yp"]; kvp[s, i * TOWN:(i + 1) * TOWN] = r["o_kvp"]
        if i == 3:
            convp[:, s] = r["o_convp"]
        if c % 4 == 0:
            bh = c // 4
            ys[16 * bh:16 * bh + 16] = r["o_ys"].reshape(16, 8, 1024)
            kvs[16 * bh:16 * bh + 16] = r["o_kvs"].reshape(16, 8, 1536)
            convs[:, 16 * bh:16 * bh + 16] = r["o_convs"].reshape(2, 16, 2, DFF)
            sgv[0, 16 * bh:16 * bh + 16] = r["o_sgv"].reshape(16, 8, 3072)
    kvp6 = kvp.reshape(B, S, 6, 4, 64); kvs6 = kvs.reshape(32, 8, 6, 4, 64)
    W = min(512, S)
    return (yp, ys, np.ascontiguousarray(kvp6[:, :, 0:2]), np.ascontiguousarray(kvp6[:, :, 2:4]),
            np.ascontiguousarray(kvp6[:, S - W:, 4:6]), convp,
            np.ascontiguousarray(kvs6[:, :, 0:2]), np.ascontiguousarray(kvs6[:, :, 2:4]), np.ascontiguousarray(kvs6[:, :, 4:6]),
            convs, sgv)


_NC_CACHE = {}


def run(cfg, inp):
    key = (cfg.S, cfg.PAST, cfg.NPHYS, cfg.stage)
    if key not in _NC_CACHE:
        _NC_CACHE[key] = build(cfg)
    nc = _NC_CACHE[key]
    maps = make_in_maps(cfg, inp)
    res = run_bass_kernel_spmd(nc, maps, core_ids=list(range(8)))
    return assemble(cfg, res.results, inp)


def kernel(**inputs):
    inp = {k: np.asarray(v) for k, v in inputs.items()}
    cfg = Cfg(S=inp["x_prompt"].shape[1], PAST=inp["page_table"].shape[1] * 128, NPHYS=inp["cache_kv_cmp"].shape[0])
    return run(cfg, inp)
```

```python
import contextlib, math
import numpy as np
import concourse.bass as bass
import concourse.mybir as mybir
from concourse.bass_utils import run_bass_kernel_spmd

F32 = mybir.dt.float32; BF16 = mybir.dt.bfloat16; I32 = mybir.dt.int32; U32 = mybir.dt.uint32
AF = mybir.ActivationFunctionType; ALU = mybir.AluOpType; AX = mybir.AxisListType

DEPTH = 2
ALPHA = (2 * DEPTH) ** 0.25
LN_EPS = 1e-5
NEG = -30000.0
BIG = 1e9


class Cfg:
    def __init__(self, S=8192, PAST=16384, NPHYS=5120, stage=9):
        self.S = S; self.PAST = PAST; self.NPHYS = NPHYS; self.stage = stage
        self.D = 1024; self.DSG = 3072; self.DFF = 2816; self.L = 8
        self.TOWN = S // 4; self.NTO = self.TOWN // 128; self.NTP = self.NTO + 1
        self.NKT = S // 128; self.NBLK = S // 64
        self.NPG = PAST // 128; self.NPB = PAST // 64


class Buf:
    __slots__ = ("name", "w", "r")

    def __init__(self, name):
        self.name = name; self.w = None; self.r = {}


class TB:
    __slots__ = ("t", "b")

    def __init__(self, t, b):
        self.t = t; self.b = b


class Eng:
    def __init__(self, name, h, sem):
        self.name = name; self.h = h; self.sem = sem; self.count = 0; self.seen = {}; self.ops = []


class FW:
    NQ = 8

    def __init__(self, nc, stack):
        self.nc = nc; self.stack = stack

        def S(n):
            return stack.enter_context(nc.semaphore(n))
        self.eng = {"pe": Eng("pe", nc.tensor, S("s_pe")), "act": Eng("act", nc.scalar, S("s_act")),
                    "dve": Eng("dve", nc.vector, S("s_dve")), "pool": Eng("pool", nc.gpsimd, S("s_pool")),
                    "sp": Eng("sp", nc.sync, S("s_sp"))}
        self.qsem = {q: [S(f"q_{q}_{i}") for i in range(self.NQ)] for q in ("sp", "pool")}
        self.qcnt = {"sp": 0, "pool": 0}
        self.ccsem = S("s_cc"); self.cccnt = 0
        self.nb = 0

    def buf(self, name=None):
        self.nb += 1
        return Buf(name or f"b{self.nb}")

    def sb(self, name, shape, dt, stack=None):
        name = f"{name}_{self.nb}"
        return TB((stack or self.stack).enter_context(self.nc.sbuf_tensor(name, list(shape), dt)), self.buf(name))

    def ps(self, name, shape, dt):
        return TB(self.stack.enter_context(self.nc.psum_tensor(name, list(shape), dt)), self.buf(name))

    def dram(self, name, shape, dt):
        return TB(self.nc.dram_tensor(name, list(shape), dt), self.buf(name))

    def _deps(self, e, reads, writes):
        deps = {}

        def add(tok):
            if tok is None:
                return
            s, v = tok
            if deps.get(s, 0) < v:
                deps[s] = v
        for b in reads:
            add(b.w)
        for b in writes:
            add(b.w)
            for s, v in b.r.items():
                add((s, v))
        out = []
        for s, v in deps.items():
            if s is e.sem and e.name == "pe":
                continue
            if e.seen.get(s, 0) >= v:
                continue
            e.seen[s] = v; out.append((s, v))
        return out

    def _mark(self, tok, reads, writes):
        s, v = tok
        for b in reads:
            if b.r.get(s, 0) < v:
                b.r[s] = v
        for b in writes:
            b.w = tok; b.r = {}

    def op(self, en, fn, reads=(), writes=()):
        e = self.eng[en]; waits = self._deps(e, reads, writes)
        e.count += 1; tok = (e.sem, e.count)

        def run(h=e.h, waits=waits, fn=fn, sem=e.sem):
            for s, v in waits:
                h.wait_ge(s, v)
            fn(h).then_inc(sem, 1)
        run(); self._mark(tok, reads, writes)
        return tok

    def dma(self, q, fn, reads=(), writes=()):
        e = self.eng[q]; i = self.qcnt[q]; self.qcnt[q] += 1
        sem = self.qsem[q][i % self.NQ]; val = 16 * (i // self.NQ + 1)
        waits = self._deps(e, reads, writes)
        if i >= self.NQ and e.seen.get(sem, 0) < val - 16:
            waits.append((sem, val - 16)); e.seen[sem] = val - 16
        tok = (sem, val)

        def run(h=e.h, waits=waits, fn=fn, sem=sem):
            for s, v in waits:
                h.wait_ge(s, v)
            fn(h).then_inc(sem, 16)
        run(); self._mark(tok, reads, writes)
        return tok

    def allreduce(self, src, dst, n=8):
        e = self.eng["pool"]; waits = self._deps(e, [src.b], [dst.b])
        self.cccnt += 1; tok = (self.ccsem, self.cccnt)

        def run(h=e.h, waits=waits, sem=self.ccsem):
            for s, v in waits:
                h.wait_ge(s, v)
            h.collective_compute("AllReduce", ALU.add, replica_groups=[list(range(n))],
                                 ins=[src.t.ap().opt()], outs=[dst.t.ap().opt()]).then_inc(sem)
        run(); self._mark(tok, [src.b], [dst.b])
        return tok

    def barrier(self):
        toks = [(e.sem, e.count) for e in self.eng.values() if e.count > 0]
        for q, n in self.qcnt.items():
            for k in range(min(n, self.NQ)):
                last = ((n - 1 - k) // self.NQ) * self.NQ + k
                toks.append((self.qsem[q][k], 16 * (last // self.NQ + 1)))
        if self.cccnt:
            toks.append((self.ccsem, self.cccnt))
        for e in self.eng.values():
            for s_, v in toks:
                if s_ is e.sem:
                    continue
                if e.seen.get(s_, 0) >= v:
                    continue
                e.seen[s_] = v
                e.h.wait_ge(s_, v)

    def finish(self, final_bufs):
        e = self.eng["sp"]
        for s_, v in self._deps(e, final_bufs, []):
            e.h.wait_ge(s_, v)


def t5_bucket_np(n):
    n = np.maximum(n, 0)
    nf = np.maximum(n, 1).astype(np.float32)
    large = 16 + (np.log(nf / np.float32(16)) / np.float32(math.log(1024 / 16)) * np.float32(16)).astype(np.int32)
    return np.where(n < 16, n, np.minimum(large, 31))


def host_consts(cfg):
    c = {}
    c["ident"] = np.eye(128, dtype=np.float32)
    s = np.arange(128)
    c["tril_p"] = (s[:, None] <= s[None, :]).astype(np.float32)
    c["tril_s"] = ((s[:, None] // 8 == s[None, :] // 8) & (s[:, None] <= s[None, :])).astype(np.float32)
    m = np.arange(1152)
    bk = t5_bucket_np(m - 127)
    oh = np.zeros((32, 1152), np.float32); oh[bk, m] = 1.0
    c["oh1"] = oh; c["oh1r"] = np.ascontiguousarray(oh[:, ::-1])
    p = np.arange(128)
    kl = 127 - p
    c["causT"] = np.where(kl[:, None] > s[None, :], NEG * 8, 0.0).astype(np.float32)
    c["winfarT"] = np.where(kl[:, None] <= s[None, :], NEG * 8, 0.0).astype(np.float32)
    NBLK, NKT = cfg.NBLK, cfg.NKT
    E = np.zeros((NBLK, NKT, 128), np.float32)
    for kt in range(NKT):
        E[2 * kt + 1, kt, 0:64] = 1.0
        E[2 * kt, kt, 64:128] = 1.0
    c["Emat"] = E.reshape(NBLK, NKT * 128)
    q = np.arange(128)
    cm = np.zeros((128, 16), np.float32)
    cm[:, 15] = np.where(q < 127, NEG, 0.0); cm[:, 14] = np.where(q < 63, NEG, 0.0)
    c["cmaskc"] = cm
    fq = np.zeros((128, 3), np.float32)
    fq[:, 2] = np.where(q >= 64, BIG, -1.0)
    fq[:, 1] = BIG
    fq[:, 0] = np.where(q < 64, BIG, 0.0)
    c["fq"] = fq
    NPB = cfg.NPB
    c["iota_c"] = np.arange(NPB + 1, dtype=np.float32).reshape(1, -1)
    fs = np.zeros((1, NPB + 1), np.float32); fs[0, 0] = BIG; fs[0, NPB - 1] = 2 * BIG; fs[0, NPB] = 3 * BIG
    c["forced_s"] = fs
    c["iota13"] = np.arange(13, dtype=np.float32).reshape(1, -1)
    tt = np.arange(128) % 8
    c["nmask"] = np.where(np.arange(8)[None, :] <= tt[:, None], 0.0, NEG).astype(np.float32)
    c["wmask0"] = np.where(np.arange(64)[None, :] >= tt[:, None] + 1, 0.0, NEG).astype(np.float32)
    c["idxw"] = ((np.arange(128) // 8)[:, None] * 8 + np.arange(8)[None, :]).astype(np.int32)
    return c


def build(cfg):
    nc = bass.Bass("TRN2", target_bir_lowering=False)
    D, DSG, DFF = cfg.D, cfg.DSG, cfg.DFF
    NTP, NTO = cfg.NTP, cfg.NTO
    NTOK = NTP * 128

    def din(name, shape, dt=F32):
        return nc.dram_tensor(name, list(shape), dt, kind="ExternalInput").ap()

    def dout(name, shape, dt=F32):
        return nc.dram_tensor(name, list(shape), dt, kind="ExternalOutput").ap()

    I = {}
    I["xp"] = din("xp", [NTOK, D]); I["xs"] = din("xs", [128, D])
    I["cvec"] = din("cvec", [1, 8])
    I["sconv"] = din("sconv", [2, 32, DFF])
    I["sidx"] = din("sidx", [128, NTO], I32); I["cidx"] = din("cidx", [128, 1], I32)
    I["gidx"] = din("gidx", [128, cfg.NKT], I32); I["gidxb"] = din("gidxb", [128, 1], I32)
    NPG, NPB = cfg.NPG, cfg.NPB
    I["pool_cmp"] = din("pool_cmp", [cfg.NPHYS * 8, 2048]); I["pool_slc"] = din("pool_slc", [cfg.NPHYS * 2, 8192])
    I["cwin"] = din("cwin", [128, 8192]); I["ptab"] = din("ptab", [16, NPG], I32)
    I["wkv_g"] = din("wkv_g", [D, 256]); I["relb_g"] = din("relb_g", [32, 4])
    I["iota_c"] = din("iota_c", [1, NPB + 1]); I["forced_s"] = din("forced_s", [1, NPB + 1]); I["iota13"] = din("iota13", [1, 13])
    I["nmask"] = din("nmask", [128, 8]); I["wmask0"] = din("wmask0", [128, 64])
    I["idxw"] = din("idxw", [128, 8], I32); I["yidx"] = din("yidx", [128, 1], I32)
    for n, shp in [("ln_g", [4, D]), ("ln_b", [4, D]), ("sg_w_in", [D, 2 * DSG]), ("sg_b_in", [1, 2 * DSG]),
                   ("sg_ln_g", [1, DSG]), ("sg_ln_b", [1, DSG]), ("sg_w_s", [8, 128, 128]), ("sg_b_s", [8, 128]),
                   ("sg_w_out", [DSG, D]), ("sg_b_out", [1, D]), ("ffn_w_up", [2, D, 2 * DFF]),
                   ("ffn_b_up", [2, 2 * DFF]), ("ffn_w_dw", [2, 3, DFF]), ("ffn_b_dw", [2, DFF]),
                   ("ffn_w_down", [2, DFF, D]), ("ffn_b_down", [2, D]), ("kv_w", [D, 1536]),
                   ("nsa_w_qg", [D, 1072]), ("nsa_b_qg", [1, 1072]), ("nsa_w_o", [D, D]), ("nsa_b_o", [1, D]),
                   ("wq_g", [D, 268]), ("bq_g", [1, 268]), ("wo_g", [256, D]),
                   ("ident", [128, 128]), ("tril_p", [128, 128]), ("tril_s", [128, 128]),
                   ("cmp_pe", [2, 64, 64]), ("cmp_w1", [2, 4096, 128]), ("cmp_b1", [2, 128]), ("cmp_w2", [2, 128, 64]),
                   ("cmp_b2", [1, 128]), ("rel_bias", [32, 16]), ("oh1", [32, 1152]), ("oh1r", [32, 1152]),
                   ("causT", [128, 128]), ("winfarT", [128, 128]), ("Emat", [cfg.NBLK, cfg.NKT * 128]),
                   ("cmaskc", [128, 16]), ("fq", [128, 3]), ("kvalid", [128, cfg.NKT]),
                   ("cvalid", [1, cfg.NBLK]), ("cval01", [1, cfg.NBLK]), ("firstblk", [1, cfg.NBLK])]:
        I[n] = din(n, shp)
    O = {}
    O["yp"] = dout("o_yp", [NTO * 128, D]); O["kvp"] = dout("o_kvp", [NTO * 128, 1536])
    O["convp"] = dout("o_convp", [2, 2, DFF])
    O["ys"] = dout("o_ys", [128, D]); O["kvs"] = dout("o_kvs", [128, 1536])
    O["convs"] = dout("o_convs", [2, 32, DFF]); O["sgv"] = dout("o_sgv", [128, DSG])

    with contextlib.ExitStack() as st:
        fw = FW(nc, st)
        OB = {k: fw.buf("out_" + k) for k in O}

        def MM(o, oap, l, lap, r, rap, start, stop):
            fw.op("pe", lambda h: h.matmul(oap, lhsT=lap, rhs=rap, start=start, stop=stop), [l.b, r.b], [o.b])

        def TR(o, oap, i, iap, idn):
            n = iap.shape[0]
            fw.op("pe", lambda h: h.transpose(out=oap, in_=iap, identity=idn.t[0:n, 0:n]), [i.b, idn.b], [o.b])

        def ACT(o, oap, i, iap, func, bias=None, scale=1.0, extra=()):
            rd = [i.b] + [x.b for x in extra]
            if bias is None:
                fw.op("act", lambda h: h.activation(out=oap, in_=iap, func=func, scale=scale), rd, [o.b])
            else:
                fw.op("act", lambda h: h.activation(out=oap, in_=iap, func=func, bias=bias, scale=scale), rd, [o.b])

        def TT(en, o, oap, a, aap, b, bap, op):
            fw.op(en, lambda h: h.tensor_tensor(out=oap, in0=aap, in1=bap, op=op), [a.b, b.b], [o.b])

        def TS(en, o, oap, a, aap, s1, s2, op0, op1=None, extra=()):
            rd = [a.b] + [x.b for x in extra]
            if op1 is None:
                fw.op(en, lambda h: h.tensor_scalar(out=oap, in0=aap, scalar1=s1, scalar2=None, op0=op0), rd, [o.b])
            else:
                fw.op(en, lambda h: h.tensor_scalar(out=oap, in0=aap, scalar1=s1, scalar2=s2, op0=op0, op1=op1), rd, [o.b])

        def STT(o, oap, a, aap, sc, b, bap, op0, op1, extra=()):
            rd = [a.b, b.b] + [x.b for x in extra]
            fw.op("dve", lambda h: h.scalar_tensor_tensor(out=oap, in0=aap, scalar=sc, in1=bap, op0=op0, op1=op1), rd, [o.b])

        def CP(en, o, oap, i, iap):
            if en == "act":
                fw.op("act", lambda h: h.copy(out=oap, in_=iap), [i.b], [o.b])
            else:
                fw.op(en, lambda h: h.tensor_copy(out=oap, in_=iap), [i.b], [o.b])

        def LD(o, oap, src_ap, src_b=None, q="sp", **kw):
            fw.dma(q, lambda h: h.dma_start(out=oap, in_=src_ap, **kw), [src_b] if src_b else [], [o.b])

        def STO(dst_ap, dst_b, i, iap, q="sp", **kw):
            fw.dma(q, lambda h: h.dma_start(out=dst_ap, in_=iap, **kw), [i.b], [dst_b])

        def bcast(ap_row, n, parts=128):
            return bass.AP(tensor=ap_row.tensor, offset=ap_row.offset, ap=[[0, parts], [1, n]])

        PM = [fw.ps(f"pm{i}", [128, 512], F32) for i in range(3)]
        PF = [fw.ps(f"pf{i}", [128, 4, 128], F32) for i in range(3)]
        PT = [fw.ps(f"pt{i}", [128, 8, 128], BF16) for i in range(2)]
        rr = {"pm": 0, "pf": 0, "pt": 0}

        def nxt(kind):
            lst = {"pm": PM, "pf": PF, "pt": PT}[kind]
            rr[kind] = (rr[kind] + 1) % len(lst)
            return lst[rr[kind]]

        idf = fw.sb("idf", [128, 128], F32); idb = fw.sb("idb", [128, 128], BF16)
        LD(idf, idf.t[:], I["ident"][:, :])
        CP("dve", idb, idb.t[:], idf, idf.t[:])
        cv = fw.sb("cv", [128, 8], F32)
        LD(cv, cv.t[:], bcast(I["cvec"][0:1, :], 8))

        WS = {}

        def prep_w(name, src, K, cols):
            nks = (K + 1023) // 1024
            scr = fw.dram("ws_" + name, [len(cols), nks, 128, 8, 512], BF16)
            sbufs = {}
            for cb, pieces in enumerate(cols):
                for ks in range(nks):
                    sbufs[(cb, ks)] = fw.buf(f"ws_{name}_{cb}_{ks}")
                    nkc = min(8, K // 128 - ks * 8)
                    off = 0
                    for (c0, w) in pieces:
                        sap = src[ks * 1024: ks * 1024 + nkc * 128, c0:c0 + w].rearrange("(kc p) n -> p kc n", p=128)
                        dap = scr.t.ap()[cb, ks, :, 0:nkc, off:off + w]
                        fw.dma("pool", lambda h, dap=dap, sap=sap: h.dma_start(out=dap, in_=sap), [], [sbufs[(cb, ks)]])
                        off += w
            WS[name] = (scr, nks, K, sbufs)

        def blocks(n0, n, w=512):
            return [[(c, min(w, n0 + n - c))] for c in range(n0, n0 + n, w)]

        prep_w("w_in_v", I["sg_w_in"], D, blocks(DSG, DSG))
        prep_w("w_in_u", I["sg_w_in"], D, blocks(0, DSG))
        prep_w("w_out", I["sg_w_out"], DSG, blocks(0, D))
        def prep_ffn(l):
            prep_w(f"w_up{l}", I["ffn_w_up"][l], D, [[(256 * j, 256), (DFF + 256 * j, 256)] for j in range(11)])
            prep_w(f"w_dn{l}", I["ffn_w_down"][l], DFF, blocks(0, D))
        prep_ffn(0)
        prep_w("w_kv", I["kv_w"], D, blocks(0, 1536))
        prep_w("w_q", I["nsa_w_qg"], D, blocks(0, 1024))
        prep_w("w_gt", I["nsa_w_qg"], D, [[(1024, 48)]])
        prep_w("w_o", I["nsa_w_o"], D, blocks(0, D))
        prep_w("w_qs", I["wq_g"], D, [[(0, 268)]])
        prep_w("w_kvs", I["wkv_g"], D, [[(0, 256)]])
        prep_w("w_og", I["wo_g"], 256, blocks(0, D))
        prep_ffn(1)

        NSLOT = 4
        slots = [fw.sb(f"slab{i}", [128, 8, 512], BF16) for i in range(NSLOT)]

        class SlabStream:
            def __init__(self, specs, look=3):
                self.specs = specs; self.issued = 0; self.used = 0; self.look = look

            def _issue(self):
                name, cb, ks = self.specs[self.issued]
                scr, nks, K, sbufs = WS[name]
                sl = slots[self.issued % NSLOT]
                LD(sl, sl.t[:], scr.t.ap()[cb, ks], sbufs[(cb, ks)])
                self.issued += 1

            def get(self, name, cb, ks=0):
                assert self.specs[self.used] == (name, cb, ks), (self.specs[self.used], name, cb, ks)
                while self.issued < min(len(self.specs), self.used + self.look):
                    self._issue()
                sl = slots[self.used % NSLOT]
                self.used += 1
                return sl

        bias_list = [("b_in_v", I["sg_b_in"][0:1, DSG:2 * DSG], DSG), ("b_out", I["sg_b_out"][0:1, :], D),
                     ("b_dn0", I["ffn_b_down"][0:1, :], D), ("b_dn1", I["ffn_b_down"][1:2, :], D),
                     ("b_o", I["nsa_b_o"][0:1, :], D), ("b_gt", I["nsa_b_qg"][0:1, 1024:1072], 48),
                     ("b_qs", I["bq_g"][0:1, :], 268)]
        NBP = 128 * 60
        bcat = fw.dram("bcat", [NBP], F32); bhs = fw.dram("bhs", [NBP], BF16); bls = fw.dram("bls", [NBP], BF16)
        bias_hl = fw.sb("bias_hl", [128, 6 * 512], BF16)
        ones_t = fw.sb("ones_t", [128, 128], BF16)
        fw.op("dve", lambda h: h.memset(ones_t.t[:], 1.0), [], [ones_t.b])
        BIASPOS = {}
        with contextlib.ExitStack() as sb_:
            bf_ = fw.sb("bf_", [128, 60], F32, sb_); bh_ = fw.sb("bh_", [128, 60], BF16, sb_)
            bt_ = fw.sb("bt_", [128, 60], F32, sb_); bl_ = fw.sb("bl_", [128, 60], BF16, sb_)
            fw.op("dve", lambda h: h.memset(bf_.t[:], 0.0), [], [bf_.b])
            STO(bcat.t.ap().rearrange("(p k) -> p k", k=60), bcat.b, bf_, bf_.t[:])
            off = 0; blk = 0
            for (nm, row, n) in bias_list:
                fw.dma("sp", lambda h, row=row, off=off, n=n: h.dma_start(out=bcat.t.ap()[off:off + n].rearrange("(o n) -> o n", o=1), in_=row),
                       [], [bcat.b])
                for c0 in range(0, n, 512):
                    BIASPOS[(nm, c0 // 512)] = (32 * (blk % 3), 512 * (blk // 3), off + c0, min(512, n - c0)); blk += 1
                off += n
            assert off <= NBP and blk <= 18
            LD(bf_, bf_.t[:], bcat.t.ap().rearrange("(p k) -> p k", k=60), bcat.b)
            CP("dve", bh_, bh_.t[:], bf_, bf_.t[:])
            CP("dve", bt_, bt_.t[:], bh_, bh_.t[:])
            TT("dve", bt_, bt_.t[:], bf_, bf_.t[:], bt_, bt_.t[:], ALU.subtract)
            CP("dve", bl_, bl_.t[:], bt_, bt_.t[:])
            STO(bhs.t.ap().rearrange("(p k) -> p k", k=60), bhs.b, bh_, bh_.t[:])
            STO(bls.t.ap().rearrange("(p k) -> p k", k=60), bls.b, bl_, bl_.t[:])
            for key, (pb, col, o, w) in BIASPOS.items():
                LD(bias_hl, bias_hl.t[pb:pb + 1, col:col + w], bhs.t.ap()[o:o + w].rearrange("(o n) -> o n", o=1), bhs.b)
                LD(bias_hl, bias_hl.t[pb + 1:pb + 2, col:col + w], bls.t.ap()[o:o + w].rearrange("(o n) -> o n", o=1), bls.b)
            fw.barrier()

        def add_bias(p, pap, nm, cb, w):
            pb, col, o, wb = BIASPOS[(nm, cb)]
            MM(p, pap, ones_t, ones_t.t[pb:pb + 2, :], bias_hl, bias_hl.t[pb:pb + 2, col:col + w], False, True)

        def bc_tile(name, src_row, n, stack=None):
            t = fw.sb(name, [128, n], F32, stack)
            LD(t, t.t[:], bcast(src_row, n))
            return t

        lng = fw.sb("lng", [128, 2, D], F32); lnb = fw.sb("lnb", [128, 2, D], F32)

        def load_ln(l):
            for j in range(2):
                LD(lng, lng.t[:, j, :], bcast(I["ln_g"][2 * l + j:2 * l + j + 1, :], D))
                LD(lnb, lnb.t[:, j, :], bcast(I["ln_b"][2 * l + j:2 * l + j + 1, :], D))

        def pp_tile(name, src_row, nch):
            t = fw.sb(name, [128, nch], F32)
            fw.dma("sp", lambda h: h.dma_start(out=t.t[:], in_=src_row.rearrange("o (c p) -> p (o c)", p=128),
                                               allow_slow_non_contiguous=True), [], [t.b])
            return t

        b_in_u = pp_tile("b_in_u", I["sg_b_in"][0:1, 0:DSG], 24)
        b_up = [pp_tile(f"b_up{l}", I["ffn_b_up"][l:l + 1, :], 44) for l in range(2)]
        b_dw = [pp_tile(f"b_dw{l}", I["ffn_b_dw"][l:l + 1, :], 22) for l in range(2)]
        w_dw = [[pp_tile(f"w_dw{l}_{k}", I["ffn_w_dw"][l, k:k + 1, :], 22) for k in range(3)] for l in range(2)]
        bq = fw.sb("bq", [64, 16], F32)
        fw.dma("sp", lambda h: h.dma_start(out=bq.t[:], in_=I["nsa_b_qg"][0:1, 0:1024].rearrange("o (c p) -> p (o c)", p=64),
                                           allow_slow_non_contiguous=True), [], [bq.b])

        wts_shared = []

        def make_wsT(name, sample, stack):
            if not wts_shared:
                wts_shared.append(fw.sb("wts_f", [128, 8, 128], F32, stack)); wts_shared.append(fw.sb("wts_m", [128, 128], F32, stack))
            wts, msk = wts_shared
            if not sample:
                LD(wts, wts.t[:], I["sg_w_s"].rearrange("g t s -> t g s"))
            else:
                fw.op("pool", lambda h: h.memset(wts.t[:], 0.0), [], [wts.b])
                for b in range(16):
                    LD(wts, wts.t[8 * b:8 * b + 8, :, 8 * b:8 * b + 8],
                       I["sg_w_s"][:, 0:8, 0:8].rearrange("g t s -> t g s"), allow_slow_non_contiguous=True)
            LD(msk, msk.t[:], I["tril_s" if sample else "tril_p"][:, :])
            wT = fw.sb(name, [128, 8, 128], BF16, stack)
            for g in range(8):
                p = nxt("pf")
                fw.op("pe", lambda h, p=p, g=g: h.transpose(out=p.t[:, 0, :], in_=wts.t[:, g, :], identity=idf.t[:]),
                      [wts.b, idf.b], [p.b])
                TT("dve", wT, wT.t[:, g, :], p, p.t[:, 0, :], msk, msk.t[:], ALU.mult)
            bs = fw.sb(name + "_b", [128, 8, 128], F32, stack)
            if not sample:
                LD(bs, bs.t[:], bass.AP(tensor=I["sg_b_s"].tensor, offset=I["sg_b_s"].offset, ap=[[0, 128], [128, 8], [1, 128]]))
            else:
                for g in range(8):
                    LD(bs, bs.t[:, g, :].rearrange("p (b t) -> p b t", t=8),
                       bass.AP(tensor=I["sg_b_s"].tensor, offset=I["sg_b_s"].offset + 128 * g, ap=[[0, 128], [0, 16], [1, 8]]),
                       allow_slow_non_contiguous=True)
            return wT, bs

        gates = fw.sb("gates", [128, NTP, 48], F32)
        qs = fw.sb("qs", [128, 256], F32); gates_s = fw.sb("gates_s", [128, 12], F32)
        kvn = fw.sb("kvn", [128, 256], F32); kvnscr = fw.dram("kvnscr", [128, 256], F32)
        stats = fw.sb("stats", [128, 6, 6], F32); mv = fw.sb("mv", [128, 2], F32); rstd = fw.sb("rstd", [128, 1], F32)
        xcur = fw.sb("xcur", [128, D], F32); xb = fw.sb("xb", [128, D], BF16); xT = fw.sb("xT", [128, 8, 128], BF16)
        rtmp = fw.sb("rtmp", [128, D], F32)
        hT = fw.sb("hT", [128, 22, 128], BF16)
        a2 = fw.sb("a2", [128, 2, 160], F32)
        cm = fw.sb("cm", [128, 2, 128], F32); hc = fw.sb("hc", [128, 2, 128], F32)
        cstate_p = fw.sb("cstate_p", [128, 22, 2], F32)
        cstate_s = fw.sb("cstate_s", [128, 22, 32], F32)
        sA = contextlib.ExitStack(); sA0 = contextlib.ExitStack()
        cmpT = fw.sb("cmpT", [64, 2, 128, 4 * NTP], BF16, sA0)
        zt = fw.sb("zt", [128, 1024], BF16, sA0)
        sidx = fw.sb("sidx_sb", [128, NTO], I32, sA0); cidx = fw.sb("cidx_sb", [128, 1], I32, sA0)
        vg = fw.sb("vg", [128, DSG], F32, sA); v_bf = fw.sb("v_bf", [128, DSG], BF16, sA)
        u4 = fw.sb("u4", [128, 4, 128], F32, sA); g4 = fw.sb("g4", [128, 4, 128], F32, sA)
        zT = fw.sb("zT", [128, 24, 128], BF16, sA)
        kv = fw.sb("kv", [128, 1536], F32, sA); kvb = fw.sb("kvb", [128, 1024], BF16, sA)
        qTt = fw.sb("qTt", [64, 16, 128], BF16, sA)
        cst = [(vg, vg.t[:, :])]
        gv_bc = bc_tile("gv_bc", I["sg_ln_g"][0:1, :], DSG, sA)
        bv_bc = bc_tile("bv_bc", I["sg_ln_b"][0:1, :], DSG, sA)
        wsT_p, bs_p = make_wsT("wsT_p", False, sA)
        wsT_s, bs_s = make_wsT("wsT_s", True, sA)

        x1scr = fw.dram("x1scr", [NTOK + 128, D], F32)
        qscr = fw.dram("qscr", [NTP, 16, 64, 128], BF16)
        NROWS = 2 * cfg.S + 2 * cfg.NKT
        xsrc = fw.dram("xsrc", [NROWS, 1024], BF16); xdst = fw.dram("xdst", [NROWS, 1024], BF16)
        fw.op("pool", lambda h: h.memset(zt.t[:], 0.0), [], [zt.b])
        for r0 in range(0, NROWS, 128):
            nr = min(128, NROWS - r0)
            fw.dma("pool", lambda h, r0=r0, nr=nr: h.dma_start(out=xsrc.t.ap()[r0:r0 + nr, :], in_=zt.t[0:nr, :]), [zt.b], [xsrc.b])
        LD(sidx, sidx.t[:], I["sidx"][:, :]); LD(cidx, cidx.t[:], I["cidx"][:, :])

        def scatter_rows(src, src_ap, idx, idx_ap):
            fw.dma("pool", lambda h: h.indirect_dma_start(out=xsrc.t.ap()[:, :], out_offset=bass.IndirectOffsetOnAxis(ap=idx_ap, axis=0),
                                                          in_=src_ap, in_offset=None, bounds_check=NROWS - 1, oob_is_err=False),
                   [src.b, idx.b], [xsrc.b])

        def to_T(src, dstT):
            CP("act", xb, xb.t[:], src, src.t[:])
            p = nxt("pt")
            for c in range(8):
                TR(p, p.t[:, c, :], xb, xb.t[:, c * 128:(c + 1) * 128], idb)
            CP("dve", dstT, dstT.t[:], p, p.t[:])

        def layer_norm(x, n, g=None, g_ap=None, b=None, b_ap=None):
            nch = n // 512
            for c in range(nch):
                fw.op("dve", lambda h, c=c: h.bn_stats(out=stats.t[:, c, :], in_=x.t[:, c * 512:(c + 1) * 512]), [x.b], [stats.b])
            fw.op("dve", lambda h: h.bn_aggr(out=mv.t[:], in_=stats.t[:, 0:nch, :]), [stats.b], [mv.b])
            TS("dve", rstd, rstd.t[:], mv, mv.t[:, 1:2], LN_EPS, None, ALU.add)
            fw.op("act", lambda h: h.activation(out=rstd.t[:], in_=rstd.t[:], func=AF.Sqrt), [rstd.b], [rstd.b])
            fw.op("dve", lambda h: h.reciprocal(out=rstd.t[:], in_=rstd.t[:]), [rstd.b], [rstd.b])
            TS("dve", x, x.t[:, 0:n], x, x.t[:, 0:n], mv.t[:, 0:1], rstd.t[:, 0:1], ALU.subtract, ALU.mult, extra=(mv, rstd))
            if g is not None:
                TT("pool", x, x.t[:, 0:n], x, x.t[:, 0:n], g, g_ap, ALU.mult)
                TT("dve", x, x.t[:, 0:n], x, x.t[:, 0:n], b, b_ap, ALU.add)

        def proj_tm(stream, wname, ncb, lhs, nkc_total, bias_hl, consume):
            nks = (nkc_total + 7) // 8
            for cb in range(ncb):
                p = nxt("pm")
                for ks in range(nks):
                    sl = stream.get(wname, cb, ks)
                    for kc in range(min(8, nkc_total - ks * 8)):
                        cc = ks * 8 + kc
                        MM(p, p.t[:], lhs, lhs.t[:, cc, :], sl, sl.t[:, kc, :], cc == 0, (bias_hl is None and cc == nkc_total - 1))
                if bias_hl is not None:
                    add_bias(p, p.t[:], bias_hl, cb, 512)
                consume(cb, p)

        def ffn(stream, l, sample, ti):
            B, T = (16, 8) if sample else (1, 128)
            cstate = cstate_s if sample else cstate_p
            a2v = a2.t[:, :, 0:B * (T + 2)].rearrange("p c (b t) -> p c b t", t=T + 2)
            to_T(xcur, xT)
            for j in range(11):
                sl = stream.get(f"w_up{l}", j, 0)
                p = nxt("pf")
                for q in range(4):
                    for kc in range(8):
                        MM(p, p.t[:, q, :], sl, sl.t[:, kc, q * 128:(q + 1) * 128], xT, xT.t[:, kc, :], kc == 0, kc == 7)
                CP("pool", a2, a2v[:, :, :, 0:2], cstate, cstate.t[:, 2 * j:2 * j + 2, :].rearrange("p c (b k) -> p c b k", k=2))
                for q in range(2):
                    c = 2 * j + q
                    ACT(a2, a2v[:, q, :, 2:T + 2], p, p.t[:, q, :].rearrange("p (b t) -> p b t", t=T), AF.Identity,
                        bias=b_up[l].t[:, c:c + 1], extra=(b_up[l],))
                CP("pool", cstate, cstate.t[:, 2 * j:2 * j + 2, :].rearrange("p c (b k) -> p c b k", k=2), a2, a2v[:, :, :, T:T + 2])
                for q in range(2):
                    c = 2 * j + q
                    cmv = cm.t[:, q, :].rearrange("p (b t) -> p b t", t=T)
                    TS("dve", cm, cmv, a2, a2v[:, q, :, 0:T], w_dw[l][0].t[:, c:c + 1], None, ALU.mult, extra=(w_dw[l][0],))
                    STT(cm, cmv, a2, a2v[:, q, :, 1:T + 1], w_dw[l][1].t[:, c:c + 1], cm, cmv, ALU.mult, ALU.add, extra=(w_dw[l][1],))
                    STT(cm, cmv, a2, a2v[:, q, :, 2:T + 2], w_dw[l][2].t[:, c:c + 1], cm, cmv, ALU.mult, ALU.add, extra=(w_dw[l][2],))
                    ACT(hc, hc.t[:, q, :], cm, cm.t[:, q, :], AF.Gelu, bias=b_dw[l].t[:, c:c + 1], extra=(b_dw[l],))
                    STT(hT, hT.t[:, c, :], p, p.t[:, 2 + q, :], b_up[l].t[:, 22 + c:23 + c], hc, hc.t[:, q, :], ALU.add, ALU.mult,
                        extra=(b_up[l],))

            def cons(cb, p):
                STT(rtmp, rtmp.t[:, cb * 512:(cb + 1) * 512], xcur, xcur.t[:, cb * 512:(cb + 1) * 512], ALPHA, p, p.t[:], ALU.mult, ALU.add)
            proj_tm(stream, f"w_dn{l}", 2, hT, 22, f"b_dn{l}", cons)
            CP("act", xcur, xcur.t[:], rtmp, rtmp.t[:])
            layer_norm(xcur, D, lng, lng.t[:, 1, :], lnb, lnb.t[:, 1, :])

        def conv_state_out(l, sample):
            cstate = cstate_s if sample else cstate_p
            ncol = 32 if sample else 2
            for c in range(22):
                p = nxt("pf")
                fw.op("pe", lambda h, p=p, c=c: h.transpose(out=p.t[0:ncol, 0, :], in_=cstate.t[:, c, 0:ncol], identity=idf.t[:]),
                      [cstate.b, idf.b], [p.b])
                CP("dve", cst[0][0], cst[0][1][0:ncol, c * 128:(c + 1) * 128], p, p.t[0:ncol, 0, :])
            if sample:
                STO(O["convs"][l], OB["convs"], cst[0][0], cst[0][1][0:32, 0:DFF])
            else:
                STO(O["convp"][l], OB["convp"], cst[0][0], cst[0][1][0:2, 0:DFF])

        def tile_specs_A(sample):
            sp = [("w_in_v", cb, 0) for cb in range(6)] + [("w_in_u", cb, 0) for cb in range(6)]
            sp += [("w_out", cb, ks) for cb in range(2) for ks in range(3)]
            sp += [("w_up0", j, 0) for j in range(11)] + [("w_dn0", cb, ks) for cb in range(2) for ks in range(3)]
            sp += [("w_kv", cb, 0) for cb in range(3)]
            sp += [("w_qs", 0, 0), ("w_kvs", 0, 0)] if sample else [("w_q", 0, 0), ("w_q", 1, 0), ("w_gt", 0, 0)]
            return sp

        specsA = []
        for ti in range(NTP):
            specsA += tile_specs_A(False)
        specsA += tile_specs_A(True)
        SA = SlabStream(specsA)
        load_ln(0)

        def phaseA_tile(ti, sample):
            wsT, bs = (wsT_s, bs_s) if sample else (wsT_p, bs_p)
            LD(xcur, xcur.t[:], I["xs"][:, :] if sample else I["xp"][ti * 128:(ti + 1) * 128, :])
            to_T(xcur, xT)

            def cons_v(cb, p):
                ACT(vg, vg.t[:, cb * 512:(cb + 1) * 512], p, p.t[:], AF.Gelu)
            proj_tm(SA, "w_in_v", 6, xT, 8, "b_in_v", cons_v)
            layer_norm(vg, DSG, gv_bc, gv_bc.t[:], bv_bc, bv_bc.t[:])
            if sample:
                STO(O["sgv"][:, :], OB["sgv"], vg, vg.t[:])
            CP("act", v_bf, v_bf.t[:], vg, vg.t[:])
            for cb in range(6):
                sl = SA.get("w_in_u", cb, 0)
                p = nxt("pf")
                for j in range(4):
                    for kc in range(8):
                        MM(p, p.t[:, j, :], sl, sl.t[:, kc, j * 128:(j + 1) * 128], xT, xT.t[:, kc, :], kc == 0, kc == 7)
                for j in range(4):
                    ACT(u4, u4.t[:, j, :], p, p.t[:, j, :], AF.Gelu, bias=b_in_u.t[:, cb * 4 + j:cb * 4 + j + 1], extra=(b_in_u,))
                pg = nxt("pf")
                for j in range(4):
                    cc = cb * 4 + j
                    MM(pg, pg.t[:, j, :], v_bf, v_bf.t[:, cc * 128:(cc + 1) * 128], wsT, wsT.t[:, cc // 3, :], True, True)
                for j in range(4):
                    cc = cb * 4 + j
                    TT("dve", g4, g4.t[:, j, :], pg, pg.t[:, j, :], bs, bs.t[:, cc // 3, :], ALU.add)
                TT("dve", zT, zT.t[:, cb * 4:cb * 4 + 4, :], u4, u4.t[:], g4, g4.t[:], ALU.mult)

            def cons_o(cb, p):
                STT(rtmp, rtmp.t[:, cb * 512:(cb + 1) * 512], xcur, xcur.t[:, cb * 512:(cb + 1) * 512], ALPHA, p, p.t[:], ALU.mult, ALU.add)
            proj_tm(SA, "w_out", 2, zT, 24, "b_out", cons_o)
            CP("act", xcur, xcur.t[:], rtmp, rtmp.t[:])
            layer_norm(xcur, D, lng, lng.t[:, 0, :], lnb, lnb.t[:, 0, :])
            if sample:
                LD(cst[0][0], cst[0][1][0:32, 0:DFF], I["sconv"][0])
                for c in range(22):
                    p = nxt("pf")
                    fw.op("pe", lambda h, p=p, c=c: h.transpose(out=p.t[:, 0, 0:32], in_=cst[0][1][0:32, c * 128:(c + 1) * 128],
                                                                identity=idf.t[0:32, 0:32]), [cst[0][0].b, idf.b], [p.b])
                    CP("dve", cstate_s, cstate_s.t[:, c, :], p, p.t[:, 0, 0:32])
            elif ti == 0:
                fw.op("pool", lambda h: h.memset(cstate_p.t[:], 0.0), [], [cstate_p.b])
            ffn(SA, 0, sample, ti)
            if not sample and ti == 0:
                TS("dve", cstate_p, cstate_p.t[:], cstate_p, cstate_p.t[:], cv.t[:, 0:1], None, ALU.mult, extra=(cv,))
            if sample or ti == NTP - 1:
                conv_state_out(0, sample)
            row0 = NTOK if sample else ti * 128
            STO(x1scr.t.ap()[row0:row0 + 128, :], x1scr.b, xcur, xcur.t[:])
            to_T(xcur, xT)

            def cons_kv(cb, p):
                CP("act", kv, kv.t[:, cb * 512:(cb + 1) * 512], p, p.t[:])
            proj_tm(SA, "w_kv", 3, xT, 8, None, cons_kv)
            if sample:
                STO(O["kvs"][:, :], OB["kvs"], kv, kv.t[:])
            elif ti >= 1:
                STO(O["kvp"][(ti - 1) * 128:ti * 128, :], OB["kvp"], kv, kv.t[:])
            if not sample:
                for k in range(2):
                    p = nxt("pf")
                    for g in range(4):
                        fw.op("pe", lambda h, p=p, g=g, k=k: h.transpose(out=p.t[0:64, g, :], in_=kv.t[:, k * 256 + g * 64:k * 256 + g * 64 + 64],
                                                                         identity=idf.t[:]), [kv.b, idf.b], [p.b])
                    CP("dve", cmpT, cmpT.t[:, k, :, :].rearrange("p t (g i) -> p g i t", g=4)[:, :, ti, :], p, p.t[0:64, :, :])
                CP("pool", kvb, kvb.t[:], kv, kv.t[:, 512:1536])
                if ti >= 1:
                    scatter_rows(kvb, kvb.t[:], sidx, sidx.t[:, ti - 1:ti])
                for cb in range(2):
                    sl = SA.get("w_q", cb, 0)
                    for hh in range(8):
                        p = nxt("pf")
                        for kc in range(8):
                            MM(p, p.t[0:64, 0, :], sl, sl.t[:, kc, hh * 64:(hh + 1) * 64], xT, xT.t[:, kc, :], kc == 0, kc == 7)
                        hd = cb * 8 + hh
                        ACT(qTt, qTt.t[:, hd, :], p, p.t[0:64, 0, :], AF.Identity, bias=bq.t[:, hd:hd + 1], extra=(bq,))
                STO(qscr.t.ap()[ti].rearrange("h d t -> d h t"), qscr.b, qTt, qTt.t[:])
                sl = SA.get("w_gt", 0, 0)
                p = nxt("pm")
                for kc in range(8):
                    MM(p, p.t[:, 0:48], xT, xT.t[:, kc, :], sl, sl.t[:, kc, 0:48], kc == 0, False)
                add_bias(p, p.t[:, 0:48], "b_gt", 0, 48)
                ACT(gates, gates.t[:, ti, :], p, p.t[:, 0:48], AF.Sigmoid)
            else:
                sl = SA.get("w_qs", 0, 0)
                p = nxt("pm")
                for kc in range(8):
                    MM(p, p.t[:, 0:268], xT, xT.t[:, kc, :], sl, sl.t[:, kc, 0:268], kc == 0, False)
                add_bias(p, p.t[:, 0:268], "b_qs", 0, 268)
                CP("dve", qs, qs.t[:], p, p.t[:, 0:256])
                ACT(gates_s, gates_s.t[:], p, p.t[:, 256:268], AF.Sigmoid)
                sl = SA.get("w_kvs", 0, 0)
                p = nxt("pm")
                for kc in range(8):
                    MM(p, p.t[:, 0:256], xT, xT.t[:, kc, :], sl, sl.t[:, kc, 0:256], kc == 0, kc == 7)
                CP("dve", kvn, kvn.t[:], p, p.t[:, 0:256])
                STO(kvnscr.t.ap()[:, :], kvnscr.b, kvn, kvn.t[:])

        def phaseS():
            NPG, NPB = cfg.NPG, cfg.NPB
            SCALE = 0.125
            IOA = bass.IndirectOffsetOnAxis
            sS = contextlib.ExitStack()

            def T(name, shape, dt=F32, stack=None):
                return fw.sb("s_" + name, shape, dt, stack or sS)
            sT = contextlib.ExitStack()
            relg = T("relg", [32, 4], F32, sT); r31g = T("r31g", [32, 4], F32, sT)
            LD(relg, relg.t[:], I["relb_g"][:, :]); LD(r31g, r31g.t[:], bcast(I["relb_g"][31:32, :], 4, 32))
            TT("dve", relg, relg.t[:], relg, relg.t[:], r31g, r31g.t[:], ALU.subtract)
            oh1r = T("oh1r", [32, 1152], F32, sT); LD(oh1r, oh1r.t[:], I["oh1r"][:, :])
            Frs = T("Frs", [4, 1152], F32, sT); frscr = fw.dram("frscr", [4, 1152], F32)
            for c3 in range(3):
                p = nxt("pm")
                MM(p, p.t[0:4, 0:384], relg, relg.t[:], oh1r, oh1r.t[:, c3 * 384:(c3 + 1) * 384], True, True)
                CP("dve", Frs, Frs.t[:, c3 * 384:(c3 + 1) * 384], p, p.t[0:4, 0:384])
            STO(frscr.t.ap()[:, :], frscr.b, Frs, Frs.t[:])
            fw.barrier()
            sT.close()

            def tbl(dst, dst_ap_fn, base, inner):
                for t in range(8):
                    LD(dst, dst_ap_fn(t), bass.AP(tensor=frscr.t, offset=base - t, ap=[[0, 16], [1152, 4]] + inner), frscr.b,
                       allow_slow_non_contiguous=True)
            bias_cs = T("bias_cs", [128, 4, 16])
            for t in range(8):
                for r in range(4):
                    LD(bias_cs, bias_cs.t[t:128:8, r, :], bass.AP(tensor=frscr.t, offset=63 - t + 1152 * r, ap=[[0, 16], [64, 16]]), frscr.b,
                       allow_slow_non_contiguous=True)
            wbias = T("wbias", [128, 4, 512]); tbl(wbias, lambda t: wbias.t[t:128:8, :, :], 512, [[1, 512]])
            nbias = T("nbias", [128, 4, 8]); tbl(nbias, lambda t: nbias.t[t:128:8, :, :], 1024, [[1, 8]])
            cand = T("cand", [128, 13, 4, 64])
            for jr in range(13):
                tbl(cand, lambda t, jr=jr: cand.t[t:128:8, jr, :, :], 960 - 64 * jr, [[1, 64]])
            nmask = T("nmask", [128, 8]); LD(nmask, nmask.t[:], I["nmask"][:, :])
            wmask0 = T("wmask0", [128, 64]); LD(wmask0, wmask0.t[:], I["wmask0"][:, :])
            idxw = T("idxw", [128, 8], I32); LD(idxw, idxw.t[:], I["idxw"][:, :])
            yidx = T("yidx", [128, 1], I32); LD(yidx, yidx.t[:], I["yidx"][:, :])
            iotc = T("iotc", [128, NPB + 1]); LD(iotc, iotc.t[:], bcast(I["iota_c"][0:1, :], NPB + 1))
            forced = T("forced", [128, NPB + 1]); LD(forced, forced.t[:], bcast(I["forced_s"][0:1, :], NPB + 1))
            iot13 = T("iot13", [128, 13]); LD(iot13, iot13.t[:], bcast(I["iota13"][0:1, :], 13))
            Vc_all = T("Vc_all", [128, 16, 2, 65], BF16)
            fw.op("pool", lambda h: h.memset(Vc_all.t[:, :, :, 64:65], 1.0), [], [Vc_all.b])
            lcs = T("lcs", [128, 4, NPB]); ees = T("ees", [128, 4, NPB])
            imps = T("imps", [128, NPB + 1]); sc2s = T("sc2s", [128, NPB + 1])
            qTs = T("qTs", [64, 4, 128], BF16)
            p = nxt("pf")
            for r in range(4):
                fw.op("pe", lambda h, p=p, r=r: h.transpose(out=p.t[0:64, r, :], in_=qs.t[:, r * 64:(r + 1) * 64], identity=idf.t[:]),
                      [qs.b, idf.b], [p.b])
            CP("dve", qTs, qTs.t[:], p, p.t[0:64, :, :])

            sS1 = contextlib.ExitStack()
            Wz, pebias, w2sb, b2bc = load_compress_weights(sS1)
            b2pp = T("b2pp", [64, 2], F32, sS1)
            fw.dma("sp", lambda h: h.dma_start(out=b2pp.t[:], in_=I["cmp_b2"].rearrange("o (k d) -> d (o k)", k=2), allow_slow_non_contiguous=True),
                   [], [b2pp.b])
            ptT = T("ptT", [128, 16], I32, sS1)
            fw.dma("sp", lambda h: h.dma_start(out=ptT.t[0:NPG, :], in_=I["ptab"].rearrange("b p -> p b"), allow_slow_non_contiguous=True), [], [ptT.b])
            ptf8 = T("ptf8", [128, 16], F32, sS1); pidxf = T("pidxf", [128, 16, 8], F32, sS1); pidx = T("pidx", [128, 16, 8], I32, sS1)
            CP("dve", ptf8, ptf8.t[0:NPG, :], ptT, ptT.t[0:NPG, :])
            TS("dve", ptf8, ptf8.t[0:NPG, :], ptf8, ptf8.t[0:NPG, :], 8.0, None, ALU.mult)
            TT("dve", pidxf, pidxf.t[0:NPG], ptf8, ptf8.t[0:NPG, :].unsqueeze(2).to_broadcast([NPG, 16, 8]),
               iotc, iotc.t[0:NPG, 0:8].unsqueeze(1).to_broadcast([NPG, 16, 8]), ALU.add)
            CP("dve", pidx, pidx.t[0:NPG], pidxf, pidxf.t[0:NPG])
            pgs = [T(f"pg{i}", [128, 2048], F32, sS1) for i in range(2)]
            rTs = [T(f"rT{i}", [64, 8, 128], BF16, sS1) for i in range(2)]
            pgbs = [T(f"pgb{i}", [128, 2048], BF16, sS1) for i in range(2)]
            hidTs = T("hidTs", [128, 4, 128], BF16, sS1)
            KcTb = [T(f"KcTb{i}", [64, 256], BF16, sS1) for i in range(2)]
            qbs = [T(f"qb{i}", [64, 4, 128], BF16, sS1) for i in range(2)]
            ACCc = PM[0]; LC = [PM[1], PM[2]]
            accv = ACCc.t[:].rearrange("p (a c) -> p a c", a=4)
            cnt = 0
            for b in range(16):
                for r8 in range(8):
                    pg = pgs[cnt % 2]; cnt += 1
                    fw.dma("pool", lambda h, pg=pg, b=b, r8=r8: h.indirect_dma_start(
                        out=pg.t[0:NPG, :], out_offset=None, in_=I["pool_cmp"][:, :],
                        in_offset=IOA(ap=pidx.t[0:NPG, b, r8:r8 + 1], axis=0)), [pidx.b], [pg.b])
                    pgb = pgbs[cnt % 2]
                    CP("act" if cnt % 2 == 0 else "pool", pgb, pgb.t[0:NPG, :], pg, pg.t[0:NPG, :])
                    for r4 in range(4):
                        p = nxt("pt"); rt = rTs[r4 % 2]
                        for q4 in range(4):
                            rr = r4 * 4 + q4
                            for k in range(2):
                                TR(p, p.t[0:64, q4 * 2 + k, 0:NPG], pgb, pgb.t[0:NPG, rr * 128 + k * 64:rr * 128 + k * 64 + 64], idb)
                        CP("dve", rt, rt.t[:, :, 0:NPG], p, p.t[0:64, :, 0:NPG])
                        for q4 in range(4):
                            lp = r8 * 16 + r4 * 4 + q4; h2 = lp // 64; l = lp % 64
                            for k in range(2):
                                MM(ACCc, accv[:, h2 * 2 + k, 0:NPG], Wz, Wz.t[:, l, k, :], rt, rt.t[:, q4 * 2 + k, 0:NPG], l == 0, l == 63)
                for h2 in range(2):
                    for k in range(2):
                        ACT(hidTs, hidTs.t[:, h2 * 2 + k, 0:NPG], ACCc, accv[:, h2 * 2 + k, 0:NPG], AF.Gelu, bias=pebias.t[:, k:k + 1], extra=(pebias,))
                KcT = KcTb[b % 2]
                pk = nxt("pf")
                for h2 in range(2):
                    MM(pk, pk.t[0:64, h2, 0:NPG], w2sb, w2sb.t[:, 0, :], hidTs, hidTs.t[:, h2 * 2, 0:NPG], True, True)
                for h2 in range(2):
                    ACT(KcT, KcT.t[:, 0:NPB].rearrange("d (p h) -> d h p", h=2)[:, h2, :], pk, pk.t[0:64, h2, 0:NPG], AF.Identity,
                        bias=b2pp.t[:, 0:1], extra=(b2pp,))
                pv = nxt("pf")
                for h2 in range(2):
                    MM(pv, pv.t[0:NPG, h2, 0:64], hidTs, hidTs.t[:, h2 * 2 + 1, 0:NPG], w2sb, w2sb.t[:, 1, :], True, True)
                TT("dve", Vc_all, Vc_all.t[0:NPG, b, :, 0:64], pv, pv.t[0:NPG, 0:2, 0:64], b2bc,
                   b2bc.t[0:NPG, 64:128].unsqueeze(1).to_broadcast([NPG, 2, 64]), ALU.add)
                qbt = qbs[b % 2]
                fw.op("pool", lambda h, qbt=qbt: h.memset(qbt.t[:], 0.0), [], [qbt.b])
                CP("dve", qbt, qbt.t[:, :, 8 * b:8 * b + 8], qTs, qTs.t[:, :, 8 * b:8 * b + 8])
                for r in range(4):
                    MM(LC[r // 2], LC[r // 2].t[:].rearrange("p (a c) -> p a c", a=2)[:, r % 2, 0:NPB], qbt, qbt.t[:, r, :], KcT, KcT.t[:, 0:NPB],
                       b == 0, b == 15)
            for hf in range(2):
                TS("dve", lcs, lcs.t[:, 2 * hf:2 * hf + 2, :], LC[hf], LC[hf].t[:].rearrange("p (a c) -> p a c", a=2)[:, :, 0:NPB], SCALE, None, ALU.mult)
            fw.barrier()
            sS1.close()

            sS2 = contextlib.ExitStack()
            rmx = T("rmx", [128, 4], F32, sS2); sms = T("sms", [128, 4], F32, sS2)
            TT("dve", lcs, lcs.t[:, :, NPB - 16:NPB], lcs, lcs.t[:, :, NPB - 16:NPB], bias_cs, bias_cs.t[:], ALU.add)
            fw.op("dve", lambda h: h.tensor_reduce(out=rmx.t[:], in_=lcs.t[:], axis=AX.X, op=ALU.max), [lcs.b], [rmx.b])
            TS("dve", rmx, rmx.t[:], rmx, rmx.t[:], -100.0, -1.0, ALU.max, ALU.mult)
            for r in range(4):
                ACT(ees, ees.t[:, r, :], lcs, lcs.t[:, r, :], AF.Exp, bias=rmx.t[:, r:r + 1], extra=(rmx,))
            fw.op("dve", lambda h: h.tensor_reduce(out=sms.t[:], in_=ees.t[:], axis=AX.X, op=ALU.add), [ees.b], [sms.b])
            TS("dve", sms, sms.t[:], sms, sms.t[:], 1e-30, None, ALU.max)
            fw.op("dve", lambda h: h.reciprocal(out=sms.t[:], in_=sms.t[:]), [sms.b], [sms.b])
            fw.op("pool", lambda h: h.memset(imps.t[:], 0.0), [], [imps.b])
            TS("dve", imps, imps.t[:, 0:NPB], ees, ees.t[:, 0, :], sms.t[:, 0:1], None, ALU.mult, extra=(sms,))
            for r in range(1, 4):
                STT(imps, imps.t[:, 0:NPB], ees, ees.t[:, r, :], sms.t[:, r:r + 1], imps, imps.t[:, 0:NPB], ALU.mult, ALU.add, extra=(sms,))
            TT("dve", imps, imps.t[:], imps, imps.t[:], forced, forced.t[:], ALU.add)
            m8s = T("m8s", [128, 16], F32, sS2); i8s = T("i8s", [128, 16], U32, sS2)
            fw.op("dve", lambda h: h.max(out=m8s.t[:, 0:8], in_=imps.t[:]), [imps.b], [m8s.b])
            fw.op("dve", lambda h: h.max_index(out=i8s.t[:, 0:8], in_max=m8s.t[:, 0:8], in_values=imps.t[:]), [imps.b, m8s.b], [i8s.b])
            fw.op("dve", lambda h: h.match_replace(out=sc2s.t[:], in_to_replace=m8s.t[:, 0:8], in_values=imps.t[:], imm_value=-1e30),
                  [imps.b, m8s.b], [sc2s.b])
            fw.op("dve", lambda h: h.max(out=m8s.t[:, 8:16], in_=sc2s.t[:]), [sc2s.b], [m8s.b])
            fw.op("dve", lambda h: h.max_index(out=i8s.t[:, 8:16], in_max=m8s.t[:, 8:16], in_values=sc2s.t[:]), [sc2s.b, m8s.b], [i8s.b])
            idxf = T("idxf", [128, 16], F32, sS2)
            CP("dve", idxf, idxf.t[:], i8s, i8s.t[:])
            ecbs = T("ecbs", [128, 4, NPB], BF16, sS2)
            CP("pool", ecbs, ecbs.t[:], ees, ees.t[:])
            ecTs = T("ecTs", [128, 8, 128], BF16, sS2)
            p = nxt("pt")
            for h2 in range(2):
                for r in range(4):
                    TR(p, p.t[0:NPG, h2 * 4 + r, :], ecbs, ecbs.t[:, r, h2:NPB:2], idb)
            CP("dve", ecTs, ecTs.t[0:NPG], p, p.t[0:NPG])
            pmb = [T(f"pmb{i}", [128, 8, 128], BF16, sS2) for i in range(2)]
            ACO = PM[0]
            for b in range(16):
                t_ = pmb[b % 2]
                fw.op("pool", lambda h, t_=t_: h.memset(t_.t[:], 0.0), [], [t_.b])
                CP("dve", t_, t_.t[0:NPG, :, 8 * b:8 * b + 8], ecTs, ecTs.t[0:NPG, :, 8 * b:8 * b + 8])
                for h2 in range(2):
                    for r in range(4):
                        MM(ACO, ACO.t[:, r * 65:(r + 1) * 65], t_, t_.t[0:NPG, h2 * 4 + r, :], Vc_all, Vc_all.t[0:NPG, b, h2, :],
                           b == 0 and h2 == 0, b == 15 and h2 == 1)
            ptb = T("ptb", [128, 128], I32, sS2)
            for t in range(8):
                LD(ptb, ptb.t[t:128:8, 0:NPG], I["ptab"][:, :])
            tblf = T("tblf", [128, 128, 2], F32, sS2)
            CP("dve", tblf, tblf.t[:, 0:NPG, 0], ptb, ptb.t[:, 0:NPG])
            TS("dve", tblf, tblf.t[:, 0:NPG, 0], tblf, tblf.t[:, 0:NPG, 0], 2.0, None, ALU.mult)
            TS("dve", tblf, tblf.t[:, 0:NPG, 1], tblf, tblf.t[:, 0:NPG, 0], 1.0, None, ALU.add)
            tblv = tblf.t[:, 0:NPG, :].rearrange("p a b -> p (a b)")
            eqt = T("eqt", [128, 256], F32, sS2); physf = T("physf", [128, 16], F32, sS2); physi = T("physi", [128, 16], I32, sS2)
            for n in range(16):
                TS("dve", eqt, eqt.t[:, 0:NPB], iotc, iotc.t[:, 0:NPB], idxf.t[:, n:n + 1], None, ALU.is_equal, extra=(idxf,))
                TT("dve", eqt, eqt.t[:, 0:NPB], eqt, eqt.t[:, 0:NPB], tblf, tblv, ALU.mult)
                fw.op("dve", lambda h, n=n: h.tensor_reduce(out=physf.t[:, n:n + 1], in_=eqt.t[:, 0:NPB], axis=AX.X, op=ALU.add), [eqt.b], [physf.b])
            CP("dve", physi, physi.t[:], physf, physf.t[:])
            validn = T("validn", [128, 16], F32, sS2); jrf = T("jrf", [128, 16], F32, sS2)
            TS("dve", validn, validn.t[:], idxf, idxf.t[:], float(NPB), None, ALU.is_lt)
            TS("dve", jrf, jrf.t[:], idxf, idxf.t[:], -1.0, float(NPB - 1), ALU.mult, ALU.add)
            ohjr = T("ohjr", [128, 16, 13], F32, sS2)
            TT("dve", ohjr, ohjr.t[:], jrf, jrf.t[:].unsqueeze(2).to_broadcast([128, 16, 13]), iot13,
               iot13.t[:].unsqueeze(1).to_broadcast([128, 16, 13]), ALU.is_equal)

            blk = T("blk", [128, 8192], F32, sS2); prod = T("prod", [128, 64, 64], F32, sS2)
            newkv = T("newkv", [128, 8, 256], F32, sS2)
            for t in range(8):
                LD(newkv, newkv.t[t:128:8].rearrange("p a b -> p (a b)"), kvnscr.t.ap().rearrange("(b t) c -> b (t c)", t=8), kvnscr.b)
            lg = T("lg", [128, 4, 64], F32, sS2); pr = T("pr", [128, 4, 64], F32, sS2); sbn = T("sbn", [128, 4, 64], F32, sS2)
            otmp = T("otmp", [128, 4, 64], F32, sS2); dtmp = T("dtmp", [128, 4], F32, sS2)
            den2 = T("den2", [128, 2, 4], F32, sS2); acc2 = T("acc2", [128, 2, 4, 64], F32, sS2)
            fw.op("pool", lambda h: h.memset(den2.t[:], 0.0), [], [den2.b])
            fw.op("pool", lambda h: h.memset(acc2.t[:], 0.0), [], [acc2.b])

            def attend(src, K_ap, V_ap, L, bi, bias=None, mask=None, valid=None):
                for r in range(4):
                    TT("dve", prod, prod.t[:, 0:L, :], src, K_ap, qs, qs.t[:, r * 64:(r + 1) * 64].unsqueeze(1).to_broadcast([128, L, 64]), ALU.mult)
                    fw.op("dve", lambda h, r=r: h.tensor_reduce(out=lg.t[:, r, 0:L], in_=prod.t[:, 0:L, :], axis=AX.X, op=ALU.add), [prod.b], [lg.b])
                if bias is not None:
                    STT(lg, lg.t[:, :, 0:L], lg, lg.t[:, :, 0:L], SCALE, bias[0], bias[1], ALU.mult, ALU.add)
                else:
                    TS("dve", lg, lg.t[:, :, 0:L], lg, lg.t[:, :, 0:L], SCALE, None, ALU.mult)
                if mask is not None:
                    TT("dve", lg, lg.t[:, :, 0:L], lg, lg.t[:, :, 0:L], mask[0], mask[1].unsqueeze(1).to_broadcast([128, 4, L]), ALU.add)
                ACT(pr, pr.t[:, :, 0:L], lg, lg.t[:, :, 0:L], AF.Exp)
                if valid is not None:
                    TS("dve", pr, pr.t[:, :, 0:L], pr, pr.t[:, :, 0:L], valid[1], None, ALU.mult, extra=(valid[0],))
                fw.op("dve", lambda h: h.tensor_reduce(out=dtmp.t[:], in_=pr.t[:, :, 0:L], axis=AX.X, op=ALU.add), [pr.b], [dtmp.b])
                TT("dve", den2, den2.t[:, bi, :], den2, den2.t[:, bi, :], dtmp, dtmp.t[:], ALU.add)
                for r in range(4):
                    TT("dve", prod, prod.t[:, :, 0:L], src, V_ap, pr, pr.t[:, r, 0:L].unsqueeze(1).to_broadcast([128, 64, L]), ALU.mult)
                    fw.op("dve", lambda h, r=r: h.tensor_reduce(out=otmp.t[:, r, :], in_=prod.t[:, :, 0:L], axis=AX.X, op=ALU.add), [prod.b], [otmp.b])
                TT("dve", acc2, acc2.t[:, bi], acc2, acc2.t[:, bi], otmp, otmp.t[:], ALU.add)

            blk4 = blk.t[:, :].rearrange("p (l k d) -> p l k d", k=2, d=64)
            Kb = blk4[:, :, 0, :]; Vb = blk4[:, :, 1, :].rearrange("p l d -> p d l")
            for n in range(16):
                fw.dma("pool", lambda h, n=n: h.indirect_dma_start(out=blk.t[:, :], out_offset=None, in_=I["pool_slc"][:, :],
                                                                  in_offset=IOA(ap=physi.t[:, n:n + 1], axis=0)), [physi.b], [blk.b])
                TS("dve", sbn, sbn.t[:], cand, cand.t[:, 0], ohjr.t[:, n, 0:1], None, ALU.mult, extra=(ohjr,))
                for jr in range(1, 13):
                    STT(sbn, sbn.t[:], cand, cand.t[:, jr], ohjr.t[:, n, jr:jr + 1], sbn, sbn.t[:], ALU.mult, ALU.add, extra=(ohjr,))
                attend(blk, Kb, Vb, 64, 0, bias=(sbn, sbn.t[:]), valid=(validn, validn.t[:, n:n + 1]))
            attend(newkv, newkv.t[:, :, 0:64], newkv.t[:, :, 64:128].rearrange("p l d -> p d l"), 8, 0,
                   bias=(nbias, nbias.t[:]), mask=(nmask, nmask.t[:]))
            for wb in range(8):
                fw.dma("pool", lambda h, wb=wb: h.indirect_dma_start(out=blk.t[:, :], out_offset=None, in_=I["cwin"][:, :],
                                                                    in_offset=IOA(ap=idxw.t[:, wb:wb + 1], axis=0)), [idxw.b], [blk.b])
                attend(blk, Kb, Vb, 64, 1, bias=(wbias, wbias.t[:, :, wb * 64:(wb + 1) * 64]), mask=(wmask0, wmask0.t[:]) if wb == 0 else None)
            attend(newkv, newkv.t[:, :, 128:192], newkv.t[:, :, 192:256].rearrange("p l d -> p d l"), 8, 1,
                   bias=(nbias, nbias.t[:]), mask=(nmask, nmask.t[:]))
            dens = T("dens", [128, 3, 4], F32, sS2); wgt = T("wgt", [128, 3, 4], F32, sS2); osum = T("osum", [128, 4, 64], F32, sS2)
            CP("dve", dens, dens.t[:, 0, :], ACO, ACO.t[:, 0:260].rearrange("p (r e) -> p r e", e=65)[:, :, 64])
            CP("dve", dens, dens.t[:, 1:3, :], den2, den2.t[:])
            TS("dve", dens, dens.t[:], dens, dens.t[:], 1e-30, None, ALU.max)
            fw.op("dve", lambda h: h.reciprocal(out=dens.t[:], in_=dens.t[:]), [dens.b], [dens.b])
            TT("dve", wgt, wgt.t[:], dens, dens.t[:], gates_s, gates_s.t[:].rearrange("p (r b) -> p b r", b=3), ALU.mult)
            for r in range(4):
                TS("dve", osum, osum.t[:, r, :], ACO, ACO.t[:, r * 65:r * 65 + 64], wgt.t[:, 0, r:r + 1], None, ALU.mult, extra=(wgt,))
                for bi in (1, 2):
                    STT(osum, osum.t[:, r, :], acc2, acc2.t[:, bi - 1, r, :], wgt.t[:, bi, r:r + 1], osum, osum.t[:, r, :], ALU.mult, ALU.add,
                        extra=(wgt,))
            osb = T("osb", [128, 256], BF16, sS2); oTs = T("oTs", [128, 2, 128], BF16, sS2)
            CP("act", osb, osb.t[:], osum, osum.t[:].rearrange("p r d -> p (r d)"))
            p = nxt("pt")
            for c2 in range(2):
                TR(p, p.t[:, c2, :], osb, osb.t[:, c2 * 128:(c2 + 1) * 128], idb)
            CP("dve", oTs, oTs.t[:], p, p.t[:, 0:2, :])
            SS = SlabStream([("w_og", 0, 0), ("w_og", 1, 0)] + [("w_up1", jj, 0) for jj in range(11)] + [("w_dn1", cb, ks) for cb in range(2) for ks in range(3)])
            ypart = T("ypart", [128, 1024], F32, sS2)
            for cb in range(2):
                sl = SS.get("w_og", cb, 0)
                p = nxt("pm")
                for kc in range(2):
                    MM(p, p.t[:], oTs, oTs.t[:, kc, :], sl, sl.t[:, kc, :], kc == 0, kc == 1)
                CP("act", ypart, ypart.t[:, cb * 512:(cb + 1) * 512], p, p.t[:])
            ysrc = fw.dram("ysrc", [256, 1024], F32); ydst = fw.dram("ydst", [256, 1024], F32)
            fw.op("pool", lambda h: h.memset(blk.t[:, 0:1024], 0.0), [], [blk.b])
            for hh in range(2):
                STO(ysrc.t.ap()[hh * 128:(hh + 1) * 128, :], ysrc.b, blk, blk.t[:, 0:1024])
            fw.dma("pool", lambda h: h.indirect_dma_start(out=ysrc.t.ap()[:, :], out_offset=IOA(ap=yidx.t[:, 0:1], axis=0), in_=ypart.t[:, :],
                                                          in_offset=None), [ypart.b, yidx.b], [ysrc.b])
            fw.allreduce(ysrc, ydst)
            fw.dma("pool", lambda h: h.indirect_dma_start(out=ypart.t[:, :], out_offset=None, in_=ydst.t.ap()[:, :],
                                                          in_offset=IOA(ap=yidx.t[:, 0:1], axis=0)), [ydst.b, yidx.b], [ypart.b])
            load_ln(1)
            LD(xcur, xcur.t[:], x1scr.t.ap()[NTOK:NTOK + 128, :], x1scr.b)
            for cb in range(2):
                pb, col, o_, w_ = BIASPOS[("b_o", cb)]
                p = nxt("pm")
                MM(p, p.t[:], ones_t, ones_t.t[pb:pb + 2, :], bias_hl, bias_hl.t[pb:pb + 2, col:col + 512], True, True)
                TT("dve", rtmp, rtmp.t[:, cb * 512:(cb + 1) * 512], ypart, ypart.t[:, cb * 512:(cb + 1) * 512], p, p.t[:], ALU.add)
            STT(rtmp, rtmp.t[:], xcur, xcur.t[:], ALPHA, rtmp, rtmp.t[:], ALU.mult, ALU.add)
            CP("act", xcur, xcur.t[:], rtmp, rtmp.t[:])
            layer_norm(xcur, D, lng, lng.t[:, 0, :], lnb, lnb.t[:, 0, :])
            cst[0] = (blk, blk.t[:, :])
            LD(blk, blk.t[0:32, 0:DFF], I["sconv"][1])
            for c in range(22):
                p = nxt("pf")
                fw.op("pe", lambda h, p=p, c=c: h.transpose(out=p.t[:, 0, 0:32], in_=blk.t[0:32, c * 128:(c + 1) * 128],
                                                            identity=idf.t[0:32, 0:32]), [blk.b, idf.b], [p.b])
                CP("dve", cstate_s, cstate_s.t[:, c, :], p, p.t[:, 0, 0:32])
            ffn(SS, 1, True, 0)
            STO(O["ys"][:, :], OB["ys"], xcur, xcur.t[:])
            conv_state_out(1, True)
            fw.barrier()
            sS2.close(); sS.close()

        def phaseC():
            S_, NKT, NBLK = cfg.S, cfg.NKT, cfg.NBLK
            SCALE = 0.125
            NW = NTP + 4; KW0 = NKT - NW
            sC = contextlib.ExitStack()
            relb = fw.sb("relb", [32, 16], F32, sC); r31 = fw.sb("r31", [32, 16], F32, sC)
            LD(relb, relb.t[:], I["rel_bias"][:, :]); LD(r31, r31.t[:], bcast(I["rel_bias"][31:32, :], 16, 32))
            TT("dve", relb, relb.t[:], relb, relb.t[:], r31, r31.t[:], ALU.subtract)
            oh1 = fw.sb("oh1_sb", [32, 1152], F32, sC)
            LD(oh1, oh1.t[:], I["oh1"][:, :])
            Fsb = fw.sb("Fsb", [16, 1152], F32, sC)
            fscr = fw.dram("fscr", [16, 1152], F32)
            for c3 in range(3):
                p = nxt("pm")
                MM(p, p.t[0:16, 0:384], relb, relb.t[:], oh1, oh1.t[:, c3 * 384:(c3 + 1) * 384], True, True)
                CP("dve", Fsb, Fsb.t[:, c3 * 384:(c3 + 1) * 384], p, p.t[0:16, 0:384])
            STO(fscr.t.ap()[:, :], fscr.b, Fsb, Fsb.t[:])
            def ldc(name, shape, src, dt=F32, q="sp"):
                t = fw.sb(name + "_sb", shape, dt, sC)
                LD(t, t.t[:], src, q=q)
                return t
            Eb = fw.sb("Eb", [NBLK, NKT, 128], BF16, sC)
            LD(Eb, Eb.t[:].rearrange("c k p -> c (k p)"), I["Emat"][:, :], q="pool")
            causT = fw.sb("causT4", [128, 4, 128], BF16, sC); winfarT = fw.sb("winfarT4", [128, 4, 128], BF16, sC)
            for r in range(4):
                LD(causT, causT.t[:, r, :], I["causT"][:, :], q="pool"); LD(winfarT, winfarT.t[:, r, :], I["winfarT"][:, :], q="pool")
            cmaskc = ldc("cmaskc", [128, 16], I["cmaskc"][:, :]); fq = ldc("fq", [128, 3], I["fq"][:, :])
            kvalid = ldc("kvalid", [128, NKT], I["kvalid"][:, :])
            cvn = ldc("cvn", [128, NBLK], bcast(I["cvalid"][0:1, :], NBLK))
            cv01 = ldc("cv01", [128, NBLK], bcast(I["cval01"][0:1, :], NBLK))
            fbk = ldc("fbk", [128, NBLK], bcast(I["firstblk"][0:1, :], NBLK))
            TT("dve", cv01, cv01.t[:], cv01, cv01.t[:], fbk, fbk.t[:], ALU.add)
            gidx = ldc("gidx", [128, NKT], I["gidx"][:, :], I32); gidxb = ldc("gidxb", [128, 1], I["gidxb"][:, :], I32)
            ktscr = fw.dram("ktscr", [2, 4, 64, S_], BF16); vscr = fw.dram("vscr", [2, 4, NKT, 128, 64], BF16)
            oscr = fw.dram("oscr", [NTOK, 1024], F32)
            kvts = [fw.sb(f"kvt{i}", [128, 1024], BF16, sC) for i in range(2)]
            ktt = [fw.sb(f"ktt{i}", [64, 4, 128], BF16, sC) for i in range(2)]
            for kt in range(NKT):
                kvt = kvts[kt % 2]
                fw.dma("pool", lambda h, kvt=kvt, kt=kt: h.indirect_dma_start(
                    out=kvt.t[:], out_offset=None, in_=xdst.t.ap()[:, :],
                    in_offset=bass.IndirectOffsetOnAxis(ap=gidx.t[:, kt:kt + 1], axis=0)), [xdst.b, gidx.b], [kvt.b])
                for kind in range(2):
                    if kind == 1 and kt < KW0:
                        continue
                    p = nxt("pt"); kk = ktt[kind]
                    for g in range(4):
                        TR(p, p.t[0:64, g, :], kvt, kvt.t[:, kind * 512 + g * 64:kind * 512 + g * 64 + 64], idb)
                    CP("dve", kk, kk.t[:], p, p.t[0:64, 0:4, :])
                    STO(ktscr.t.ap()[kind, :, :, kt * 128:(kt + 1) * 128].rearrange("g d t -> d g t"), ktscr.b, kk, kk.t[:])
                    STO(vscr.t.ap()[kind, :, kt].rearrange("g p d -> p g d"), vscr.b, kvt,
                        kvt.t[:, kind * 512 + 256:kind * 512 + 512].rearrange("p (g d) -> p g d", g=4))
            kcb = fw.sb("kcb", [128, 512], BF16, sC)
            xdst512 = xdst.t.ap().rearrange("r (two c) -> (r two) c", two=2)
            fw.dma("pool", lambda h: h.indirect_dma_start(out=kcb.t[0:NBLK, :], out_offset=None, in_=xdst512,
                                                          in_offset=bass.IndirectOffsetOnAxis(ap=gidxb.t[0:NBLK, 0:1], axis=0)),
                   [xdst.b, gidxb.b], [kcb.b])
            KTs = fw.sb("KTs", [64, S_], BF16, sC); Vs = fw.sb("Vs", [128, NKT, 65], BF16, sC)
            KTw = fw.sb("KTw", [64, NW * 128], BF16, sC); Vw = fw.sb("Vw", [128, NW, 65], BF16, sC)
            KcT = fw.sb("KcT", [64, 128], BF16, sC); Vc = fw.sb("Vc", [128, 65], BF16, sC)
            qTg = fw.sb("qTg", [64, NTP, 4, 128], BF16, sC)
            Tn = fw.sb("Tn", [128, 4, 1024], F32, sC); Bc = fw.sb("Bc", [128, 4, 16], F32, sC); Bcr = fw.sb("Bcr", [128, 4, 16], F32, sC)
            fw.op("pool", lambda h: h.memset(Vs.t[:, :, 64:65], 1.0), [], [Vs.b])
            fw.op("pool", lambda h: h.memset(Vw.t[:, :, 64:65], 1.0), [], [Vw.b])
            fw.op("pool", lambda h: h.memset(Vc.t[:, 64:65], 1.0), [], [Vc.b])
            lc = fw.sb("lc", [128, 4, 128], F32, sC); ee = fw.sb("ee", [128, 4, 128], F32, sC)
            ecb = fw.sb("ecb", [128, 4, 128], BF16, sC); ecT = fw.sb("ecT", [128, 4, 128], BF16, sC)
            rmx = fw.sb("rmx", [128, 4], F32, sC); sms = fw.sb("sms", [128, 4], F32, sC)
            imp = fw.sb("imp", [128, 128], F32, sC); sc2 = fw.sb("sc2", [128, 128], F32, sC)
            m8 = fw.sb("m8", [128, 16], F32, sC)
            selneg = fw.sb("selneg", [128, 128], F32, sC); selT = fw.sb("selT", [128, 4, 128], BF16, sC)
            fw.op("pool", lambda h: h.memset(selneg.t[:], 0.0), [], [selneg.b])
            stmp = [fw.sb(f"stmp{i}", [128, 4, 128], F32, sC) for i in range(2)]
            PTb = [fw.sb(f"PTb{i}", [128, 4, 128], BF16, sC) for i in range(3)]
            dens = fw.sb("dens", [128, 3, 4], F32, sC); wgt = fw.sb("wgt", [128, 3, 4], F32, sC)
            og = fw.sb("og", [128, 4, 64], F32, sC)
            ACC = PM
            cnt = {"pt": 0, "st": 0}

            for g in range(4):
                LD(KTs, KTs.t[:], ktscr.t.ap()[0, g], ktscr.b)
                LD(Vs, Vs.t[:, :, 0:64], vscr.t.ap()[0, g].rearrange("k p d -> p k d"), vscr.b)
                LD(KTw, KTw.t[:], ktscr.t.ap()[1, g, :, KW0 * 128:], ktscr.b)
                LD(Vw, Vw.t[:, :, 0:64], vscr.t.ap()[1, g, KW0:].rearrange("k p d -> p k d"), vscr.b)
                p = nxt("pt")
                TR(p, p.t[0:64, 0, 0:NBLK], kcb, kcb.t[0:NBLK, g * 64:(g + 1) * 64], idb)
                CP("dve", KcT, KcT.t[:, 0:NBLK], p, p.t[0:64, 0, 0:NBLK])
                CP("dve", Vc, Vc.t[0:NBLK, 0:64], kcb, kcb.t[0:NBLK, 256 + g * 64:256 + (g + 1) * 64])
                for jj in range(NTP):
                    LD(qTg, qTg.t[:, jj, :, :], qscr.t.ap()[jj, 4 * g:4 * g + 4].rearrange("h d t -> d h t"), qscr.b)
                for r in range(4):
                    LD(Tn, Tn.t[:, r, :], bass.AP(tensor=fscr.t, offset=(4 * g + r) * 1152, ap=[[1, 128], [1, 1024]]), fscr.b)
                    LD(Bcr, Bcr.t[:, r, :], bass.AP(tensor=fscr.t, offset=(4 * g + r) * 1152, ap=[[1, 128], [64, 16]]), fscr.b,
                       allow_slow_non_contiguous=True)
                for a in range(16):
                    TS("dve", Bc, Bc.t[:, :, a], Bcr, Bcr.t[:, :, 15 - a], cmaskc.t[:, a:a + 1], None, ALU.add, extra=(cmaskc,))

                for j in range(NTP):
                    qt = NKT - NTP + j; ncol = 2 * qt + 2
                    qsl = qTg.t[:, j, :, :].rearrange("p r t -> p (r t)")
                    pl = nxt("pf")
                    for r in range(4):
                        MM(pl, pl.t[:, r, 0:ncol], qTg, qTg.t[:, j, r, :], KcT, KcT.t[:, 0:ncol], True, True)
                    STT(lc, lc.t[:, :, 0:ncol], pl, pl.t[:, :, 0:ncol], SCALE, cvn,
                        cvn.t[:, 0:ncol].unsqueeze(1).to_broadcast([128, 4, ncol]), ALU.mult, ALU.add)
                    TT("dve", lc, lc.t[:, :, ncol - 16:ncol], lc, lc.t[:, :, ncol - 16:ncol], Bc, Bc.t[:], ALU.add)
                    fw.op("dve", lambda h, ncol=ncol: h.tensor_reduce(out=rmx.t[:], in_=lc.t[:, :, 0:ncol], axis=AX.X, op=ALU.max), [lc.b], [rmx.b])
                    TS("dve", rmx, rmx.t[:], rmx, rmx.t[:], -100.0, -1.0, ALU.max, ALU.mult)
                    for r in range(4):
                        ACT(ee, ee.t[:, r, 0:ncol], lc, lc.t[:, r, 0:ncol], AF.Exp, bias=rmx.t[:, r:r + 1], extra=(rmx,))
                    fw.op("dve", lambda h, ncol=ncol: h.tensor_reduce(out=sms.t[:], in_=ee.t[:, :, 0:ncol], axis=AX.X, op=ALU.add), [ee.b], [sms.b])
                    TS("dve", sms, sms.t[:], sms, sms.t[:], 1e-30, None, ALU.max)
                    fw.op("dve", lambda h: h.reciprocal(out=sms.t[:], in_=sms.t[:]), [sms.b], [sms.b])
                    TS("dve", imp, imp.t[:, 0:ncol], ee, ee.t[:, 0, 0:ncol], sms.t[:, 0:1], None, ALU.mult, extra=(sms,))
                    for r in range(1, 4):
                        STT(imp, imp.t[:, 0:ncol], ee, ee.t[:, r, 0:ncol], sms.t[:, r:r + 1], imp, imp.t[:, 0:ncol], ALU.mult, ALU.add, extra=(sms,))
                    CP("pool", ecb, ecb.t[:, :, 0:ncol], ee, ee.t[:, :, 0:ncol])
                    p = nxt("pt")
                    for r in range(4):
                        TR(p, p.t[0:ncol, r, :], ecb, ecb.t[:, r, 0:ncol], idb)
                    CP("dve", ecT, ecT.t[0:ncol, :, :], p, p.t[0:ncol, 0:4, :])
                    for r in range(4):
                        MM(ACC[0], ACC[0].t[:, r * 65:(r + 1) * 65], ecT, ecT.t[0:ncol, r, :], Vc, Vc.t[0:ncol, :], True, True)
                    TT("dve", imp, imp.t[:, 0:ncol], imp, imp.t[:, 0:ncol], cv01, cv01.t[:, 0:ncol], ALU.add)
                    TT("dve", imp, imp.t[:, ncol - 3:ncol], imp, imp.t[:, ncol - 3:ncol], fq, fq.t[:], ALU.add)
                    fw.op("dve", lambda h, ncol=ncol: h.max(out=m8.t[:, 0:8], in_=imp.t[:, 0:ncol]), [imp.b], [m8.b])
                    fw.op("dve", lambda h, ncol=ncol: h.match_replace(out=sc2.t[:, 0:ncol], in_to_replace=m8.t[:, 0:8], in_values=imp.t[:, 0:ncol],
                                                                      imm_value=-1e30), [imp.b, m8.b], [sc2.b])
                    fw.op("dve", lambda h, ncol=ncol: h.max(out=m8.t[:, 8:16], in_=sc2.t[:, 0:ncol]), [sc2.b], [m8.b])
                    TS("dve", selneg, selneg.t[:, 0:ncol], imp, imp.t[:, 0:ncol], m8.t[:, 15:16], 1.0, ALU.is_ge, ALU.subtract, extra=(m8,))
                    TS("dve", selneg, selneg.t[:, 0:ncol], selneg, selneg.t[:, 0:ncol], -NEG * 8.0, None, ALU.mult)
                    pf_ = nxt("pf")
                    fw.op("pe", lambda h, pf_=pf_: h.transpose(out=pf_.t[0:NBLK, 0, :], in_=selneg.t[:, 0:NBLK], identity=idf.t[:]),
                          [selneg.b, idf.b], [pf_.b])
                    CP("dve", selT, selT.t[0:NBLK, :, :], pf_, pf_.t[0:NBLK, 0:1, :].to_broadcast([NBLK, 4, 128]))
                    selb = selT.t[0:NBLK, :, :].rearrange("p r t -> p (r t)")
                    def s1(tk):
                        (acc, KT, KTb, kcol, Vt, Vb, vidx, kt, first, last, masks, near) = tk
                        ps = nxt("pf")
                        MM(ps, ps.t[:].rearrange("p r t -> p (r t)"), KTb, KT[:, kcol * 128:(kcol + 1) * 128], qTg, qsl, True, len(masks) == 0)
                        for mi, (ltb, lap, rtb, rap) in enumerate(masks):
                            MM(ps, ps.t[:].rearrange("p r t -> p (r t)"), ltb, lap, rtb, rap, False, mi == len(masks) - 1)
                        return ps

                    def s2(tk, ps):
                        (acc, KT, KTb, kcol, Vt, Vb, vidx, kt, first, last, masks, near) = tk
                        pt_ = PTb[cnt["pt"] % 3]; cnt["pt"] += 1
                        if near is not None:
                            st_ = stmp[cnt["st"] % 2]; cnt["st"] += 1
                            STT(st_, st_.t[:], ps, ps.t[:], SCALE, Tn, Tn.t[:, :, near * 128:(near + 1) * 128], ALU.mult, ALU.add)
                            ACT(pt_, pt_.t[:], st_, st_.t[:], AF.Exp, bias=kvalid.t[:, kt:kt + 1], extra=(kvalid,))
                        else:
                            ACT(pt_, pt_.t[:], ps, ps.t[:], AF.Exp, bias=kvalid.t[:, kt:kt + 1], scale=SCALE, extra=(kvalid,))
                        return pt_

                    def s3(tk, pt_):
                        (acc, KT, KTb, kcol, Vt, Vb, vidx, kt, first, last, masks, near) = tk
                        for r in range(4):
                            MM(acc, acc.t[:, r * 65:(r + 1) * 65], pt_, pt_.t[:, r, :], Vb, Vt[:, vidx, :], first, last)

                    caus = (idb, idb.t[:], causT, causT.t[:].rearrange("p r t -> p (r t)"))
                    wfar = (idb, idb.t[:], winfarT, winfarT.t[:].rearrange("p r t -> p (r t)"))
                    tasks = []
                    for kt in range(qt + 1):
                        masks = [(Eb, Eb.t[0:NBLK, kt, :], selT, selb)]
                        if kt == qt:
                            masks.append(caus)
                        delta = qt - kt
                        tasks.append((ACC[1], KTs.t, KTs, kt, Vs.t, Vs, kt, kt, kt == 0, kt == qt, masks, delta if delta <= 7 else None))
                    for kt in range(qt - 4, qt + 1):
                        masks = []
                        if kt == qt:
                            masks.append(caus)
                        if kt == qt - 4:
                            masks.append(wfar)
                        tasks.append((ACC[2], KTw.t, KTw, kt - KW0, Vw.t, Vw, kt - KW0, kt, kt == qt - 4, kt == qt, masks, qt - kt))
                    LOOK = 2
                    live = {}
                    for i in range(min(LOOK, len(tasks))):
                        live[i] = s1(tasks[i])
                    for i in range(len(tasks)):
                        pt_ = s2(tasks[i], live.pop(i))
                        if i + LOOK < len(tasks):
                            live[i + LOOK] = s1(tasks[i + LOOK])
                        s3(tasks[i], pt_)
                    for bi in range(3):
                        CP("dve", dens, dens.t[:, bi, :], ACC[bi], ACC[bi].t[:, 0:260].rearrange("p (r e) -> p r e", e=65)[:, :, 64])
                    TS("dve", dens, dens.t[:], dens, dens.t[:], 1e-30, None, ALU.max)
                    fw.op("dve", lambda h: h.reciprocal(out=dens.t[:], in_=dens.t[:]), [dens.b], [dens.b])
                    TT("dve", wgt, wgt.t[:], dens, dens.t[:], gates,
                       gates.t[:, j, 12 * g:12 * g + 12].rearrange("p (r b) -> p b r", b=3), ALU.mult)
                    for r in range(4):
                        TS("dve", og, og.t[:, r, :], ACC[0], ACC[0].t[:, r * 65:r * 65 + 64], wgt.t[:, 0, r:r + 1], None, ALU.mult, extra=(wgt,))
                        for bi in (1, 2):
                            STT(og, og.t[:, r, :], ACC[bi], ACC[bi].t[:, r * 65:r * 65 + 64], wgt.t[:, bi, r:r + 1], og, og.t[:, r, :],
                                ALU.mult, ALU.add, extra=(wgt,))
                    STO(oscr.t.ap()[j * 128:(j + 1) * 128, g * 256:(g + 1) * 256], oscr.b, og, og.t[:].rearrange("p r d -> p (r d)"))

            load_ln(1)
            specsC = []
            for ti in range(NTP):
                specsC += [("w_o", 0, 0), ("w_o", 1, 0)] + [("w_up1", jj, 0) for jj in range(11)] + [("w_dn1", cb, ks) for cb in range(2) for ks in range(3)]
            SC = SlabStream(specsC)
            fw.op("pool", lambda h: h.memset(cstate_p.t[:], 0.0), [], [cstate_p.b])
            for ti in range(NTP):
                LD(rtmp, rtmp.t[:], oscr.t.ap()[ti * 128:(ti + 1) * 128, :], oscr.b)
                to_T(rtmp, xT)
                LD(xcur, xcur.t[:], x1scr.t.ap()[ti * 128:(ti + 1) * 128, :], x1scr.b)

                def cons_o(cb, p):
                    STT(rtmp, rtmp.t[:, cb * 512:(cb + 1) * 512], xcur, xcur.t[:, cb * 512:(cb + 1) * 512], ALPHA, p, p.t[:], ALU.mult, ALU.add)
                proj_tm(SC, "w_o", 2, xT, 8, "b_o", cons_o)
                CP("act", xcur, xcur.t[:], rtmp, rtmp.t[:])
                layer_norm(xcur, D, lng, lng.t[:, 0, :], lnb, lnb.t[:, 0, :])
                ffn(SC, 1, False, ti)
                if ti == 0:
                    TS("dve", cstate_p, cstate_p.t[:], cstate_p, cstate_p.t[:], cv.t[:, 0:1], None, ALU.mult, extra=(cv,))
                else:
                    STO(O["yp"][(ti - 1) * 128:ti * 128, :], OB["yp"], xcur, xcur.t[:])
            cst[0] = (Tn, Tn.t[:].rearrange("p r n -> p (r n)"))
            conv_state_out(1, False)
            fw.barrier()
            sC.close()

        for ti in range(NTP):
            phaseA_tile(ti, False)
        phaseA_tile(0, True)

        def load_compress_weights(stack):
            Wz = fw.sb("Wz", [64, 64, 2, 128], BF16, stack)
            for k in range(2):
                fw.dma("pool", lambda h, k=k: h.dma_start(out=Wz.t[:, :, k, :], in_=I["cmp_w1"][k].rearrange("(l d) h -> d l h", d=64)), [], [Wz.b])
            pef = fw.sb("pef", [64, 128], F32, stack); pez = fw.sb("pez", [64, 2, 64], BF16, stack)
            LD(pef, pef.t[:].rearrange("l (k d) -> l k d", k=2), I["cmp_pe"].rearrange("k l d -> l k d"))
            p = nxt("pf")
            for k in range(2):
                fw.op("pe", lambda h, k=k: h.transpose(out=p.t[0:64, k, 0:64], in_=pef.t[:, k * 64:(k + 1) * 64], identity=idf.t[0:64, 0:64]),
                      [pef.b, idf.b], [p.b])
            CP("dve", pez, pez.t[:], p, p.t[0:64, 0:2, 0:64])
            b1pp = fw.sb("b1pp", [128, 2], F32, stack)
            fw.dma("sp", lambda h: h.dma_start(out=b1pp.t[:], in_=I["cmp_b1"].rearrange("k h -> h k"), allow_slow_non_contiguous=True), [], [b1pp.b])
            pebias = fw.sb("pebias", [128, 2], F32, stack)
            p2 = nxt("pf")
            for k in range(2):
                for l in range(64):
                    MM(p2, p2.t[:, k, 0:1], Wz, Wz.t[:, l, k, :], pez, pez.t[:, k, l:l + 1], l == 0, l == 63)
            TT("dve", pebias, pebias.t[:], p2, p2.t[:, 0:2, 0], b1pp, b1pp.t[:], ALU.add)
            w2sb = fw.sb("w2sb", [128, 2, 64], BF16, stack)
            fw.dma("pool", lambda h: h.dma_start(out=w2sb.t[:], in_=I["cmp_w2"].rearrange("k h d -> h k d")), [], [w2sb.b])
            b2bc = fw.sb("b2bc", [128, 128], F32, stack)
            LD(b2bc, b2bc.t[:], bcast(I["cmp_b2"][0:1, :], 128))
            return Wz, pebias, w2sb, b2bc

        fw.barrier()
        sA.close()
        sA2 = contextlib.ExitStack()
        Wz, pebias, w2sb, b2bc = load_compress_weights(sA2)
        NCC = 4 * NTP
        hidT = fw.sb("hidT", [128, 4, NCC], BF16, sA2)
        for h2 in range(2):
            for k in range(2):
                p = nxt("pm")
                for l in range(64):
                    MM(p, p.t[:, 0:NCC], Wz, Wz.t[:, l, k, :], cmpT, cmpT.t[:, k, h2 * 64 + l, :], l == 0, l == 63)
                ACT(hidT, hidT.t[:, h2 * 2 + k, :], p, p.t[:, 0:NCC], AF.Gelu, bias=pebias.t[:, k:k + 1], extra=(pebias,))
        kvc_own = fw.sb("kvc_own", [128, 1024], BF16, sA2)
        for h2 in range(2):
            p = nxt("pm")
            for k in range(2):
                for g in range(4):
                    MM(p, p.t[0:NTP, (k * 4 + g) * 64:(k * 4 + g + 1) * 64], hidT, hidT.t[:, h2 * 2 + k, g * NTP:(g + 1) * NTP],
                       w2sb, w2sb.t[:, k, :], True, True)
            for k in range(2):
                TT("dve", kvc_own, kvc_own.t[0:NTP, h2 * 512 + k * 256:h2 * 512 + (k + 1) * 256].rearrange("p (g d) -> p g d", g=4),
                   p, p.t[0:NTP, k * 256:(k + 1) * 256].rearrange("p (g d) -> p g d", g=4),
                   b2bc, b2bc.t[0:NTP, k * 64:(k + 1) * 64].unsqueeze(1).to_broadcast([NTP, 4, 64]), ALU.add)
        scatter_rows(kvc_own, kvc_own.t[0:NTP, :], cidx, cidx.t[0:NTP, 0:1])
        fw.barrier()
        sA2.close(); sA0.close()
        fw.allreduce(xsrc, xdst)

        if cfg.stage >= 3:
            phaseS()
        if cfg.stage >= 2:
            phaseC()
            fw.finish(list(OB.values()))
            return nc

        if cfg.stage <= 1:
            for ti in range(1, NTP):
                LD(xcur, xcur.t[:], x1scr.t.ap()[ti * 128:(ti + 1) * 128, :], x1scr.b)
                STO(O["yp"][(ti - 1) * 128:ti * 128, :], OB["yp"], xcur, xcur.t[:])
            LD(xcur, xcur.t[:], x1scr.t.ap()[NTOK:NTOK + 128, :], x1scr.b)
            STO(O["ys"][:, :], OB["ys"], xcur, xcur.t[:])
            fw.finish(list(OB.values()))
            return nc

        fw.finish(list(OB.values()))
    return nc


def make_in_maps(cfg, inp):
    S = cfg.S; TOWN = cfg.TOWN
    consts = host_consts(cfg)
    f32 = np.float32
    shared = {
        "ln_g": np.ascontiguousarray(inp["ln_g"].reshape(4, 1024), f32), "ln_b": np.ascontiguousarray(inp["ln_b"].reshape(4, 1024), f32),
        "sg_w_in": inp["sg_w_in"][0], "sg_b_in": inp["sg_b_in"].reshape(1, -1), "sg_ln_g": inp["sg_ln_g"].reshape(1, -1),
        "sg_ln_b": inp["sg_ln_b"].reshape(1, -1), "sg_w_s": inp["sg_w_s"][0], "sg_b_s": inp["sg_b_s"][0],
        "sg_w_out": inp["sg_w_out"][0], "sg_b_out": inp["sg_b_out"].reshape(1, -1), "ffn_w_up": inp["ffn_w_up"],
        "ffn_b_up": inp["ffn_b_up"], "ffn_w_dw": inp["ffn_w_dw"], "ffn_b_dw": inp["ffn_b_dw"], "ffn_w_down": inp["ffn_w_down"],
        "ffn_b_down": inp["ffn_b_down"], "kv_w": inp["kv_w"], "nsa_w_qg": inp["nsa_w_qg"][0], "nsa_b_qg": inp["nsa_b_qg"].reshape(1, -1),
        "nsa_w_o": inp["nsa_w_o"][0], "nsa_b_o": inp["nsa_b_o"].reshape(1, -1),
        "cmp_pe": inp["cmp_pe"], "cmp_w1": inp["cmp_w1"], "cmp_b1": inp["cmp_b1"], "cmp_w2": inp["cmp_w2"],
        "cmp_b2": inp["cmp_b2"].reshape(1, -1), "rel_bias": inp["rel_bias"],
    }
    shared = {k: np.ascontiguousarray(v, dtype=f32) for k, v in shared.items()}
    shared.update(consts)
    maps = []
    for c in range(8):
        s, i = c // 4, c % 4
        g, bh = c % 4, c // 4
        m = dict(shared)
        xp = np.zeros((cfg.NTP * 128, 1024), f32)
        t0 = i * TOWN
        if i > 0:
            xp[0:128] = inp["x_prompt"][s, t0 - 128:t0]
        xp[128:] = inp["x_prompt"][s, t0:t0 + TOWN]
        m["xp"] = xp
        m["xs"] = np.ascontiguousarray(inp["x_sample"][16 * bh:16 * bh + 16].reshape(128, 1024), f32)
        cvec = np.zeros((1, 8), f32); cvec[0, 0] = 1.0 if i > 0 else 0.0
        m["cvec"] = cvec
        m["sconv"] = np.ascontiguousarray(inp["state_conv"][:, 16 * bh:16 * bh + 16].reshape(2, 32, cfg.DFF), f32)
        wq = inp["nsa_w_qg"][0]; bqv = inp["nsa_b_qg"][0]
        qcols = list(range(256 * g, 256 * g + 256)) + list(range(1024 + 12 * g, 1024 + 12 * g + 12))
        m["wq_g"] = np.ascontiguousarray(wq[:, qcols], f32)
        m["bq_g"] = np.ascontiguousarray(bqv[qcols].reshape(1, -1), f32)
        m["wo_g"] = np.ascontiguousarray(inp["nsa_w_o"][0][256 * g:256 * g + 256], f32)
        NKT, NBLK, NTO, NTP = cfg.NKT, cfg.NBLK, cfg.NTO, cfg.NTP
        NROWS = 2 * S + 2 * NKT
        pp = np.arange(128)
        m["sidx"] = (s * S + i * TOWN + np.arange(NTO)[None, :] * 128 + pp[:, None]).astype(np.int32)
        cidx = np.full((128, 1), NROWS + 7, np.int32)
        cidx[1:NTP, 0] = 2 * S + s * NKT + i * NTO + np.arange(NTO)
        m["cidx"] = cidx
        pad = (3 - i) * TOWN
        pos = np.arange(NKT)[None, :] * 128 + 127 - pp[:, None]
        real = pos - pad
        m["gidx"] = (s * S + np.maximum(real, 0)).astype(np.int32)
        m["kvalid"] = np.where(real >= 0, 0.0, NEG).astype(f32)
        rb = np.arange(NBLK) - pad // 64
        gb = np.zeros((128, 1), np.int32)
        gb[:NBLK, 0] = 4 * S + 2 * s * NKT + np.maximum(rb, 0)
        m["gidxb"] = gb
        m["cvalid"] = np.where(rb >= 0, 0.0, NEG).astype(f32).reshape(1, -1)
        m["cval01"] = np.where(rb >= 0, 0.0, -1.0).astype(f32).reshape(1, -1)
        fb = np.zeros((1, NBLK), f32); fb[0, pad // 64] = BIG
        m["firstblk"] = fb
        NPH = cfg.NPHYS
        m["pool_cmp"] = np.ascontiguousarray(inp["cache_kv_cmp"][:, :, :, g, :]).reshape(NPH * 8, 2048)
        m["pool_slc"] = np.ascontiguousarray(inp["cache_kv_slc"][:, :, :, g, :]).reshape(NPH * 2, 8192)
        m["cwin"] = np.ascontiguousarray(inp["cache_win"][16 * bh:16 * bh + 16, :, :, g, :]).reshape(128, 8192)
        m["ptab"] = np.ascontiguousarray(inp["page_table"][16 * bh:16 * bh + 16], np.int32)
        kvw = inp["kv_w"]
        kcols = [k0 * 256 + g * 64 + d for k0 in (2, 3, 4, 5) for d in range(64)]
        m["wkv_g"] = np.ascontiguousarray(kvw[:, kcols], f32)
        m["relb_g"] = np.ascontiguousarray(inp["rel_bias"][:, 4 * g:4 * g + 4], f32)
        m["yidx"] = (128 * bh + np.arange(128)).astype(np.int32).reshape(128, 1)
        maps.append(m)
    return maps


def assemble(cfg, res, inp):
    S = cfg.S; TOWN = cfg.TOWN; DFF = cfg.DFF
    B = inp["x_prompt"].shape[0]
    yp = np.zeros((B, S, 1024), np.float32); kvp = np.zeros((B, S, 1536), np.float32)
    convp = np.zeros((2, B, 2, DFF), np.float32)
    ys = np.zeros((32, 8, 1024), np.float32); kvs = np.zeros((32, 8, 1536), np.float32)
    convs = np.zeros((2, 32, 2, DFF), np.float32); sgv = np.zeros((1, 32, 8, 3072), np.float32)
    for c in range(8):
        s, i = c // 4, c % 4
        r = res[c]
        yp[s, i * TOWN:(i + 1) * TOWN] = r["o_yp"]; kvp[s, i * TOWN:(i + 1) * TOWN] = r["o_kvp"]
        if i == 3:
            convp[:, s] = r["o_convp"]
        if c % 4 == 0:
            bh = c // 4
            ys[16 * bh:16 * bh + 16] = r["o_ys"].reshape(16, 8, 1024)
            kvs[16 * bh:16 * bh + 16] = r["o_kvs"].reshape(16, 8, 1536)
            convs[:, 16 * bh:16 * bh + 16] = r["o_convs"].reshape(2, 16, 2, DFF)
            sgv[0, 16 * bh:16 * bh + 16] = r["o_sgv"].reshape(16, 8, 3072)
    kvp6 = kvp.reshape(B, S, 6, 4, 64); kvs6 = kvs.reshape(32, 8, 6, 4, 64)
    W = min(512, S)
    return (yp, ys, np.ascontiguousarray(kvp6[:, :, 0:2]), np.ascontiguousarray(kvp6[:, :, 2:4]),
            np.ascontiguousarray(kvp6[:, S - W:, 4:6]), convp,
            np.ascontiguousarray(kvs6[:, :, 0:2]), np.ascontiguousarray(kvs6[:, :, 2:4]), np.ascontiguousarray(kvs6[:, :, 4:6]),
            convs, sgv)


_NC_CACHE = {}


def run(cfg, inp):
    key = (cfg.S, cfg.PAST, cfg.NPHYS, cfg.stage)
    if key not in _NC_CACHE:
        _NC_CACHE[key] = build(cfg)
    nc = _NC_CACHE[key]
    maps = make_in_maps(cfg, inp)
    res = run_bass_kernel_spmd(nc, maps, core_ids=list(range(8)))
    return assemble(cfg, res.results, inp)


def kernel(**inputs):
    inp = {k: np.asarray(v) for k, v in inputs.items()}
    cfg = Cfg(S=inp["x_prompt"].shape[1], PAST=inp["page_table"].shape[1] * 128, NPHYS=inp["cache_kv_cmp"].shape[0])
    return run(cfg, inp)
```
